# Optimizing a Trainium2 kernel written in Bass

```python
import jax, jax.numpy as jnp
from jax import lax
import numpy as np

D_MODEL = 1024
BATCH = 8
SEQ = 8192
DEPTH = 1

HEAD_DIM = 64
NSA_HEADS = 8
NSA_KV_HEADS = 2
NSA_GROUP = NSA_HEADS // NSA_KV_HEADS
NSA_WIDTH = NSA_HEADS * HEAD_DIM
KV_WIDTH = NSA_KV_HEADS * HEAD_DIM
N_BRANCH = 3
CMP_LEN = 32
CMP_STRIDE = 16
SEL_BLOCK = 64
SEL_TOPK = 16
WINDOW = 512
FORCE_SCORE = 1e4
POOL_WINDOWS = (2, 4, 8, 16)
POOL_GROUPS = 4
POOL_GROUP_DIM = 64
POOL_WIDTH = POOL_GROUPS * POOL_GROUP_DIM
MEM_HEADS = 4
MEM_LEN = 256
MEM_WIDTH = MEM_HEADS * HEAD_DIM
MIX_WIDTH = NSA_WIDTH + POOL_WIDTH + MEM_WIDTH
ROPE_THETA = 500000.0
ROPE_DIM = HEAD_DIM // 4
EPS = 1e-6
Q_BLOCK = 32
IN_SPLITS = (NSA_WIDTH, 6 * KV_WIDTH, NSA_HEADS * N_BRANCH, NSA_WIDTH,
             POOL_WIDTH, POOL_WIDTH, MEM_WIDTH, MEM_WIDTH)
IN_WIDTH = sum(IN_SPLITS)

kernel_name = "hymba_nsa_pool_memory_layer"


def rms_norm(x, g):
    x32 = x.astype(jnp.float32)
    y = x32 * lax.rsqrt(jnp.mean(x32 * x32, axis=-1, keepdims=True) + EPS)
    return (y * g.astype(jnp.float32)).astype(x.dtype)


def partial_rope(x, pos):
    half = ROPE_DIM // 2
    inv_freq = ROPE_THETA ** (-jnp.arange(half, dtype=jnp.float32) / half)
    ang = pos.astype(jnp.float32)[:, None, :, None] * inv_freq
    cos, sin = jnp.cos(ang), jnp.sin(ang)
    x32 = x.astype(jnp.float32)
    x1, x2 = x32[..., :half], x32[..., half:ROPE_DIM]
    out = jnp.concatenate([x1 * cos - x2 * sin, x1 * sin + x2 * cos, x32[..., ROPE_DIM:]], axis=-1)
    return out.astype(x.dtype)


def masked_softmax(s, mask):
    s = jnp.where(mask, s, -jnp.inf)
    m = jnp.max(s, axis=-1, keepdims=True)
    m = jnp.where(jnp.isfinite(m), m, 0.0)
    e = jnp.where(mask, jnp.exp(s - m), 0.0)
    return e / jnp.maximum(jnp.sum(e, axis=-1, keepdims=True), jnp.finfo(jnp.float32).tiny)


def compress_blocks(kv_raw, pos_emb, w1, w2):
    S = kv_raw.shape[1]
    nc = (S - CMP_LEN) // CMP_STRIDE + 1
    idx = np.arange(nc)[:, None] * CMP_STRIDE + np.arange(CMP_LEN)[None, :]
    blocks = kv_raw[:, idx] + pos_emb[None, None, :, None, :]
    hid = jax.nn.gelu(jnp.einsum('bnlgd,lde->bgne', blocks, w1))
    return jnp.einsum('bgne,ef->bgnf', hid, w2)


def cmp_to_sel_map(nc, ns):
    c0 = np.arange(nc) * CMP_STRIDE
    c1 = c0 + CMP_LEN
    s0 = np.arange(ns) * SEL_BLOCK
    s1 = s0 + SEL_BLOCK
    ov = np.clip(np.minimum(c1[:, None], s1[None, :]) - np.maximum(c0[:, None], s0[None, :]), 0, None)
    return (ov / CMP_LEN).astype(np.float32)


def nsa_mixer(q_n, kv_n, gate_n, positions, g_q_nsa, g_k_cmp, g_k_slc, g_k_win,
              cmp_pos_k, w_cmp_k1, w_cmp_k2, cmp_pos_v, w_cmp_v1, w_cmp_v2):
    B, S, _ = q_n.shape
    G, R, hd = NSA_KV_HEADS, NSA_GROUP, HEAD_DIM
    dt = q_n.dtype
    scale = HEAD_DIM ** -0.5
    nc = (S - CMP_LEN) // CMP_STRIDE + 1
    ns = S // SEL_BLOCK
    top_n = min(SEL_TOPK, ns)

    q = q_n.reshape(B, S, NSA_HEADS, hd).transpose(0, 2, 1, 3)
    q = partial_rope(rms_norm(q, g_q_nsa), positions)
    kv = kv_n.reshape(B, S, 6, G, hd)
    kc_raw, vc_raw = kv[:, :, 0], kv[:, :, 1]
    k_s, v_s = kv[:, :, 2].transpose(0, 2, 1, 3), kv[:, :, 3].transpose(0, 2, 1, 3)
    k_w, v_w = kv[:, :, 4].transpose(0, 2, 1, 3), kv[:, :, 5].transpose(0, 2, 1, 3)

    end_idx = np.arange(nc) * CMP_STRIDE + CMP_LEN - 1
    k_c = compress_blocks(kc_raw, cmp_pos_k, w_cmp_k1, w_cmp_k2)
    k_c = partial_rope(rms_norm(k_c, g_k_cmp), positions[:, end_idx])
    v_c = compress_blocks(vc_raw, cmp_pos_v, w_cmp_v1, w_cmp_v2)

    k_s = partial_rope(rms_norm(k_s, g_k_slc), positions)
    kb = k_s.reshape(B, G, ns, SEL_BLOCK, hd)
    vb = v_s.reshape(B, G, ns, SEL_BLOCK, hd)

    k_w = partial_rope(rms_norm(k_w, g_k_win), positions)
    pad = ((0, 0), (0, 0), (WINDOW, 0), (0, 0))
    kw_pad, vw_pad = jnp.pad(k_w, pad), jnp.pad(v_w, pad)

    gates = jax.nn.sigmoid(gate_n.astype(jnp.float32)).reshape(B, S, NSA_HEADS, N_BRANCH)
    gates = gates.transpose(0, 2, 1, 3).astype(dt)

    sel_map = jnp.asarray(cmp_to_sel_map(nc, ns))
    end_idx_j = jnp.asarray(end_idx)
    blk = jnp.arange(ns)
    b_ix = jnp.arange(B)[:, None, None, None]
    g_ix = jnp.arange(G)[None, :, None, None]

    def query_block(i):
        start = i * Q_BLOCK
        t = start + jnp.arange(Q_BLOCK)
        qb = lax.dynamic_slice_in_dim(q, start, Q_BLOCK, axis=2).reshape(B, G, R, Q_BLOCK, hd)

        s_c = jnp.einsum('bgrqd,bgnd->bgrqn', qb, k_c, preferred_element_type=jnp.float32) * scale
        c_mask = end_idx_j[None, :] <= t[:, None]
        p_c = masked_softmax(s_c, c_mask)
        o_c = jnp.einsum('bgrqn,bgnd->bgrqd', p_c.astype(dt), v_c)

        imp = jnp.einsum('bgqn,nj->bgqj', jnp.sum(p_c, axis=2), sel_map)
        cur = t // SEL_BLOCK
        forced = (blk[None, :] == 0) | (blk[None, :] == cur[:, None]) | (blk[None, :] == cur[:, None] - 1)
        valid = blk[None, :] <= cur[:, None]
        imp = jnp.where(forced, FORCE_SCORE, jnp.where(valid, imp, -1.0))
        _, sel = lax.top_k(imp, top_n)
        ks_g = kb[b_ix, g_ix, sel]
        vs_g = vb[b_ix, g_ix, sel]
        s_pos = (sel[..., None] * SEL_BLOCK + jnp.arange(SEL_BLOCK)).reshape(B, G, Q_BLOCK, top_n * SEL_BLOCK)
        s_mask = (s_pos <= t[None, None, :, None])[:, :, None]
        s_s = jnp.einsum('bgrqd,bgqkld->bgrqkl', qb, ks_g, preferred_element_type=jnp.float32)
        s_s = s_s.reshape(B, G, R, Q_BLOCK, top_n * SEL_BLOCK) * scale
        p_s = masked_softmax(s_s, s_mask).reshape(B, G, R, Q_BLOCK, top_n, SEL_BLOCK)
        o_s = jnp.einsum('bgrqkl,bgqkld->bgrqd', p_s.astype(dt), vs_g)

        kw_blk = lax.dynamic_slice_in_dim(kw_pad, start, WINDOW + Q_BLOCK, axis=2)
        vw_blk = lax.dynamic_slice_in_dim(vw_pad, start, WINDOW + Q_BLOCK, axis=2)
        kpos = start - WINDOW + jnp.arange(WINDOW + Q_BLOCK)
        w_mask = (kpos[None, :] <= t[:, None]) & (kpos[None, :] > t[:, None] - WINDOW) & (kpos[None, :] >= 0)
        s_w = jnp.einsum('bgrqd,bgkd->bgrqk', qb, kw_blk, preferred_element_type=jnp.float32) * scale
        p_w = masked_softmax(s_w, w_mask)
        o_w = jnp.einsum('bgrqk,bgkd->bgrqd', p_w.astype(dt), vw_blk)

        gb = lax.dynamic_slice_in_dim(gates, start, Q_BLOCK, axis=2).reshape(B, G, R, Q_BLOCK, N_BRANCH)
        o = gb[..., 0:1] * o_c + gb[..., 1:2] * o_s + gb[..., 2:3] * o_w
        return o.reshape(B, NSA_HEADS, Q_BLOCK, hd)

    o_blocks = lax.map(query_block, jnp.arange(S // Q_BLOCK))
    return o_blocks.transpose(1, 0, 3, 2, 4).reshape(B, S, NSA_WIDTH)


def pool_mixer(v_p, w_pool, pool_scale):
    B, S, _ = v_p.shape
    v32 = v_p.reshape(B, S, POOL_GROUPS, POOL_GROUP_DIM).astype(jnp.float32)
    cs = jnp.cumsum(v32, axis=1)
    count_base = jnp.arange(1, S + 1, dtype=jnp.float32)
    pooled = []
    for g, w in enumerate(POOL_WINDOWS):
        c = cs[:, :, g]
        lower = jnp.pad(c[:, :S - w], ((0, 0), (w, 0), (0, 0)))
        cnt = jnp.minimum(count_base, float(w))[None, :, None]
        pooled.append((c - lower) / cnt - v32[:, :, g])
    pooled = jnp.stack(pooled, axis=2).astype(v_p.dtype)
    out = jnp.einsum('bsgc,gce->bsge', pooled, w_pool).reshape(B, S, POOL_WIDTH)
    return out * pool_scale


def memory_mixer(q_m, mem, g_mem, w_mem_kv, g_q_mem, g_k_mem):
    B, S, _ = q_m.shape
    M = mem.shape[1]
    scale = HEAD_DIM ** -0.5
    m_h = rms_norm(mem, g_mem)
    mkv = jnp.einsum('bmd,de->bme', m_h, w_mem_kv).reshape(B, M, 2, MEM_HEADS, HEAD_DIM)
    mk = rms_norm(mkv[:, :, 0], g_k_mem)
    mv = mkv[:, :, 1]
    mq = rms_norm(q_m.reshape(B, S, MEM_HEADS, HEAD_DIM), g_q_mem)
    s_m = jnp.einsum('bshd,bmhd->bhsm', mq, mk, preferred_element_type=jnp.float32) * scale
    p_m = jax.nn.softmax(s_m, axis=-1)
    return jnp.einsum('bhsm,bmhd->bshd', p_m.astype(q_m.dtype), mv).reshape(B, S, MEM_WIDTH)


def hybrid_layer(x, mem, positions, g_norm, w_in, g_q_nsa, g_k_cmp, g_k_slc, g_k_win,
                 cmp_pos_k, w_cmp_k1, w_cmp_k2, cmp_pos_v, w_cmp_v1, w_cmp_v2,
                 w_pool, pool_scale, g_mem, w_mem_kv, g_q_mem, g_k_mem, w_out):
    h = rms_norm(x, g_norm)
    proj = jnp.einsum('bsd,de->bse', h, w_in)
    split_at = np.cumsum(IN_SPLITS)[:-1].tolist()
    q_n, kv_n, gate_n, z_n, v_p, z_p, q_m, z_m = jnp.split(proj, split_at, axis=-1)
    o_nsa = nsa_mixer(q_n, kv_n, gate_n, positions, g_q_nsa, g_k_cmp, g_k_slc, g_k_win,
                      cmp_pos_k, w_cmp_k1, w_cmp_k2, cmp_pos_v, w_cmp_v1, w_cmp_v2)
    o_pool = pool_mixer(v_p, w_pool, pool_scale)
    o_mem = memory_mixer(q_m, mem, g_mem, w_mem_kv, g_q_mem, g_k_mem)
    y = jnp.concatenate([o_nsa * jax.nn.silu(z_n),
                         o_pool * jax.nn.silu(z_p),
                         o_mem * jax.nn.silu(z_m)], axis=-1)
    return x + jnp.einsum('bse,ed->bsd', y, w_out)


def setup_inputs(seed: int = 0) -> dict:
    key = jax.random.key(seed)
    ks = jax.random.split(key, 24)
    f32 = jnp.float32
    L = DEPTH

    def nrm(k, shape, s):
        return jax.random.normal(k, shape, f32) * s

    def gain(k, shape):
        return 1.0 + 0.01 * jax.random.normal(k, shape, f32)

    x = nrm(ks[0], (BATCH, SEQ, D_MODEL), 1.0)
    mem = nrm(ks[1], (BATCH, MEM_LEN, D_MODEL), 1.0)
    positions = (jnp.arange(SEQ, dtype=jnp.int32)[None, :]
                 + jax.random.randint(ks[2], (BATCH, 1), 0, 1024, dtype=jnp.int32))
    return {
        "x": x,
        "mem": mem,
        "positions": positions,
        "g_norm": gain(ks[3], (L, D_MODEL)),
        "w_in": nrm(ks[4], (L, D_MODEL, IN_WIDTH), D_MODEL ** -0.5),
        "g_q_nsa": gain(ks[5], (L, HEAD_DIM)),
        "g_k_cmp": gain(ks[6], (L, HEAD_DIM)),
        "g_k_slc": gain(ks[7], (L, HEAD_DIM)),
        "g_k_win": gain(ks[8], (L, HEAD_DIM)),
        "cmp_pos_k": nrm(ks[9], (L, CMP_LEN, HEAD_DIM), 0.1),
        "w_cmp_k1": nrm(ks[10], (L, CMP_LEN, HEAD_DIM, HEAD_DIM), (CMP_LEN * HEAD_DIM) ** -0.5),
        "w_cmp_k2": nrm(ks[11], (L, HEAD_DIM, HEAD_DIM), HEAD_DIM ** -0.5),
        "cmp_pos_v": nrm(ks[12], (L, CMP_LEN, HEAD_DIM), 0.1),
        "w_cmp_v1": nrm(ks[13], (L, CMP_LEN, HEAD_DIM, HEAD_DIM), (CMP_LEN * HEAD_DIM) ** -0.5),
        "w_cmp_v2": nrm(ks[14], (L, HEAD_DIM, HEAD_DIM), HEAD_DIM ** -0.5),
        "w_pool": nrm(ks[15], (L, POOL_GROUPS, POOL_GROUP_DIM, POOL_GROUP_DIM), POOL_GROUP_DIM ** -0.5),
        "pool_scale": gain(ks[16], (L, POOL_WIDTH)),
        "g_mem": gain(ks[17], (L, D_MODEL)),
        "w_mem_kv": nrm(ks[18], (L, D_MODEL, 2 * MEM_WIDTH), D_MODEL ** -0.5),
        "g_q_mem": gain(ks[19], (L, HEAD_DIM)),
        "g_k_mem": gain(ks[20], (L, HEAD_DIM)),
        "w_out": nrm(ks[21], (L, MIX_WIDTH, D_MODEL), MIX_WIDTH ** -0.5),
    }


def reference(x, mem, positions, g_norm, w_in, g_q_nsa, g_k_cmp, g_k_slc, g_k_win,
              cmp_pos_k, w_cmp_k1, w_cmp_k2, cmp_pos_v, w_cmp_v1, w_cmp_v2,
              w_pool, pool_scale, g_mem, w_mem_kv, g_q_mem, g_k_mem, w_out):
    for l in range(DEPTH):
        x = hybrid_layer(x, mem, positions, g_norm[l], w_in[l], g_q_nsa[l], g_k_cmp[l],
                         g_k_slc[l], g_k_win[l], cmp_pos_k[l], w_cmp_k1[l], w_cmp_k2[l],
                         cmp_pos_v[l], w_cmp_v1[l], w_cmp_v2[l], w_pool[l], pool_scale[l],
                         g_mem[l], w_mem_kv[l], g_q_mem[l], g_k_mem[l], w_out[l])
    return x
```

```python
import contextlib
import numpy as np
import concourse.bass as bass
import concourse.mybir as mybir
from concourse.bass_utils import run_bass_kernel_spmd

F32, BF16, I32 = mybir.dt.float32, mybir.dt.bfloat16, mybir.dt.int32
ALU = mybir.AluOpType
AF = mybir.ActivationFunctionType
AX = mybir.AxisListType

ENG_ATTR = {"pe": "tensor", "act": "scalar", "dve": "vector", "pool": "gpsimd", "sp": "sync"}
EPS = 1e-6
NEG = -30000.0
GELU_C = 0.7978845608028654


class Prog:
    def __init__(self, nc):
        self.nc = nc
        self.ops = []
        self.lw = {}
        self.rd = {}
        self.epoch = 0
        self.capture = None

    PSUM_KEYS = {"pX", "pA0", "pA1", "pA2", "pO0", "pI", "pM0", "pT"}

    def op(self, eng, fn, reads=(), writes=(), dma_slot=None):
        if self.capture is not None:
            self.capture.append((eng, fn, list(reads), list(writes), dma_slot))
            return None
        i = len(self.ops)
        pk = [k for k in reads if k in self.PSUM_KEYS]
        if pk:
            reads = [k for k in reads if k not in self.PSUM_KEYS]
            writes = list(writes) + pk
        deps = set()
        for k in reads:
            w = self.lw.get(k)
            if w is not None:
                deps.add(w)
        for k in writes:
            w = self.lw.get(k)
            if w is not None:
                deps.add(w)
            for r in self.rd.get(k, ()):
                deps.add(r)
        deps.discard(i)
        for k in reads:
            self.rd.setdefault(k, []).append(i)
        for k in writes:
            self.lw[k] = i
            self.rd[k] = []
        import sys as _s
        self.ops.append(dict(eng=eng, fn=fn, deps=deps, dma=dma_slot, sig=False, line=_s._getframe(1).f_lineno, epoch=self.epoch))
        return i

    def emit(self, final_wait_ops=()):
        import os
        nc = self.nc
        lim = int(os.environ.get('KLIMIT', '0'))
        print('total ops', len(self.ops))
        if lim:
            self.ops = self.ops[:lim]
            final_wait_ops = ()
            print('total ops limited to', lim)
        ops = self.ops
        print('nops', len(ops))
        for o in ops:
            nd = set()
            for d in o["deps"]:
                p = ops[d]
                if p["dma"] is None and o["dma"] is None and p["eng"] == "pe" and o["eng"] == "pe":
                    continue
                nd.add(d)
            own = [d for d in nd if ops[d]["dma"] is None and o["dma"] is None and ops[d]["eng"] == o["eng"]]
            if len(own) > 1:
                keep = max(own)
                nd = set(d for d in nd if d not in own or d == keep)
            o["deps"] = nd
            for d in nd:
                ops[d]["sig"] = True
        for d in final_wait_ops:
            ops[d]["sig"] = True
        for o in ops:
            if o["dma"] is not None:
                o["sig"] = True
        counts = {}
        for i, o in enumerate(ops):
            if not o["sig"]:
                continue
            if o["dma"] is not None:
                key = ("dma", o["dma"])
                inc = 16
            else:
                own = [d for d in o["deps"] if ops[d]["dma"] is None and ops[d]["eng"] == o["eng"]]
                par = 0
                if own:
                    par = 1 - ops[max(own)]["semkey"][2]
                key = ("eng", o["eng"], par, o["epoch"])
                inc = 1
            counts[key] = counts.get(key, 0) + inc
            o["semkey"] = key
            o["ticket"] = counts[key]
            o["inc"] = inc
        keys = list(counts.keys())
        if os.environ.get('KDEBUG'):
            print('sem counts', {str(k): v for k, v in counts.items()})
        with contextlib.ExitStack() as es:
            sems = {}
            for n, k in enumerate(keys):
                sems[k] = es.enter_context(nc.semaphore("s%d" % n))
            block = es.enter_context(nc.Block())
            per_eng = {}
            for i, o in enumerate(ops):
                per_eng.setdefault(o["eng"], []).append(i)
            if final_wait_ops and "pool" not in per_eng:
                per_eng["pool"] = []

            def make(engname, idxs):
                def body(eng):
                    waited = {}
                    for i in idxs:
                        o = ops[i]
                        need = {}
                        for d in o["deps"]:
                            p = ops[d]
                            k = p["semkey"]
                            need[k] = max(need.get(k, 0), p["ticket"])
                        for k, v in need.items():
                            if waited.get(k, 0) >= v:
                                continue
                            eng.wait_ge(sems[k], v)
                            waited[k] = v
                        ins = o["fn"](eng)
                        if o["sig"]:
                            ins.then_inc(sems[o["semkey"]], o["inc"])
                    if engname == "pool":
                        for d in final_wait_ops:
                            p = ops[d]
                            eng.wait_ge(sems[p["semkey"]], p["ticket"])
                return body

            for engname, idxs in per_eng.items():
                getattr(block, ENG_ATTR[engname])(make(engname, idxs))
        return len(keys)


def cmp_to_sel_map(nc_, ns):
    c0 = np.arange(nc_) * 16
    c1 = c0 + 32
    s0 = np.arange(ns) * 64
    s1 = s0 + 64
    ov = np.clip(np.minimum(c1[:, None], s1[None, :]) - np.maximum(c0[:, None], s0[None, :]), 0, None)
    return (ov / 32).astype(np.float32)


def host_consts():
    k = np.arange(128)[:, None]
    q = np.arange(128)[None, :]
    c = {}
    c["ident"] = np.eye(128, dtype=np.float32)
    c["tri_le"] = (k <= q).astype(np.float32)
    c["tri_gt"] = (k > q).astype(np.float32)
    c["d16"] = (q - 16 * k).astype(np.float32)
    sm = np.zeros((512, 128), np.float32)
    sm[:511] = cmp_to_sel_map(511, 128)
    c["selmap"] = np.ascontiguousarray(sm.reshape(4, 128, 128).transpose(1, 0, 2))
    tq = np.arange(128)[:, None]
    jr = np.arange(256)[None, :] - 128
    cur = (tq >= 64).astype(np.int64)
    forced = (jr == cur) | (jr == cur - 1)
    valid = jr <= cur
    c["tmul"] = (valid & ~forced).astype(np.float32)
    c["tadd"] = np.where(forced, 1e4, np.where(valid, 0.0, -1.0)).astype(np.float32)
    band = np.zeros((128, 4, 128), np.float32)
    bandp = np.zeros((128, 4, 128), np.float32)
    band0 = np.zeros((128, 4, 128), np.float32)
    s = np.arange(128)[:, None]
    t = np.arange(128)[None, :]
    for g, w in enumerate((2, 4, 8, 16)):
        band[:, g, :] = ((s <= t) & (s >= t - w + 1)) / float(w) - (s == t)
        bandp[:, g, :] = ((s - 128 >= t - w + 1) & (s - 128 <= t)) / float(w)
        cnt = np.minimum(t + 1, w).astype(np.float32)
        band0[:, g, :] = ((s <= t) & (s >= t - w + 1)) / cnt - (s == t)
    c["band"] = band
    c["bandp"] = bandp
    c["band0"] = band0
    return c


SEGS = [(0, 512, 0),
        (768, 896, 512), (1024, 1152, 640), (2328, 2584, 768),
        (512, 640, 1024), (640, 768, 1152), (896, 1024, 1280), (1152, 1280, 1408),
        (1304, 1816, 1536),
        (2072, 2328, 2048), (2584, 2840, 2304),
        (1816, 2072, 2560), (1280, 1304, 2816)]
R_GQ, R_GK1, R_GKC, R_GKM, R_PSC, R_INVF, R_GN, R_GM, R_END = 0, 64, 576, 704, 960, 1216, 1224, 1232, 1240


def build(NT):
    S = 128 * NT
    nc = bass.Bass("TRN2", target_bir_lowering=False)

    def din(name, shape, dt=F32):
        return nc.dram_tensor(name, shape, dt, kind="ExternalInput").ap()

    x_d = din("x", [S, 1024])
    mem_d = din("mem", [256, 1024])
    posl_d = din("posl", [128, NT], I32)
    posc_d = din("posc", [8, NT], I32)
    win_d = din("w_in", [1024, 2840])
    wout_d = din("w_out", [1024, 1024])
    wmem_d = din("w_mem_kv", [1024, 512])
    w1k_d = din("w_cmp_k1", [32, 64, 64])
    w1v_d = din("w_cmp_v1", [32, 64, 64])
    w2k_d = din("w_cmp_k2", [64, 64])
    w2v_d = din("w_cmp_v2", [64, 64])
    pk_d = din("cmp_pos_k", [32, 64])
    pv_d = din("cmp_pos_v", [32, 64])
    wpool_d = din("w_pool", [4, 64, 64])
    rep_d = din("rep", [128, R_END])
    cst = {n: din("c_" + n, list(v.shape)) for n, v in host_consts().items()}
    out_d = nc.dram_tensor("out", [S, 1024], F32, kind="ExternalOutput").ap()

    with contextlib.ExitStack() as es:
        def sb(name, shape, dt=F32):
            return es.enter_context(nc.sbuf_tensor(name, shape, dt))

        def ps(name, shape, dt=F32):
            return es.enter_context(nc.psum_tensor(name, shape, dt))

        P = Prog(nc)
        rr = [0]

        def anyeng():
            rr[0] += 1
            return ("dve", "pool", "act")[rr[0] % 3]

        Wb = sb("Wb", [128, 8, 2840], BF16)
        Woutb = sb("Woutb", [128, 8, 1024], BF16)
        KE = [sb("KE%d" % g, [128, S], BF16) for g in range(2)]
        VS = sb("VS", [128, NT, 2, 65], BF16)
        KW = [sb("KW%d" % g, [64, 6, 128], BF16) for g in range(2)]
        VW = sb("VW", [128, 6, 2, 65], BF16)
        KCT = [sb("KCT%d" % g, [64, 520], BF16) for g in range(2)]
        VC = sb("VC", [128, 4, 2, 65], BF16)
        HT = [sb("HT%d" % kv, [128, 520], BF16) for kv in range(2)]
        W1bd = [sb("W1bd%d" % kv, [128, 32, 128], BF16) for kv in range(2)]
        W2bd = [sb("W2bd%d" % kv, [128, 128], BF16) for kv in range(2)]
        CVEC = sb("CVEC", [128, 2])
        RAWT = [sb("RAWT%d" % kv, [128, 144], BF16) for kv in range(2)]
        POST = sb("POST", [128, 2, 32], BF16)
        QB2 = [[sb("QB%d_%d" % (par, g), [128, 2, 512], BF16) for g in range(2)] for par in range(2)]
        QMT = sb("QMT", [64, 512], BF16)
        MKT = sb("MKT", [64, 4, 256], BF16)
        MV = sb("MV", [128, 2, 4, 65], BF16)
        XT = [sb("XT%d" % j, [128, 1024]) for j in range(2)]
        Xb = sb("Xb", [128, 1024], BF16)
        XTr = sb("XTr", [128, 8, 128], BF16)
        ST = sb("ST", [128, 16])
        RSTD = sb("RSTD", [128, 2])
        Qf = sb("Qf", [128, 8, 64])
        Qs = sb("Qs", [128, 8, 64])
        Qb16 = sb("Qb16", [128, 8, 64], BF16)
        Kb16 = sb("Kb16", [128, 8, 64], BF16)
        RT = sb("RT", [128, 8, 4, 8])
        CKV = sb("CKV", [128, 256], BF16)
        SZ = [sb("SZ%d" % j, [128, 512], BF16) for j in range(2)]
        EZ = sb("EZ", [128, 512])
        VP = [sb("VP%d" % j, [128, 256], BF16) for j in range(2)]
        GATE = sb("GATE", [128, 24])
        PLT = sb("PLT", [64, 4, 128], BF16)
        Wpl = sb("Wpl", [64, 4, 64], BF16)
        PTb = [sb("PT%d" % j, [128, 512], BF16) for j in range(3)]
        CMs = [sb("CM%d" % j, [128, 128], BF16) for j in range(2)]
        cmc = [0]
        IMP = sb("IMP", [128, 128])
        WK = sb("WK", [128, 128])
        MX = sb("MX", [128, 16])
        BIASg = [sb("BIAS%d" % g, [128, 192], BF16) for g in range(2)]
        DEN = sb("DEN", [128, 8])
        OACC = sb("OACC", [128, 8, 64])
        OTMP = sb("OTMP", [128, 4, 64])
        OTS = [sb("OTS%d" % j, [65, 512], BF16) for j in range(2)]
        otc = [0]
        Y = Xb
        YM = sb("YM", [128, 256], BF16)
        GX = sb("GX", [128, 4, 8])
        K8 = sb("K8", [8, 2, 64])
        K8s = sb("K8s", [8, 2, 64])
        K8b = sb("K8b", [8, 2, 64], BF16)
        K8r = sb("K8r", [8, 2, 4, 8])
        K8st = sb("K8st", [8, 4])
        REP = sb("REP", [128, R_END])
        POSF = sb("POSF", [128, NT])
        POSI = sb("POSI", [128, NT], I32)
        COS = sb("COS", [128, NT, 8])
        SIN = sb("SIN", [128, NT, 8])
        IDf = sb("IDf", [128, 128])
        IDb = sb("IDb", [128, 128], BF16)
        TLE = sb("TLE", [128, 128], BF16)
        TGT = sb("TGT", [128, 128], BF16)
        D16 = sb("D16", [128, 128])
        SELM = sb("SELM", [128, 4, 128], BF16)
        TMUL = sb("TMUL", [128, 256])
        TADD = sb("TADD", [128, 256])
        BAND = sb("BAND", [128, 4, 128], BF16)
        BANDP = sb("BANDP", [128, 4, 128], BF16)
        BAND0 = sb("BAND0", [128, 4, 128], BF16)
        pX = ps("pX", [128, 1024], BF16)
        pA = [ps("pA%d" % j, [128, 512]) for j in range(3)]
        pO = [ps("pO%d" % j, [128, 4, 128]) for j in range(1)]
        pI = ps("pI", [128, 4, 128])
        pM0_ = ps("pM0", [128, 512])
        pM = [pM0_, pM0_]
        pT = ps("pT", [128, 4, 128], BF16)

        fin = []

        def load(dst_ap, src_ap, key, slot, eng="sp"):
            return P.op(eng, lambda e: e.dma_start(out=dst_ap, in_=src_ap), writes=[key], dma_slot=slot)

        load(REP[:], rep_d, "REP", "REP")
        load(IDf[:], cst["ident"], "IDf", "IDf")
        P.op("dve", lambda e: e.tensor_copy(out=IDb[:], in_=IDf[:]), reads=["IDf"], writes=["IDb"])
        load(D16[:], cst["d16"], "D16", "D16")
        load(TMUL[:], cst["tmul"], "TMUL", "TMUL")
        load(TADD[:], cst["tadd"], "TADD", "TADD")
        for (bt, bn) in ((BAND, "band"), (BANDP, "bandp"), (BAND0, "band0")):
            load(XT[1][:, 0:512], cst[bn].rearrange("p g t -> p (g t)"), "XT1", "XT1")
            P.op("dve", lambda e, bt=bt: e.tensor_copy(out=bt[:].rearrange("p g t -> p (g t)"), in_=XT[1][:, 0:512]), reads=["XT1"], writes=[bn.upper()])
        load(XT[0][:, 0:128], cst["tri_le"], "XT0", "XT0")
        P.op("dve", lambda e: e.tensor_scalar(out=TLE[:], in0=XT[0][:, 0:128], scalar1=-1.0, scalar2=-NEG, op0=ALU.add, op1=ALU.mult), reads=["XT0"], writes=["NTLE"])
        load(XT[0][:, 0:128], cst["tri_gt"], "XT0", "XT0")
        P.op("dve", lambda e: e.tensor_scalar(out=TGT[:], in0=XT[0][:, 0:128], scalar1=-1.0, scalar2=-NEG, op0=ALU.add, op1=ALU.mult), reads=["XT0"], writes=["NTGT"])
        NEGM = {"TLE": TLE, "TGT": TGT}
        load(XT[0][:, 0:512], cst["selmap"].rearrange("p c j -> p (c j)"), "XT0", "XT0")
        P.op("dve", lambda e: e.tensor_copy(out=SELM[:].rearrange("p c j -> p (c j)"), in_=XT[0][:, 0:512]), reads=["XT0"], writes=["SELM"])
        load(POSI[:], posl_d, "POSI", "POSI")
        P.op("dve", lambda e: e.tensor_copy(out=POSF[:], in_=POSI[:]), reads=["POSI"], writes=["POSF"])

        def cos_sin(COSt, SINt, ANGt, POSFt, npart, n, kp, KIt, KFt):
            invf = REP[0:npart, R_INVF:R_INVF + 8]
            P.op("dve", lambda e: e.tensor_tensor(out=ANGt, in0=POSFt.unsqueeze(2).to_broadcast([npart, n, 8]),
                                                  in1=invf.unsqueeze(1).to_broadcast([npart, n, 8]), op=ALU.mult),
                 reads=[kp + "POSF", "REP"], writes=["qsrc"])
            for dst, off, nm in ((SINt, 0.5, "SIN"), (COSt, 0.75, "COS")):
                P.op("dve", lambda e, dst=dst, off=off: e.tensor_scalar(out=dst, in0=ANGt, scalar1=float(1.0 / (2 * np.pi)), scalar2=float(off),
                                                                        op0=ALU.mult, op1=ALU.add), reads=["qsrc"], writes=[kp + nm])
                P.op("dve", lambda e, dst=dst: e.tensor_copy(out=KIt, in_=dst), reads=[kp + nm], writes=["qsq"])
                P.op("dve", lambda e: e.tensor_copy(out=KFt, in_=KIt), reads=["qsq"], writes=["EZ"])
                P.op("dve", lambda e, dst=dst: e.tensor_tensor(out=dst, in0=dst, in1=KFt, op=ALU.subtract), reads=[kp + nm, "EZ"], writes=[kp + nm])
                P.op("dve", lambda e, dst=dst: e.tensor_scalar(out=KFt, in0=dst, scalar1=0.0, scalar2=None, op0=ALU.is_lt), reads=[kp + nm], writes=["EZ"])
                P.op("dve", lambda e, dst=dst: e.tensor_tensor(out=dst, in0=dst, in1=KFt, op=ALU.add), reads=[kp + nm, "EZ"], writes=[kp + nm])
                P.op("act", lambda e, dst=dst: e.activation(out=dst, in_=dst, func=AF.Sin, scale=float(2 * np.pi), bias=float(-np.pi)), reads=[kp + nm], writes=[kp + nm])

        assert NT * 8 <= 512
        ANG = Qf[:].rearrange("p h d -> p (h d)")[:, 0:NT * 8].rearrange("p (n j) -> p n j", j=8)
        KI = Qs[:].rearrange("p h d -> p (h d)")[:, 0:NT * 8].bitcast(I32).rearrange("p (n j) -> p n j", j=8)
        KF = EZ[:, 0:NT * 8].rearrange("p (n j) -> p n j", j=8)
        cos_sin(COS[:], SIN[:], ANG, POSF[:], 128, NT, "", KI, KF)
        POSCI = sb("POSCI", [8, NT], I32)
        POSCF = sb("POSCF", [8, NT])
        COSC = sb("COSC", [8, NT, 8])
        SINC = sb("SINC", [8, NT, 8])
        load(POSCI[:], posc_d, "cPOSI", "cPOSI")
        P.op("dve", lambda e: e.tensor_copy(out=POSCF[:], in_=POSCI[:]), reads=["cPOSI"], writes=["cPOSF"])
        cos_sin(COSC[:], SINC[:], ANG[0:8], POSCF[:], 8, NT, "c", KI[0:8], KF[0:8])

        stg = [(XT[0], "XT0"), (XT[1], "XT1")]
        sidx = [0]

        def stage():
            sidx[0] += 1
            return stg[sidx[0] % 2]

        def conv(out_ap, in_ap, scal, rkeys, wkeys):
            eng = anyeng()
            if scal is None:
                if eng == "act":
                    P.op("act", lambda e: e.copy(out=out_ap, in_=in_ap), reads=rkeys, writes=wkeys)
                else:
                    P.op(eng, lambda e: e.tensor_copy(out=out_ap, in_=in_ap), reads=rkeys, writes=wkeys)
            else:
                if eng == "act":
                    P.op("act", lambda e: e.mul(out=out_ap, in_=in_ap, mul=scal), reads=rkeys + ["REP"], writes=wkeys)
                else:
                    P.op(eng, lambda e: e.tensor_scalar(out=out_ap, in0=in_ap, scalar1=scal, scalar2=None, op0=ALU.mult),
                         reads=rkeys + ["REP"], writes=wkeys)

        for kc in range(8):
            for (a, b) in ((0, 1024), (1024, 2048), (2048, 2840)):
                T_, tk = stage()
                load(T_[:, 0:b - a], win_d[kc * 128:(kc + 1) * 128, a:b], tk, tk)
                for (s0, s1, d0) in SEGS:
                    lo, hi = max(a, s0), min(b, s1)
                    if lo < hi:
                        conv(Wb[:, kc, d0 + lo - s0:d0 + hi - s0], T_[:, lo - a:hi - a], REP[:, R_GN + kc:R_GN + kc + 1], [tk], ["Wb"])
        for kc in range(8):
            T_, tk = stage()
            load(T_[:], wout_d[kc * 128:(kc + 1) * 128, :], tk, tk)
            conv(Woutb[:, kc, :], T_[:], None, [tk], ["Woutb"])
        for kv, (w1_d, w2_d, p_d) in enumerate(((w1k_d, w2k_d, pk_d), (w1v_d, w2v_d, pv_d))):
            P.op("pool", lambda e, kv=kv: e.memset(W1bd[kv][:], 0.0), writes=["W1bd%d" % kv])
            P.op("pool", lambda e, kv=kv: e.memset(W2bd[kv][:], 0.0), writes=["W2bd%d" % kv])
            for g in range(2):
                for half in range(2):
                    T_, tk = stage()
                    Tv = T_[g * 64:(g + 1) * 64, 0:1024].rearrange("p (l e) -> p l e", e=64)
                    P.op("sp", lambda e, Tv=Tv, half=half, w1_d=w1_d: e.dma_start(
                        out=Tv, in_=w1_d[half * 16:(half + 1) * 16].rearrange("l d e -> d l e")), writes=[tk], dma_slot=tk)
                    conv(W1bd[kv][g * 64:(g + 1) * 64, half * 16:(half + 1) * 16, g * 64:(g + 1) * 64], Tv, None, [tk], ["W1bd%d" % kv])
                T_, tk = stage()
                load(T_[g * 64:(g + 1) * 64, 0:64], w2_d, tk, tk)
                conv(W2bd[kv][g * 64:(g + 1) * 64, g * 64:(g + 1) * 64], T_[g * 64:(g + 1) * 64, 0:64], None, [tk], ["W2bd%d" % kv])
                T_, tk = stage()
                P.op("sp", lambda e, T_=T_, g=g, p_d=p_d: e.dma_start(out=T_[g * 64:(g + 1) * 64, 0:32], in_=p_d.rearrange("l d -> d l"),
                                                                       allow_slow_non_contiguous=True), writes=[tk], dma_slot=tk)
                conv(POST[g * 64:(g + 1) * 64, kv, :], T_[g * 64:(g + 1) * 64, 0:32], None, [tk], ["POST"])
        T_, tk = stage()
        load(T_[0:64, 0:256].rearrange("p (g e) -> p g e", g=4), wpool_d.rearrange("g c e -> c g e"), tk, tk)
        conv(Wpl[:], T_[0:64, 0:256].rearrange("p (g e) -> p g e", g=4), None, [tk], ["Wpl"])
        for kv in range(2):
            for l in range(32):
                P.op("pe", lambda e, kv=kv, l=l: e.matmul(pM[0][:, kv:kv + 1], lhsT=W1bd[kv][:, l, :], rhs=POST[:, kv, l:l + 1],
                                                          start=(l == 0), stop=(l == 31)),
                     reads=["W1bd%d" % kv, "POST"], writes=["pM0"])
        P.op("dve", lambda e: e.tensor_copy(out=CVEC[:], in_=pM[0][:, 0:2]), reads=["pM0"], writes=["CVEC"])
        for g in range(2):
            P.op("pool", lambda e, g=g: e.memset(KE[g][:], 1.0), writes=["KEinit%d" % g])
            for h0 in range(0, S, 4096):
                wd = min(4096, S - h0)
                v = KE[g][:, h0:h0 + wd]
                P.op("pool", lambda e, v=v, wd=wd: e.affine_select(out=v, in_=v, pattern=[[1, wd]], compare_op=ALU.is_ge, fill=0.0,
                                                                   base=4096, channel_multiplier=-64), writes=["KEinit%d" % g])
                P.op("pool", lambda e, v=v, wd=wd: e.affine_select(out=v, in_=v, pattern=[[-1, wd]], compare_op=ALU.is_ge, fill=0.0,
                                                                   base=-4096 + 63, channel_multiplier=64), writes=["KEinit%d" % g])
        P.op("pool", lambda e: e.memset(VS[:], 1.0), writes=["VSinit"])
        P.op("pool", lambda e: e.memset(VW[:], 1.0), writes=["VWinit"])
        P.op("pool", lambda e: e.memset(VC[:], 1.0), writes=["VCinit"])
        P.op("pool", lambda e: e.memset(MV[:], 1.0), writes=["MVinit"])
        for kv in range(2):
            P.op("pool", lambda e, kv=kv: e.memset(HT[kv][:], 0.0), writes=["HT%d" % kv])
            P.op("pool", lambda e, kv=kv: e.memset(RAWT[kv][:], 0.0), writes=["RAWT%d" % kv])
        for g in range(2):
            P.op("pool", lambda e, g=g: e.memset(KCT[g][:], 0.0), writes=["KCT%d" % g])
            for par in range(2):
                P.op("pool", lambda e, g=g, par=par: e.memset(QB2[par][g][:], 0.0), writes=["QB%d_%d" % (par, g), "QBb%d_%d" % (par, g), "QBc%d_%d" % (par, g)])

        def rstd_from_ss(ss_ap, out_ap, n, rk, wk):
            P.op("act", lambda e: e.activation(out=out_ap, in_=ss_ap, func=AF.Ln, scale=1.0 / n, bias=EPS), reads=rk, writes=wk)
            P.op("act", lambda e: e.activation(out=out_ap, in_=out_ap, func=AF.Exp, scale=-0.5), reads=wk, writes=wk)

        def token_norm_T(src, skey):
            P.op("act", lambda e: e.activation(out=Xb[:], in_=src[:], func=AF.Square, accum_out=ST[:, 0:1]),
                 reads=[skey], writes=["Xb", "ST"])
            rstd_from_ss(ST[:, 0:1], RSTD[:, 0:1], 1024.0, ["ST"], ["RSTD"])
            P.op("dve", lambda e: e.tensor_scalar(out=RSTD[:, 1:2], in0=RSTD[:, 0:1], scalar1=-1.0, scalar2=None, op0=ALU.mult),
                 reads=["RSTD"], writes=["RSTD"])
            P.op("dve", lambda e: e.tensor_copy(out=Xb[:], in_=src[:]), reads=[skey], writes=["Xb"])
            for kc in range(8):
                P.op("pe", lambda e, kc=kc: e.transpose(out=pX[:, kc * 128:(kc + 1) * 128], in_=Xb[:, kc * 128:(kc + 1) * 128], identity=IDb[:]),
                     reads=["Xb", "IDb"], writes=["pX"])
            P.op("act", lambda e: e.copy(out=XTr[:].rearrange("p k t -> p (k t)"), in_=pX[:]), reads=["pX"], writes=["XTr"])

        def proj(W, wkey, c0, c1, pt, pkey):
            for kc in range(8):
                P.op("pe", lambda e, kc=kc: e.matmul(pt[:, 0:c1 - c0], lhsT=XTr[:, kc, :], rhs=W[:, kc, c0:c1], start=(kc == 0), stop=(kc == 7)),
                     reads=["XTr", wkey], writes=[pkey])

        def head_norm_rope(nparts, src3, nh, gain3, cos2, sin2, nrope, kp, tmpS, tmpR, stt, out_b3, src_keys=None, out_key=None, ve="dve"):
            src_keys = src_keys or [kp + "src"]
            out_key = out_key or (kp + "b")
            P.op(ve, lambda e: e.tensor_tensor(out=tmpS, in0=src3, in1=src3, op=ALU.mult), reads=src_keys, writes=[kp + "sq"])
            P.op("dve", lambda e: e.tensor_reduce(out=stt, in_=tmpS, axis=AX.X, op=ALU.add), reads=[kp + "sq"], writes=[kp + "st"])
            rstd_from_ss(stt, stt, 64.0, [kp + "st"], [kp + "st"])
            P.op(ve, lambda e: e.tensor_tensor(out=tmpS, in0=src3, in1=stt.unsqueeze(2).to_broadcast([nparts, nh, 64]), op=ALU.mult),
                 reads=src_keys + [kp + "st"], writes=[kp + "sq"])
            P.op(ve, lambda e: e.tensor_tensor(out=tmpS, in0=tmpS, in1=gain3, op=ALU.mult), reads=[kp + "sq", "REP"], writes=[kp + "sq"])
            if nrope:
                x1 = tmpS[:, 0:nrope, 0:8]
                x2 = tmpS[:, 0:nrope, 8:16]
                cb = cos2.unsqueeze(1).to_broadcast([nparts, nrope, 8])
                sbb = sin2.unsqueeze(1).to_broadcast([nparts, nrope, 8])
                t = [tmpR[:, 0:nrope, j, :] for j in range(4)]
                for (o, a_, b_) in ((t[0], x1, cb), (t[1], x2, sbb), (t[2], x1, sbb), (t[3], x2, cb)):
                    P.op(ve, lambda e, o=o, a_=a_, b_=b_: e.tensor_tensor(out=o, in0=a_, in1=b_, op=ALU.mult),
                         reads=[kp + "sq", kp + "cs"], writes=[kp + "rt"])
                P.op(ve, lambda e: e.tensor_tensor(out=x1, in0=t[0], in1=t[1], op=ALU.subtract), reads=[kp + "rt"], writes=[kp + "sq"])
                P.op(ve, lambda e: e.tensor_tensor(out=x2, in0=t[2], in1=t[3], op=ALU.add), reads=[kp + "rt"], writes=[kp + "sq"])
            P.op(ve, lambda e: e.tensor_copy(out=out_b3, in_=tmpS), reads=[kp + "sq"], writes=[out_key])

        def silu_from_psum(pt, pkey, out_ap, okey, Z, zkey):
            P.op("act", lambda e: e.mul(out=Z, in_=pt[:], mul=RSTD[:, 0:1]), reads=[pkey, "RSTD"], writes=[zkey])
            P.op("act", lambda e: e.activation(out=EZ[:], in_=Z, func=AF.Exp, scale=-1.0), reads=[zkey], writes=["EZ"])
            P.op("dve", lambda e: e.tensor_scalar(out=EZ[:], in0=EZ[:], scalar1=1.0, scalar2=None, op0=ALU.add), reads=["EZ"], writes=["EZ"])
            P.op("dve", lambda e: e.reciprocal(out=EZ[:], in_=EZ[:]), reads=["EZ"], writes=["EZ"])
            P.op("dve", lambda e: e.tensor_tensor(out=out_ap, in0=Z, in1=EZ[:], op=ALU.mult), reads=[zkey, "EZ"], writes=[okey])

        ptc = [0]

        def next_pt():
            ptc[0] += 1
            j = ptc[0] % 3
            return PTb[j], "PT%d" % j

        pac = [0]

        pa_free = [None]

        def next_pa():
            if pa_free[0] is not None:
                j = pa_free[0]
                return pA[j], "pA%d" % j
            pac[0] += 1
            j = pac[0] % 3
            return pA[j], "pA%d" % j

        poc = [0]

        def next_po():
            return pO[0], "pO0"

        def mk_desc(lhsT, lkeys, rhs, rkeys, m, mask, v_rhs, vkeys, po, pokey, first, last, imp_kc=None, pre=None, post=None):
            def qk(pa, pak):
                add = mask is not None and mask[1] in ("TLE", "TGT")
                P.op("pe", lambda e: e.matmul(pa[0:m, :], lhsT=lhsT, rhs=rhs, start=True, stop=not add), reads=lkeys + rkeys, writes=[pak])
                if add:
                    nb = NEGM[mask[1]]
                    P.op("pe", lambda e: e.matmul(pa[:, :].rearrange("p (r q) -> p r q", r=4), lhsT=IDb[:], rhs=nb[:].unsqueeze(1).to_broadcast([128, 4, 128]),
                                                  start=False, stop=True), reads=["IDb", "N" + mask[1]], writes=[pak])

            def pv(pt, ptk):
                P.op("pe", lambda e: e.matmul(po[0:65, :, :].rearrange("p r q -> p (r q)"), lhsT=v_rhs(0), rhs=pt[0:m, :], start=first, stop=last),
                     reads=[ptk] + vkeys, writes=[pokey])
                if imp_kc is not None:
                    for r in range(4):
                        P.op("pe", lambda e, r=r: e.matmul(pI[:, r, :], lhsT=pt[0:m, r * 128:(r + 1) * 128], rhs=SELM[0:m, imp_kc, :], start=(first and r == 0), stop=last, skip_group_check=True),
                             reads=[ptk, "SELM"], writes=["pI"])
            return dict(m=m, qk=qk, pv=pv, mask=mask, pre=pre, post=post)

        def run_pipeline(descs, hooks=()):
            n = len(descs)
            hooks = sorted(hooks, key=lambda x: x[0])
            hp = [0]

            def run_hooks(k):
                while hp[0] < len(hooks) and hooks[hp[0]][0] <= k:
                    hooks[hp[0]][1]()
                    hp[0] += 1
            bufs = [None] * n

            def issue(k):
                d = descs[k]
                if d["pre"] is not None:
                    d["pre"]()
                pa, pak = pA[k % 3], "pA%d" % (k % 3)
                pt, ptk = next_pt()
                bufs[k] = (pa, pak, pt, ptk)
                d["qk"](pa, pak)
            issue(0)
            if n > 1:
                issue(1)
            deferred = []
            for k in range(n):
                for (due, fn) in [x for x in deferred if x[0] <= k]:
                    fn()
                deferred = [x for x in deferred if x[0] > k]
                if k + 2 < n:
                    issue(k + 2)
                d = descs[k]
                pa, pak, pt, ptk = bufs[k]
                m = d["m"]
                P.op("act", lambda e, pa=pa, pt=pt, m=m: e.activation(out=pt[0:m, :], in_=pa[0:m, :], func=AF.Exp, scale=0.125), reads=[pak], writes=[ptk])
                if d["mask"] is not None and d["mask"][1] not in ("TLE", "TGT"):
                    mt, mk = d["mask"]
                    P.op("pool", lambda e, pt=pt, m=m, mt=mt: e.tensor_tensor(out=pt[0:m, :].rearrange("p (r q) -> p r q", r=4), in0=pt[0:m, :].rearrange("p (r q) -> p r q", r=4),
                                                                              in1=mt[0:m, :].unsqueeze(1).to_broadcast([m, 4, 128]), op=ALU.mult),
                         reads=[ptk, mk], writes=[ptk])
                d["pv"](pt, ptk)
                if d["post"] is not None:
                    later = d["post"]()
                    if later is not None:
                        deferred.append((k + 2, later))
                pa_free[0] = k % 3
                run_hooks(k)
                pa_free[0] = None
            for (due, fn) in deferred:
                fn()
            run_hooks(10 ** 9)

        def finish_branch(po, pokey, fn):
            otc[0] += 1
            j = otc[0] % 2
            ots, otk = OTS[j], "OTS%d" % j
            src = po[0:65, :, :].rearrange("p r q -> p (r q)")
            P.op("act", lambda e: e.copy(out=ots[:], in_=src), reads=[pokey], writes=[otk])

            def later():
                for r in range(4):
                    P.op("pe", lambda e, r=r: e.transpose(out=pT[:, r, 0:65], in_=ots[:, r * 128:(r + 1) * 128], identity=IDb[0:65, 0:65]),
                         reads=[otk, "IDb"], writes=["pT"])
                fn()
            return later

        def combine(po, pokey, g, b, first):
            den = DEN[:, 0:4]
            P.op("dve", lambda e: e.tensor_scalar(out=den, in0=po[:, :, 64], scalar1=1e-30, scalar2=None, op0=ALU.max), reads=[pokey], writes=["DEN"])
            P.op("dve", lambda e: e.reciprocal(out=den, in_=den), reads=["DEN"], writes=["DEN"])
            if b is not None:
                P.op("dve", lambda e: e.tensor_tensor(out=DEN[:, 4:8], in0=den, in1=GATE[:, 12 * g + b:12 * g + 12:3], op=ALU.mult),
                     reads=["DEN", "GATE"], writes=["DEN2"])
                w = DEN[:, 4:8]
                wk = ["DEN2"]
            else:
                w = den
                wk = ["DEN"]
            dst = OACC[:, 4 * g:4 * g + 4, :]
            wb = w.unsqueeze(2).to_broadcast([128, 4, 64])
            if first:
                P.op("dve", lambda e: e.tensor_tensor(out=dst, in0=po[:, :, 0:64], in1=wb, op=ALU.mult), reads=[pokey] + wk, writes=["OACC%d" % g])
            else:
                P.op("dve", lambda e: e.tensor_tensor(out=OTMP[:], in0=po[:, :, 0:64], in1=wb, op=ALU.mult), reads=[pokey] + wk, writes=["OTMP"])
                P.op("pool", lambda e: e.tensor_tensor(out=dst, in0=dst, in1=OTMP[:], op=ALU.add), reads=["OTMP", "OACC%d" % g], writes=["OACC%d" % g])

        for mt in range(2):
            T_, tk = XT[mt], "XT%d" % mt
            load(T_[:], mem_d[mt * 128:(mt + 1) * 128, :], tk, tk)
            token_norm_T(T_, tk)
            pa, pak = next_pa()
            for kc in range(8):
                sl = kc % 2
                stg_t = (EZ[:], OACC[:].rearrange("p h d -> p (h d)"))[sl]
                stg_k = (["EZ"], ["OACC0", "OACC1"])[sl]
                P.op("sp", lambda e, kc=kc, stg_t=stg_t: e.dma_start(out=stg_t, in_=wmem_d[kc * 128:(kc + 1) * 128, :]), writes=stg_k, dma_slot="wm%d" % sl)
                conv(PTb[sl][:], stg_t, REP[:, R_GM + kc:R_GM + kc + 1], stg_k, ["PT%d" % sl])
                P.op("pe", lambda e, kc=kc, sl=sl, pa=pa: e.matmul(pa[:], lhsT=XTr[:, kc, :], rhs=PTb[sl][:], start=(kc == 0), stop=(kc == 7)),
                     reads=["XTr", "PT%d" % sl], writes=[pak])
            P.op("act", lambda e, pa=pa: e.mul(out=Qf[:, 0:4, :].rearrange("p h d -> p (h d)"), in_=pa[:, 0:256], mul=RSTD[:, 0:1]),
                 reads=[pak, "RSTD"], writes=["qsrc"])
            P.op("dve", lambda e, pa=pa, mt=mt: e.tensor_scalar(out=MV[:, mt, :, 0:64], in0=pa[:, 256:512].rearrange("p (h d) -> p h d", h=4),
                                                                scalar1=RSTD[:, 0:1], scalar2=None, op0=ALU.mult),
                 reads=[pak, "RSTD", "MVinit"], writes=["MV"])
            head_norm_rope(128, Qf[:, 0:4, :], 4, REP[:, R_GKM:R_GKM + 256].rearrange("p (h d) -> p h d", h=4), None, None, 0, "q",
                           Qs[:, 0:4, :], None, ST[:, 8:12], Qb16[:, 0:4, :])
            for h in range(4):
                P.op("pe", lambda e, h=h: e.transpose(out=pX[0:64, h * 128:(h + 1) * 128], in_=Qb16[:, h, :], identity=IDb[:]),
                     reads=["qb", "IDb"], writes=["pX"])
            P.op("act", lambda e, mt=mt: e.copy(out=MKT[:, :, mt * 128:(mt + 1) * 128], in_=pX[0:64, 0:512].rearrange("p (h m) -> p h m", h=4)),
                 reads=["pX"], writes=["MKT"])

        def compress_body(i, kv):
            pm, pmk = pM[kv], "pM0"
            for l in range(32):
                P.op("pe", lambda e, l=l: e.matmul(pm[:, 0:8], lhsT=W1bd[kv][:, l, :], rhs=RAWT[kv][:, l:l + 113:16], start=(l == 0), stop=(l == 31)),
                     reads=["W1bd%d" % kv, "RAWT%d" % kv], writes=[pmk])
            gx = [GX[:, j, :] for j in range(4)]
            P.op("dve", lambda e: e.tensor_scalar(out=gx[0], in0=pm[:, 0:8], scalar1=CVEC[:, kv:kv + 1], scalar2=None, op0=ALU.add),
                 reads=[pmk, "CVEC"], writes=["GX"])
            P.op("pool", lambda e: e.tensor_tensor(out=gx[1], in0=gx[0], in1=gx[0], op=ALU.mult), reads=["GX"], writes=["GX"])
            P.op("pool", lambda e: e.tensor_scalar(out=gx[1], in0=gx[1], scalar1=0.044715, scalar2=1.0, op0=ALU.mult, op1=ALU.add), reads=["GX"], writes=["GX"])
            P.op("pool", lambda e: e.tensor_tensor(out=gx[1], in0=gx[1], in1=gx[0], op=ALU.mult), reads=["GX"], writes=["GX"])
            P.op("act", lambda e: e.activation(out=gx[2], in_=gx[1], func=AF.Exp, scale=-2.0 * GELU_C), reads=["GX"], writes=["GX"])
            P.op("dve", lambda e: e.tensor_scalar(out=gx[2], in0=gx[2], scalar1=1.0, scalar2=None, op0=ALU.add), reads=["GX"], writes=["GX"])
            P.op("dve", lambda e: e.reciprocal(out=gx[2], in_=gx[2]), reads=["GX"], writes=["GX"])
            P.op("dve", lambda e: e.tensor_tensor(out=HT[kv][:, 8 * i:8 * i + 8], in0=gx[2], in1=gx[0], op=ALU.mult), reads=["GX"], writes=["HT%d" % kv])
            P.op("pool", lambda e: e.tensor_copy(out=RAWT[kv][:, 0:16], in_=RAWT[kv][:, 128:144]), reads=["RAWT%d" % kv], writes=["RAWT%d" % kv])

        def cmp_descs(i, g):
            nvis = 8 * i + 7
            nkc = (nvis + 127) // 128
            QBt = QB2[i % 2][g]
            q64 = QBt[0:64, 0, :]
            qk = ["QB%d_%d" % (i % 2, g)]
            po, pok = next_po()
            out = []

            def post():
                return finish_branch(po, pok, post2)

            def post2():
                den = DEN[:, 0:4]
                P.op("dve", lambda e: e.tensor_scalar(out=den, in0=pT[:, :, 64], scalar1=1e-30, scalar2=None, op0=ALU.max), reads=["pT"], writes=["DEN"])
                P.op("dve", lambda e: e.reciprocal(out=den, in_=den), reads=["DEN"], writes=["DEN"])
                P.op("dve", lambda e: e.tensor_scalar(out=IMP[:], in0=pI[:, 0, :], scalar1=DEN[:, 0:1], scalar2=None, op0=ALU.mult), reads=["pI", "DEN"], writes=["IMP"])
                for r in range(1, 4):
                    P.op("dve", lambda e, r=r: e.scalar_tensor_tensor(out=IMP[:], in0=pI[:, r, :], scalar=DEN[:, r:r + 1], in1=IMP[:], op0=ALU.mult, op1=ALU.add),
                         reads=["pI", "DEN", "IMP"], writes=["IMP"])
                P.op("dve", lambda e: e.tensor_tensor(out=IMP[:], in0=IMP[:], in1=TMUL[:, 128 - 2 * i:256 - 2 * i], op=ALU.mult), reads=["IMP", "TMUL"], writes=["IMP"])
                P.op("dve", lambda e: e.tensor_tensor(out=IMP[:], in0=IMP[:], in1=TADD[:, 128 - 2 * i:256 - 2 * i], op=ALU.add), reads=["IMP", "TADD"], writes=["IMP"])
                P.op("dve", lambda e: e.memset(IMP[:, 0:1], 1e4), reads=["IMP"], writes=["IMP"])
                P.op("dve", lambda e: e.max(out=MX[:, 0:8], in_=IMP[:]), reads=["IMP"], writes=["MX"])
                P.op("dve", lambda e: e.match_replace(out=WK[:], in_to_replace=MX[:, 0:8], in_values=IMP[:], imm_value=-3.0e38), reads=["IMP", "MX"], writes=["WK"])
                P.op("dve", lambda e: e.max(out=MX[:, 8:16], in_=WK[:]), reads=["WK"], writes=["MX"])
                B_ = BIASg[g]
                bk_ = "BIAS%d" % g
                P.op("dve", lambda e: e.tensor_scalar(out=B_[:, 0:128], in0=IMP[:], scalar1=MX[:, 15:16], scalar2=NEG, op0=ALU.is_lt, op1=ALU.mult), reads=["IMP", "MX"], writes=[bk_])
                P.op("dve", lambda e: e.tensor_scalar(out=B_[:, 128:192], in0=IMP[:, 0:64], scalar1=MX[:, 15:16], scalar2=NEG, op0=ALU.is_lt, op1=ALU.mult), reads=["IMP", "MX", bk_], writes=[bk_])
                combine(pT, "pT", g, 0, True)

            for kc in range(nkc):
                m = min(128, nvis - 128 * kc)
                full = 16 * (128 * kc + m - 1) + 31 <= 128 * i
                mask = None
                pre = None
                if not full:
                    delta = float(128 * (16 * kc - i) + 31)
                    cmc[0] += 1
                    CM = CMs[cmc[0] % 2]
                    cmk = "CM%d" % (cmc[0] % 2)
                    mask = (CM, cmk)

                    def pre(delta=delta, CM=CM, cmk=cmk):
                        P.op("pool", lambda e: e.tensor_scalar(out=CM[:], in0=D16[:], scalar1=delta, scalar2=None, op0=ALU.is_ge), reads=["D16"], writes=[cmk])
                out.append(mk_desc(KCT[g][:, 1 + kc * 128:1 + kc * 128 + m], ["KCT%d" % g], q64, qk, m, mask,
                                   lambda r, kc=kc, m=m: VC[0:m, kc, g, :], ["VC"], po, pok, kc == 0, kc == nkc - 1, imp_kc=kc, pre=pre,
                                   post=(post if kc == nkc - 1 else None)))
            return out

        def win_descs(i, g):
            QBt = QB2[i % 2][g]
            q64 = QBt[0:64, 0, :]
            qk = ["QB%d_%d" % (i % 2, g)]
            po, pok = next_po()
            kts = [kt for kt in range(i - 4, i + 1) if kt >= 0]
            out = []
            for n_, kt in enumerate(kts):
                mask = (TLE, "TLE") if kt == i else ((TGT, "TGT") if kt == i - 4 else None)
                last = n_ == len(kts) - 1
                out.append(mk_desc(KW[g][:, kt % 6, :], [("KW", g, kt % 6)], q64, qk, 128, mask,
                                   lambda r, kt=kt: VW[:, kt % 6, g, :], [("VW", kt % 6)], po, pok, n_ == 0, last,
                                   post=((lambda: finish_branch(po, pok, lambda: combine(pT, "pT", g, 2, False))) if last else None)))
            return out

        def sel_descs(i, g):
            QBt = QB2[i % 2][g]
            kq, kb, kc1 = "QB%d_%d" % (i % 2, g), "QBb%d_%d" % (i % 2, g), "QBc%d_%d" % (i % 2, g)
            qk = [kq, kb]
            po, pok = next_po()
            B_ = BIASg[g]
            bk_ = "BIAS%d" % g

            def pre():
                P.op("pe", lambda e: e.transpose(out=pT[:, 0, :], in_=B_[:, 64:192], identity=IDb[:]), reads=[bk_, "IDb"], writes=["pT"])
                if NT > 32:
                    P.op("pe", lambda e: e.transpose(out=pT[:, 1, :], in_=B_[:, 0:128], identity=IDb[:]), reads=[bk_, "IDb"], writes=["pT"])
                for r in range(4):
                    if r % 2 == 0:
                        P.op("dve", lambda e, r=r: e.tensor_copy(out=QBt[64:128, 0, r * 128:(r + 1) * 128], in_=pT[64:128, 0, :]), reads=["pT"], writes=[kb])
                    else:
                        P.op("act", lambda e, r=r: e.copy(out=QBt[64:128, 0, r * 128:(r + 1) * 128], in_=pT[64:128, 0, :]), reads=["pT"], writes=[kb])
                    if NT > 32:
                        if r % 2 == 1:
                            P.op("dve", lambda e, r=r: e.tensor_copy(out=QBt[64:128, 1, r * 128:(r + 1) * 128], in_=pT[64:128, 1, :]), reads=["pT"], writes=[kb])
                        else:
                            P.op("act", lambda e, r=r: e.copy(out=QBt[64:128, 1, r * 128:(r + 1) * 128], in_=pT[64:128, 1, :]), reads=["pT"], writes=[kb])
            out = []
            for kt in range(i + 1):
                cv = kt // 32
                mask = (TLE, "TLE") if kt == i else None
                out.append(mk_desc(KE[g][:, kt * 128:(kt + 1) * 128], [("KE", g, kt), "KEinit%d" % g], QBt[:, cv, :],
                                   qk + ([kc1] if cv == 1 else []), 128, mask,
                                   lambda r, kt=kt: VS[:, kt, g, :], [("VS", kt)], po, pok, kt == 0, kt == i,
                                   pre=(pre if kt == 0 else None), post=((lambda: finish_branch(po, pok, lambda: combine(pT, "pT", g, 1, False))) if kt == i else None)))
            return out

        def mem_descs(i):
            po, pok = next_po()
            out = []
            for mt in range(2):
                def qk(pa, pak, mt=mt):
                    for h in range(4):
                        P.op("pe", lambda e, h=h: e.matmul(pa[:, h * 128:(h + 1) * 128], lhsT=MKT[:, h, mt * 128:(mt + 1) * 128],
                                                           rhs=QMT[:, h * 128:(h + 1) * 128], start=True, stop=True), reads=["MKT", "QMT"], writes=[pak])

                def pv(pt, ptk, mt=mt):
                    for h in range(4):
                        P.op("pe", lambda e, h=h: e.matmul(po[0:65, h, :], lhsT=MV[:, mt, h, :], rhs=pt[:, h * 128:(h + 1) * 128],
                                                           start=(mt == 0 and h == 0), stop=(mt == 1), skip_group_check=True), reads=[ptk, "MV"], writes=[pok])

                def post():
                    return finish_branch(po, pok, post2)

                def post2():
                    den = DEN[:, 0:4]
                    P.op("dve", lambda e: e.tensor_scalar(out=den, in0=pT[:, :, 64], scalar1=1e-30, scalar2=None, op0=ALU.max), reads=["pT"], writes=["DEN"])
                    P.op("dve", lambda e: e.reciprocal(out=den, in_=den), reads=["DEN"], writes=["DEN"])
                    P.op("dve", lambda e: e.tensor_tensor(out=OTMP[:], in0=pT[:, :, 0:64], in1=den.unsqueeze(2).to_broadcast([128, 4, 64]), op=ALU.mult),
                         reads=["pT", "DEN"], writes=["OTMP"])
                    P.op("dve", lambda e: e.tensor_tensor(out=YM[:], in0=OTMP[:].rearrange("p h d -> p (h d)"), in1=SZ[1][:, 256:512], op=ALU.mult),
                         reads=["OTMP", "SZ1"], writes=["YM"])
                out.append(dict(m=128, qk=qk, pv=pv, mask=None, pre=None, post=(post if mt == 1 else None)))
            return out

        def front_a_pieces(i):
            X_, xk = XT[i % 2], "XT%d" % (i % 2)
            cosi, sini = COS[:, i, :], SIN[:, i, :]
            par = i % 2
            ws = i % 6
            st = {}

            def p_norm():
                P.op("act", lambda e: e.activation(out=Xb[:], in_=X_[:], func=AF.Square, accum_out=ST[:, 0:1]), reads=[xk], writes=["Xb", "ST"])
                rstd_from_ss(ST[:, 0:1], RSTD[:, 0:1], 1024.0, ["ST"], ["RSTD"])
                P.op("dve", lambda e: e.tensor_scalar(out=RSTD[:, 1:2], in0=RSTD[:, 0:1], scalar1=-1.0, scalar2=None, op0=ALU.mult), reads=["RSTD"], writes=["RSTD"])
                P.op("dve", lambda e: e.tensor_copy(out=Xb[:], in_=X_[:]), reads=[xk], writes=["Xb"])

            def p_xT():
                for kc in range(8):
                    P.op("pe", lambda e, kc=kc: e.transpose(out=pX[:, kc * 128:(kc + 1) * 128], in_=Xb[:, kc * 128:(kc + 1) * 128], identity=IDb[:]),
                         reads=["Xb", "IDb"], writes=["pX"])
                P.op("act", lambda e: e.copy(out=XTr[:].rearrange("p k t -> p (k t)"), in_=pX[:]), reads=["pX"], writes=["XTr"])

            def p_g0():
                pa, pak = pM[0], "pM0"
                proj(Wb, "Wb", 0, 512, pa, pak)
                P.op("act", lambda e: e.mul(out=Qf[:].rearrange("p h d -> p (h d)"), in_=pa[:], mul=RSTD[:, 0:1]), reads=[pak, "RSTD", "COS", "SIN"], writes=["qsrc", "qcs"])

            def p_qchain():
                head_norm_rope(128, Qf[:], 8, REP[:, R_GQ:R_GQ + 64].unsqueeze(1).to_broadcast([128, 8, 64]), cosi, sini, 8, "q", Qs[:], RT[:], ST[:, 8:16], Qb16[:])

            def p_qT():
                for h in range(8):
                    P.op("pe", lambda e, h=h: e.transpose(out=pX[0:64, h * 128:(h + 1) * 128], in_=Qb16[:, h, :], identity=IDb[:]), reads=["qb", "IDb"], writes=["pX"])
                for g in range(2):
                    P.op("act", lambda e, g=g: e.copy(out=QB2[par][g][0:64, 0, :], in_=pX[0:64, g * 512:(g + 1) * 512]), reads=["pX"], writes=["QB%d_%d" % (par, g)])
                    if NT > 32:
                        P.op("dve", lambda e, g=g: e.tensor_copy(out=QB2[par][g][0:64, 1, :], in_=pX[0:64, g * 512:(g + 1) * 512]), reads=["pX"], writes=["QBc%d_%d" % (par, g)])

            def p_g1():
                pa1, pak1 = pM[0], "pM0"
                proj(Wb, "Wb", 512, 1024, pa1, pak1)
                P.op("act", lambda e: e.mul(out=EZ[:], in_=pa1[:], mul=RSTD[:, 0:1]), reads=[pak1, "RSTD"], writes=["EZ"])

            def p_kchain():
                head_norm_rope(128, EZ[:].rearrange("p (h d) -> p h d", h=8), 8, REP[:, R_GK1:R_GK1 + 512].rearrange("p (h d) -> p h d", h=8), cosi, sini, 4, "q",
                               Qs[:], RT[:], ST[:, 8:16], Kb16[:], src_keys=["EZ"], out_key="kkb")

            def p_kT():
                for h in range(8):
                    P.op("pe", lambda e, h=h: e.transpose(out=pX[0:64, h * 128:(h + 1) * 128], in_=Kb16[:, h, :], identity=IDb[:]), reads=["kkb", "IDb"], writes=["pX"])
                for g in range(2):
                    P.op("act", lambda e, g=g: e.copy(out=KE[g][0:64, i * 128:(i + 1) * 128], in_=pX[0:64, g * 128:(g + 1) * 128]),
                         reads=["pX", "KEinit%d" % g], writes=[("KE", g, i)])
                    P.op("dve", lambda e, g=g: e.tensor_copy(out=KW[g][:, ws, :], in_=pX[0:64, (2 + g) * 128:(3 + g) * 128]), reads=["pX"], writes=[("KW", g, ws)])
                P.op("act", lambda e: e.copy(out=QMT[:], in_=pX[0:64, 512:1024]), reads=["pX"], writes=["QMT"])

            def p_g2():
                pa2, pak2 = pM[0], "pM0"
                proj(Wb, "Wb", 1024, 1536, pa2, pak2)
                P.op("act", lambda e: e.mul(out=CKV[:], in_=pa2[:, 0:256], mul=RSTD[:, 0:1]), reads=[pak2, "RSTD"], writes=["CKV"])
                P.op("dve", lambda e: e.tensor_scalar(out=VS[:, i, :, 0:64], in0=pa2[:, 256:384].rearrange("p (g d) -> p g d", g=2), scalar1=RSTD[:, 0:1],
                                                      scalar2=None, op0=ALU.mult), reads=[pak2, "RSTD", "VSinit"], writes=[("VS", i)])
                P.op("dve", lambda e: e.tensor_scalar(out=VW[:, ws, :, 0:64], in0=pa2[:, 384:512].rearrange("p (g d) -> p g d", g=2), scalar1=RSTD[:, 0:1],
                                                      scalar2=None, op0=ALU.mult), reads=[pak2, "RSTD", "VWinit"], writes=[("VW", ws)])

            def p_ckvT():
                for kv in range(2):
                    P.op("pe", lambda e, kv=kv: e.transpose(out=pX[:, kv * 128:(kv + 1) * 128], in_=CKV[:, kv * 128:(kv + 1) * 128], identity=IDb[:]),
                         reads=["CKV", "IDb"], writes=["pX"])
                for kv in range(2):
                    P.op("dve", lambda e, kv=kv: e.tensor_copy(out=RAWT[kv][:, 16:144], in_=pX[:, kv * 128:(kv + 1) * 128]), reads=["pX"], writes=["RAWT%d" % kv])

            def p_k8():
                P.op("pe", lambda e: e.matmul(pM[0][0:8, 0:128], lhsT=HT[0][:, 8 * i:8 * i + 8], rhs=W2bd[0][:], start=True, stop=True),
                     reads=["HT0", "W2bd0"], writes=["pM0"])
                P.op("act", lambda e: e.copy(out=K8[:].rearrange("p g d -> p (g d)"), in_=pM[0][0:8, 0:128]), reads=["pM0", "cCOS", "cSIN"], writes=["ksrc", "kcs"])
                head_norm_rope(8, K8[:], 2, REP[0:8, R_GKC:R_GKC + 128].rearrange("p (g d) -> p g d", g=2), COSC[:, i, :], SINC[:, i, :], 2, "k",
                               K8s[:], K8r[:], K8st[:, 0:2], K8b[:], ve="pool")

            def p_k8T():
                for g in range(2):
                    P.op("pe", lambda e, g=g: e.transpose(out=pX[0:64, g * 8:(g + 1) * 8], in_=K8b[:, g, :], identity=IDb[0:8, 0:8]), reads=["kb", "IDb"], writes=["pX"])
                for g in range(2):
                    P.op("dve", lambda e, g=g: e.tensor_copy(out=KCT[g][:, 8 * i:8 * i + 8], in_=pX[0:64, g * 8:(g + 1) * 8]), reads=["pX"], writes=["KCT%d" % g])

            def p_vc():
                clo, chi = max(8 * i - 1, 0) // 128, (8 * i + 6) // 128
                for c in range(clo, chi + 1):
                    P.op("pe", lambda e, c=c: e.matmul(pM[1][:, 0:128], lhsT=HT[1][:, 1 + c * 128:1 + (c + 1) * 128], rhs=W2bd[1][:], start=True, stop=True),
                         reads=["HT1", "W2bd1"], writes=["pM0"])
                    P.op("act", lambda e, c=c: e.copy(out=VC[:, c, :, 0:64], in_=pM[1][:, 0:128].rearrange("p (g d) -> p g d", g=2)),
                         reads=["pM0", "VCinit"], writes=["VC"])

            return [(0, p_norm), (6, p_xT), (9, p_g0), (10, p_qchain), (11, p_g1), (12, p_kchain), (13, p_g2), (17, p_ckvT),
                    (20, lambda: compress_body(i, 0)), (24, p_qT), (26, lambda: compress_body(i, 1)), (30, p_kT),
                    (34, p_k8), (36, p_vc), (46, p_k8T)]

        def front_b(i):
            pa3, pak3 = next_pa()
            proj(Wb, "Wb", 1536, 2048, pa3, pak3)
            silu_from_psum(pa3, pak3, SZ[0][:], "SZ0", Qf[:].rearrange("p h d -> p (h d)"), "qsrc")
            pa4, pak4 = next_pa()
            proj(Wb, "Wb", 2048, 2560, pa4, pak4)
            silu_from_psum(pa4, pak4, SZ[1][:], "SZ1", Qs[:].rearrange("p h d -> p (h d)"), "qsq")
            pa5, pak5 = next_pa()
            proj(Wb, "Wb", 2560, 2840, pa5, pak5)
            vpc = VP[i % 2]
            P.op("act", lambda e: e.mul(out=vpc[:], in_=pa5[:, 0:256], mul=RSTD[:, 0:1]), reads=[pak5, "RSTD"], writes=["VP%d" % (i % 2)])
            P.op("act", lambda e: e.activation(out=GATE[:], in_=pa5[:, 256:280], func=AF.Exp, scale=RSTD[:, 1:2]), reads=[pak5, "RSTD"], writes=["GATE"])
            P.op("dve", lambda e: e.tensor_scalar(out=GATE[:], in0=GATE[:], scalar1=1.0, scalar2=None, op0=ALU.add), reads=["GATE"], writes=["GATE"])
            P.op("dve", lambda e: e.reciprocal(out=GATE[:], in_=GATE[:]), reads=["GATE"], writes=["GATE"])

        def attention(i, next_pieces):
            descs = []
            for g in range(2):
                descs += cmp_descs(i, g)
                descs += win_descs(i, g)
            descs += mem_descs(i)
            mem_end = len(descs) + 1
            for g in range(2):
                descs += sel_descs(i, g)
            hooks = []
            cur = mem_end
            for (slot, fn) in next_pieces:
                P.capture = []
                fn()
                ops_, P.capture = P.capture, None
                stages = []
                for o in ops_:
                    if stages and stages[-1][-1][0] == o[0]:
                        stages[-1].append(o)
                    else:
                        stages.append([o])
                for stg_ in stages:
                    def replay(stg_=stg_):
                        for (eng, f2, r2, w2, d2) in stg_:
                            P.op(eng, f2, reads=r2, writes=w2, dma_slot=d2)
                    hooks.append((cur, replay))
                    eng0, n_ = stg_[0][0], len(stg_)
                    cur += 1 + (min(2, n_ // 4) if eng0 == "pe" else n_ // 2)
            run_pipeline(descs, hooks)

        def epilogue(i):
            X_, xk = XT[i % 2], "XT%d" % (i % 2)
            vpc, vpp = VP[i % 2], VP[(i + 1) % 2]
            P.op("dve", lambda e: e.tensor_tensor(out=Y[:, 0:512], in0=OACC[:].rearrange("p h d -> p (h d)"), in1=SZ[0][:], op=ALU.mult),
                 reads=["OACC0", "OACC1", "SZ0"], writes=["Xb"])
            bnd = BAND0 if i == 0 else BAND
            bk = "BAND0" if i == 0 else "BAND"
            for g in range(4):
                P.op("pe", lambda e, g=g: e.matmul(pM[0][0:64, g * 128:(g + 1) * 128], lhsT=vpc[:, g * 64:(g + 1) * 64], rhs=bnd[:, g, :],
                                                   start=True, stop=(i == 0)), reads=["VP%d" % (i % 2), bk], writes=["pM0"])
                if i > 0:
                    P.op("pe", lambda e, g=g: e.matmul(pM[0][0:64, g * 128:(g + 1) * 128], lhsT=vpp[:, g * 64:(g + 1) * 64], rhs=BANDP[:, g, :],
                                                       start=False, stop=True), reads=["VP%d" % ((i + 1) % 2), "BANDP"], writes=["pM0"])
            P.op("act", lambda e: e.copy(out=PLT[:].rearrange("p g t -> p (g t)"), in_=pM[0][0:64, :]), reads=["pM0"], writes=["PLT"])
            for g in range(4):
                P.op("pe", lambda e, g=g: e.matmul(pM[1][:, g * 64:(g + 1) * 64], lhsT=PLT[:, g, :], rhs=Wpl[:, g, :], start=True, stop=True),
                     reads=["PLT", "Wpl"], writes=["pM0"])
            P.op("dve", lambda e: e.tensor_tensor(out=EZ[:, 0:256], in0=pM[1][:, 0:256], in1=REP[:, R_PSC:R_PSC + 256], op=ALU.mult), reads=["pM0", "REP"], writes=["EZ"])
            P.op("dve", lambda e: e.tensor_tensor(out=Y[:, 512:768], in0=EZ[:, 0:256], in1=SZ[1][:, 0:256], op=ALU.mult), reads=["EZ", "SZ1"], writes=["Xb"])

        def epilogue_b(i):
            X_, xk = XT[i % 2], "XT%d" % (i % 2)
            for kc in range(8):
                src = Y[:, kc * 128:(kc + 1) * 128] if kc < 6 else YM[:, (kc - 6) * 128:(kc - 5) * 128]
                P.op("pe", lambda e, kc=kc, src=src: e.transpose(out=pX[:, kc * 128:(kc + 1) * 128], in_=src, identity=IDb[:]),
                     reads=["Xb", "YM", "IDb"], writes=["pX"])
            for hh in range(2):
                P.op("act" if hh == 0 else "dve", (lambda e, hh=hh: e.copy(out=PTb[hh][:], in_=pX[:, hh * 512:(hh + 1) * 512])) if hh == 0 else
                     (lambda e, hh=hh: e.tensor_copy(out=PTb[hh][:], in_=pX[:, hh * 512:(hh + 1) * 512])), reads=["pX"], writes=["PT%d" % hh])
            for hf in range(2):
                pao, pako = next_pa()
                for kc in range(8):
                    P.op("pe", lambda e, kc=kc, pao=pao, hf=hf: e.matmul(pao[:], lhsT=PTb[kc // 4][:, (kc % 4) * 128:(kc % 4 + 1) * 128],
                                                                         rhs=Woutb[:, kc, hf * 512:(hf + 1) * 512], start=(kc == 0), stop=(kc == 7)),
                         reads=["PT%d" % (kc // 4), "Woutb"], writes=[pako])
                P.op("dve", lambda e, pao=pao, hf=hf: e.tensor_tensor(out=X_[:, hf * 512:(hf + 1) * 512], in0=pao[:], in1=X_[:, hf * 512:(hf + 1) * 512], op=ALU.add),
                     reads=[pako, xk], writes=[xk])
            fin.append(P.op("pool", lambda e: e.dma_start(out=out_d[i * 128:(i + 1) * 128, :], in_=X_[:]), reads=[xk], dma_slot=xk + "st"))

        load(XT[0][:], x_d[0:128, :], "XT0", "XT0")
        P.epoch = 1
        for (slot, fn) in front_a_pieces(0):
            fn()
        for i in range(NT):
            P.epoch = 1 + i // 8
            if i + 1 < NT:
                load(XT[(i + 1) % 2][:], x_d[(i + 1) * 128:(i + 2) * 128, :], "XT%d" % ((i + 1) % 2), "XT%d" % ((i + 1) % 2))
            if i == 0:
                front_b(0)
            attention(i, front_a_pieces(i + 1) if i + 1 < NT else [])
            epilogue(i)
            if i + 1 < NT:
                front_b(i + 1)
            epilogue_b(i)
        P.emit(final_wait_ops=fin[-2:])
    return nc


def make_in_maps(NT, x, mem, positions, g_norm, w_in, g_q_nsa, g_k_cmp, g_k_slc, g_k_win, cmp_pos_k, w_cmp_k1, w_cmp_k2,
                 cmp_pos_v, w_cmp_v1, w_cmp_v2, w_pool, pool_scale, g_mem, w_mem_kv, g_q_mem, g_k_mem, w_out):
    f = lambda a: np.ascontiguousarray(np.asarray(a, dtype=np.float32))
    B = x.shape[0]
    S = 128 * NT
    consts = host_consts()
    rep = np.zeros((128, R_END), np.float32)
    rep[:, R_GQ:R_GQ + 64] = f(g_q_nsa)[0][None, :]
    gk1 = np.concatenate([f(g_k_slc)[0]] * 2 + [f(g_k_win)[0]] * 2 + [f(g_q_mem)[0]] * 4)
    rep[:, R_GK1:R_GK1 + 512] = gk1[None, :]
    rep[:, R_GKC:R_GKC + 128] = np.concatenate([f(g_k_cmp)[0]] * 2)[None, :]
    rep[:, R_GKM:R_GKM + 256] = np.concatenate([f(g_k_mem)[0]] * 4)[None, :]
    rep[:, R_PSC:R_PSC + 256] = f(pool_scale)[0][None, :]
    rep[:, R_INVF:R_INVF + 8] = (500000.0 ** (-np.arange(8, dtype=np.float32) / 8)).astype(np.float32)[None, :]
    rep[:, R_GN:R_GN + 8] = f(g_norm)[0].reshape(8, 128).T
    rep[:, R_GM:R_GM + 8] = f(g_mem)[0].reshape(8, 128).T
    pos = np.asarray(positions).astype(np.int32)
    maps = []
    for b in range(B):
        posl = np.ascontiguousarray(pos[b, :S].reshape(NT, 128).T)
        idx = 16 * (8 * np.arange(NT)[None, :] - 1 + np.arange(8)[:, None]) + 31
        idx = np.clip(idx, 0, S - 1)
        posc = np.ascontiguousarray(pos[b][idx]).astype(np.int32)
        m = {"x": f(x[b, :S]), "mem": f(mem[b]), "posl": posl, "posc": posc, "w_in": f(w_in[0]), "w_out": f(w_out[0]),
             "w_mem_kv": f(w_mem_kv[0]), "w_cmp_k1": f(w_cmp_k1[0]), "w_cmp_v1": f(w_cmp_v1[0]), "w_cmp_k2": f(w_cmp_k2[0]),
             "w_cmp_v2": f(w_cmp_v2[0]), "cmp_pos_k": f(cmp_pos_k[0]), "cmp_pos_v": f(cmp_pos_v[0]), "w_pool": f(w_pool[0]), "rep": rep}
        for n, v in consts.items():
            m["c_" + n] = v
        maps.append(m)
    return maps


def kernel(**inputs):
    x = np.asarray(inputs["x"])
    B, S, D = x.shape
    NT = S // 128
    nc = build(NT)
    maps = make_in_maps(NT, **inputs)
    res = run_bass_kernel_spmd(nc, maps, core_ids=list(range(B)))
    return np.stack([np.asarray(r["out"]) for r in res.results], axis=0).astype(np.float32)
```

```python
import contextlib
import numpy as np
import concourse.bass as bass
import concourse.mybir as mybir
from concourse.bass_utils import run_bass_kernel_spmd

F32, BF16, I32 = mybir.dt.float32, mybir.dt.bfloat16, mybir.dt.int32
ALU = mybir.AluOpType
AF = mybir.ActivationFunctionType
AX = mybir.AxisListType

ENG_ATTR = {"pe": "tensor", "act": "scalar", "dve": "vector", "pool": "gpsimd", "sp": "sync"}
EPS = 1e-6
NEG = -30000.0
GELU_C = 0.7978845608028654


class Prog:
    def __init__(self, nc):
        self.nc = nc
        self.ops = []
        self.lw = {}
        self.rd = {}
        self.epoch = 0
        self.capture = None

    PSUM_KEYS = {"pX", "pA0", "pA1", "pA2", "pO0", "pI", "pM0", "pT"}

    def op(self, eng, fn, reads=(), writes=(), dma_slot=None):
        if self.capture is not None:
            self.capture.append((eng, fn, list(reads), list(writes), dma_slot))
            return None
        i = len(self.ops)
        pk = [k for k in reads if k in self.PSUM_KEYS]
        if pk:
            reads = [k for k in reads if k not in self.PSUM_KEYS]
            writes = list(writes) + pk
        deps = set()
        for k in reads:
            w = self.lw.get(k)
            if w is not None:
                deps.add(w)
        for k in writes:
            w = self.lw.get(k)
            if w is not None:
                deps.add(w)
            for r in self.rd.get(k, ()):
                deps.add(r)
        deps.discard(i)
        for k in reads:
            self.rd.setdefault(k, []).append(i)
        for k in writes:
            self.lw[k] = i
            self.rd[k] = []
        import sys as _s
        self.ops.append(dict(eng=eng, fn=fn, deps=deps, dma=dma_slot, sig=False, line=_s._getframe(1).f_lineno, epoch=self.epoch))
        return i

    def emit(self, final_wait_ops=()):
        import os
        nc = self.nc
        lim = int(os.environ.get('KLIMIT', '0'))
        print('total ops', len(self.ops))
        if lim:
            self.ops = self.ops[:lim]
            final_wait_ops = ()
            print('total ops limited to', lim)
        ops = self.ops
        print('nops', len(ops))
        for o in ops:
            nd = set()
            for d in o["deps"]:
                p = ops[d]
                if p["dma"] is None and o["dma"] is None and p["eng"] == "pe" and o["eng"] == "pe":
                    continue
                nd.add(d)
            own = [d for d in nd if ops[d]["dma"] is None and o["dma"] is None and ops[d]["eng"] == o["eng"]]
            if len(own) > 1:
                keep = max(own)
                nd = set(d for d in nd if d not in own or d == keep)
            o["deps"] = nd
            for d in nd:
                ops[d]["sig"] = True
        for d in final_wait_ops:
            ops[d]["sig"] = True
        for o in ops:
            if o["dma"] is not None:
                o["sig"] = True
        counts = {}
        for i, o in enumerate(ops):
            if not o["sig"]:
                continue
            if o["dma"] is not None:
                key = ("dma", o["dma"])
                inc = 16
            else:
                own = [d for d in o["deps"] if ops[d]["dma"] is None and ops[d]["eng"] == o["eng"]]
                par = 0
                if own:
                    par = 1 - ops[max(own)]["semkey"][2]
                key = ("eng", o["eng"], par, o["epoch"])
                inc = 1
            counts[key] = counts.get(key, 0) + inc
            o["semkey"] = key
            o["ticket"] = counts[key]
            o["inc"] = inc
        keys = list(counts.keys())
        if os.environ.get('KDEBUG'):
            print('sem counts', {str(k): v for k, v in counts.items()})
        with contextlib.ExitStack() as es:
            sems = {}
            for n, k in enumerate(keys):
                sems[k] = es.enter_context(nc.semaphore("s%d" % n))
            block = es.enter_context(nc.Block())
            per_eng = {}
            for i, o in enumerate(ops):
                per_eng.setdefault(o["eng"], []).append(i)
            if final_wait_ops and "pool" not in per_eng:
                per_eng["pool"] = []

            def make(engname, idxs):
                def body(eng):
                    waited = {}
                    for i in idxs:
                        o = ops[i]
                        need = {}
                        for d in o["deps"]:
                            p = ops[d]
                            k = p["semkey"]
                            need[k] = max(need.get(k, 0), p["ticket"])
                        for k, v in need.items():
                            if waited.get(k, 0) >= v:
                                continue
                            eng.wait_ge(sems[k], v)
                            waited[k] = v
                        ins = o["fn"](eng)
                        if o["sig"]:
                            ins.then_inc(sems[o["semkey"]], o["inc"])
                    if engname == "pool":
                        for d in final_wait_ops:
                            p = ops[d]
                            eng.wait_ge(sems[p["semkey"]], p["ticket"])
                return body

            for engname, idxs in per_eng.items():
                getattr(block, ENG_ATTR[engname])(make(engname, idxs))
        return len(keys)


def cmp_to_sel_map(nc_, ns):
    c0 = np.arange(nc_) * 16
    c1 = c0 + 32
    s0 = np.arange(ns) * 64
    s1 = s0 + 64
    ov = np.clip(np.minimum(c1[:, None], s1[None, :]) - np.maximum(c0[:, None], s0[None, :]), 0, None)
    return (ov / 32).astype(np.float32)


def host_consts():
    k = np.arange(128)[:, None]
    q = np.arange(128)[None, :]
    c = {}
    c["ident"] = np.eye(128, dtype=np.float32)
    c["tri_le"] = (k <= q).astype(np.float32)
    c["tri_gt"] = (k > q).astype(np.float32)
    c["d16"] = (q - 16 * k).astype(np.float32)
    sm = np.zeros((512, 128), np.float32)
    sm[:511] = cmp_to_sel_map(511, 128)
    c["selmap"] = np.ascontiguousarray(sm.reshape(4, 128, 128).transpose(1, 0, 2))
    tq = np.arange(128)[:, None]
    jr = np.arange(256)[None, :] - 128
    cur = (tq >= 64).astype(np.int64)
    forced = (jr == cur) | (jr == cur - 1)
    valid = jr <= cur
    c["tmul"] = (valid & ~forced).astype(np.float32)
    c["tadd"] = np.where(forced, 1e4, np.where(valid, 0.0, -1.0)).astype(np.float32)
    band = np.zeros((128, 4, 128), np.float32)
    bandp = np.zeros((128, 4, 128), np.float32)
    band0 = np.zeros((128, 4, 128), np.float32)
    s = np.arange(128)[:, None]
    t = np.arange(128)[None, :]
    for g, w in enumerate((2, 4, 8, 16)):
        band[:, g, :] = ((s <= t) & (s >= t - w + 1)) / float(w) - (s == t)
        bandp[:, g, :] = ((s - 128 >= t - w + 1) & (s - 128 <= t)) / float(w)
        cnt = np.minimum(t + 1, w).astype(np.float32)
        band0[:, g, :] = ((s <= t) & (s >= t - w + 1)) / cnt - (s == t)
    c["band"] = band
    c["bandp"] = bandp
    c["band0"] = band0
    return c


SEGS = [(0, 512, 0),
        (768, 896, 512), (1024, 1152, 640), (2328, 2584, 768),
        (512, 640, 1024), (640, 768, 1152), (896, 1024, 1280), (1152, 1280, 1408),
        (1304, 1816, 1536),
        (2072, 2328, 2048), (2584, 2840, 2304),
        (1816, 2072, 2560), (1280, 1304, 2816)]
R_GQ, R_GK1, R_GKC, R_GKM, R_PSC, R_INVF, R_GN, R_GM, R_END = 0, 64, 576, 704, 960, 1216, 1224, 1232, 1240


def build(NT):
    S = 128 * NT
    nc = bass.Bass("TRN2", target_bir_lowering=False)

    def din(name, shape, dt=F32):
        return nc.dram_tensor(name, shape, dt, kind="ExternalInput").ap()

    x_d = din("x", [S, 1024])
    mem_d = din("mem", [256, 1024])
    posl_d = din("posl", [128, NT], I32)
    posc_d = din("posc", [8, NT], I32)
    win_d = din("w_in", [1024, 2840])
    wout_d = din("w_out", [1024, 1024])
    wmem_d = din("w_mem_kv", [1024, 512])
    w1k_d = din("w_cmp_k1", [32, 64, 64])
    w1v_d = din("w_cmp_v1", [32, 64, 64])
    w2k_d = din("w_cmp_k2", [64, 64])
    w2v_d = din("w_cmp_v2", [64, 64])
    pk_d = din("cmp_pos_k", [32, 64])
    pv_d = din("cmp_pos_v", [32, 64])
    wpool_d = din("w_pool", [4, 64, 64])
    rep_d = din("rep", [128, R_END])
    cst = {n: din("c_" + n, list(v.shape)) for n, v in host_consts().items()}
    out_d = nc.dram_tensor("out", [S, 1024], F32, kind="ExternalOutput").ap()

    with contextlib.ExitStack() as es:
        def sb(name, shape, dt=F32):
            return es.enter_context(nc.sbuf_tensor(name, shape, dt))

        def ps(name, shape, dt=F32):
            return es.enter_context(nc.psum_tensor(name, shape, dt))

        P = Prog(nc)
        rr = [0]

        def anyeng():
            rr[0] += 1
            return ("dve", "pool", "act")[rr[0] % 3]

        Wb = sb("Wb", [128, 8, 2840], BF16)
        Woutb = sb("Woutb", [128, 8, 1024], BF16)
        KE = [sb("KE%d" % g, [128, S], BF16) for g in range(2)]
        VS = sb("VS", [128, NT, 2, 65], BF16)
        KW = [sb("KW%d" % g, [64, 6, 128], BF16) for g in range(2)]
        VW = sb("VW", [128, 6, 2, 65], BF16)
        KCT = [sb("KCT%d" % g, [64, 520], BF16) for g in range(2)]
        VC = sb("VC", [128, 4, 2, 65], BF16)
        HT = [sb("HT%d" % kv, [128, 520], BF16) for kv in range(2)]
        W1bd = [sb("W1bd%d" % kv, [128, 32, 128], BF16) for kv in range(2)]
        W2bd = [sb("W2bd%d" % kv, [128, 128], BF16) for kv in range(2)]
        CVEC = sb("CVEC", [128, 2])
        RAWT = [sb("RAWT%d" % kv, [128, 144], BF16) for kv in range(2)]
        POST = sb("POST", [128, 2, 32], BF16)
        QB2 = [[sb("QB%d_%d" % (par, g), [128, 2, 512], BF16) for g in range(2)] for par in range(2)]
        QMT = sb("QMT", [64, 512], BF16)
        MKT = sb("MKT", [64, 4, 256], BF16)
        MV = sb("MV", [128, 2, 4, 65], BF16)
        XT = [sb("XT%d" % j, [128, 1024]) for j in range(2)]
        Xb = sb("Xb", [128, 1024], BF16)
        XTr = sb("XTr", [128, 8, 128], BF16)
        ST = sb("ST", [128, 16])
        RSTD = sb("RSTD", [128, 2])
        Qf = sb("Qf", [128, 8, 64])
        Qs = sb("Qs", [128, 8, 64])
        Qb16 = sb("Qb16", [128, 8, 64], BF16)
        Kb16 = sb("Kb16", [128, 8, 64], BF16)
        RT = sb("RT", [128, 8, 4, 8])
        CKV = sb("CKV", [128, 256], BF16)
        SZ = [sb("SZ%d" % j, [128, 512], BF16) for j in range(2)]
        EZ = sb("EZ", [128, 512])
        VP = [sb("VP%d" % j, [128, 256], BF16) for j in range(2)]
        GATE = sb("GATE", [128, 24])
        PLT = sb("PLT", [64, 4, 128], BF16)
        Wpl = sb("Wpl", [64, 4, 64], BF16)
        PTb = [sb("PT%d" % j, [128, 512], BF16) for j in range(3)]
        CMs = [sb("CM%d" % j, [128, 128], BF16) for j in range(2)]
        cmc = [0]
        IMP = sb("IMP", [128, 128])
        WK = sb("WK", [128, 128])
        MX = sb("MX", [128, 16])
        BIASg = [sb("BIAS%d" % g, [128, 192], BF16) for g in range(2)]
        DEN = sb("DEN", [128, 8])
        OACC = sb("OACC", [128, 8, 64])
        OTMP = sb("OTMP", [128, 4, 64])
        OTS = [sb("OTS%d" % j, [65, 512], BF16) for j in range(2)]
        otc = [0]
        Y = Xb
        YM = sb("YM", [128, 256], BF16)
        GX = sb("GX", [128, 4, 8])
        K8 = sb("K8", [8, 2, 64])
        K8s = sb("K8s", [8, 2, 64])
        K8b = sb("K8b", [8, 2, 64], BF16)
        K8r = sb("K8r", [8, 2, 4, 8])
        K8st = sb("K8st", [8, 4])
        REP = sb("REP", [128, R_END])
        POSF = sb("POSF", [128, NT])
        POSI = sb("POSI", [128, NT], I32)
        COS = sb("COS", [128, NT, 8])
        SIN = sb("SIN", [128, NT, 8])
        IDf = sb("IDf", [128, 128])
        IDb = sb("IDb", [128, 128], BF16)
        TLE = sb("TLE", [128, 128], BF16)
        TGT = sb("TGT", [128, 128], BF16)
        D16 = sb("D16", [128, 128])
        SELM = sb("SELM", [128, 4, 128], BF16)
        TMUL = sb("TMUL", [128, 256])
        TADD = sb("TADD", [128, 256])
        BAND = sb("BAND", [128, 4, 128], BF16)
        BANDP = sb("BANDP", [128, 4, 128], BF16)
        BAND0 = sb("BAND0", [128, 4, 128], BF16)
        pX = ps("pX", [128, 1024], BF16)
        pA = [ps("pA%d" % j, [128, 512]) for j in range(3)]
        pO = [ps("pO%d" % j, [128, 4, 128]) for j in range(1)]
        pI = ps("pI", [128, 4, 128])
        pM0_ = ps("pM0", [128, 512])
        pM = [pM0_, pM0_]
        pT = ps("pT", [128, 4, 128], BF16)

        fin = []

        def load(dst_ap, src_ap, key, slot, eng="sp"):
            return P.op(eng, lambda e: e.dma_start(out=dst_ap, in_=src_ap), writes=[key], dma_slot=slot)

        load(REP[:], rep_d, "REP", "REP")
        load(IDf[:], cst["ident"], "IDf", "IDf")
        P.op("dve", lambda e: e.tensor_copy(out=IDb[:], in_=IDf[:]), reads=["IDf"], writes=["IDb"])
        load(D16[:], cst["d16"], "D16", "D16")
        load(TMUL[:], cst["tmul"], "TMUL", "TMUL")
        load(TADD[:], cst["tadd"], "TADD", "TADD")
        for (bt, bn) in ((BAND, "band"), (BANDP, "bandp"), (BAND0, "band0")):
            load(XT[1][:, 0:512], cst[bn].rearrange("p g t -> p (g t)"), "XT1", "XT1")
            P.op("dve", lambda e, bt=bt: e.tensor_copy(out=bt[:].rearrange("p g t -> p (g t)"), in_=XT[1][:, 0:512]), reads=["XT1"], writes=[bn.upper()])
        load(XT[0][:, 0:128], cst["tri_le"], "XT0", "XT0")
        P.op("dve", lambda e: e.tensor_scalar(out=TLE[:], in0=XT[0][:, 0:128], scalar1=-1.0, scalar2=-NEG, op0=ALU.add, op1=ALU.mult), reads=["XT0"], writes=["NTLE"])
        load(XT[0][:, 0:128], cst["tri_gt"], "XT0", "XT0")
        P.op("dve", lambda e: e.tensor_scalar(out=TGT[:], in0=XT[0][:, 0:128], scalar1=-1.0, scalar2=-NEG, op0=ALU.add, op1=ALU.mult), reads=["XT0"], writes=["NTGT"])
        NEGM = {"TLE": TLE, "TGT": TGT}
        load(XT[0][:, 0:512], cst["selmap"].rearrange("p c j -> p (c j)"), "XT0", "XT0")
        P.op("dve", lambda e: e.tensor_copy(out=SELM[:].rearrange("p c j -> p (c j)"), in_=XT[0][:, 0:512]), reads=["XT0"], writes=["SELM"])
        load(POSI[:], posl_d, "POSI", "POSI")
        P.op("dve", lambda e: e.tensor_copy(out=POSF[:], in_=POSI[:]), reads=["POSI"], writes=["POSF"])

        def cos_sin(COSt, SINt, ANGt, POSFt, npart, n, kp, KIt, KFt):
            invf = REP[0:npart, R_INVF:R_INVF + 8]
            P.op("dve", lambda e: e.tensor_tensor(out=ANGt, in0=POSFt.unsqueeze(2).to_broadcast([npart, n, 8]),
                                                  in1=invf.unsqueeze(1).to_broadcast([npart, n, 8]), op=ALU.mult),
                 reads=[kp + "POSF", "REP"], writes=["qsrc"])
            for dst, off, nm in ((SINt, 0.5, "SIN"), (COSt, 0.75, "COS")):
                P.op("dve", lambda e, dst=dst, off=off: e.tensor_scalar(out=dst, in0=ANGt, scalar1=float(1.0 / (2 * np.pi)), scalar2=float(off),
                                                                        op0=ALU.mult, op1=ALU.add), reads=["qsrc"], writes=[kp + nm])
                P.op("dve", lambda e, dst=dst: e.tensor_copy(out=KIt, in_=dst), reads=[kp + nm], writes=["qsq"])
                P.op("dve", lambda e: e.tensor_copy(out=KFt, in_=KIt), reads=["qsq"], writes=["EZ"])
                P.op("dve", lambda e, dst=dst: e.tensor_tensor(out=dst, in0=dst, in1=KFt, op=ALU.subtract), reads=[kp + nm, "EZ"], writes=[kp + nm])
                P.op("dve", lambda e, dst=dst: e.tensor_scalar(out=KFt, in0=dst, scalar1=0.0, scalar2=None, op0=ALU.is_lt), reads=[kp + nm], writes=["EZ"])
                P.op("dve", lambda e, dst=dst: e.tensor_tensor(out=dst, in0=dst, in1=KFt, op=ALU.add), reads=[kp + nm, "EZ"], writes=[kp + nm])
                P.op("act", lambda e, dst=dst: e.activation(out=dst, in_=dst, func=AF.Sin, scale=float(2 * np.pi), bias=float(-np.pi)), reads=[kp + nm], writes=[kp + nm])

        assert NT * 8 <= 512
        ANG = Qf[:].rearrange("p h d -> p (h d)")[:, 0:NT * 8].rearrange("p (n j) -> p n j", j=8)
        KI = Qs[:].rearrange("p h d -> p (h d)")[:, 0:NT * 8].bitcast(I32).rearrange("p (n j) -> p n j", j=8)
        KF = EZ[:, 0:NT * 8].rearrange("p (n j) -> p n j", j=8)
        cos_sin(COS[:], SIN[:], ANG, POSF[:], 128, NT, "", KI, KF)
        POSCI = sb("POSCI", [8, NT], I32)
        POSCF = sb("POSCF", [8, NT])
        COSC = sb("COSC", [8, NT, 8])
        SINC = sb("SINC", [8, NT, 8])
        load(POSCI[:], posc_d, "cPOSI", "cPOSI")
        P.op("dve", lambda e: e.tensor_copy(out=POSCF[:], in_=POSCI[:]), reads=["cPOSI"], writes=["cPOSF"])
        cos_sin(COSC[:], SINC[:], ANG[0:8], POSCF[:], 8, NT, "c", KI[0:8], KF[0:8])

        stg = [(XT[0], "XT0"), (XT[1], "XT1")]
        sidx = [0]

        def stage():
            sidx[0] += 1
            return stg[sidx[0] % 2]

        def conv(out_ap, in_ap, scal, rkeys, wkeys):
            eng = anyeng()
            if scal is None:
                if eng == "act":
                    P.op("act", lambda e: e.copy(out=out_ap, in_=in_ap), reads=rkeys, writes=wkeys)
                else:
                    P.op(eng, lambda e: e.tensor_copy(out=out_ap, in_=in_ap), reads=rkeys, writes=wkeys)
            else:
                if eng == "act":
                    P.op("act", lambda e: e.mul(out=out_ap, in_=in_ap, mul=scal), reads=rkeys + ["REP"], writes=wkeys)
                else:
                    P.op(eng, lambda e: e.tensor_scalar(out=out_ap, in0=in_ap, scalar1=scal, scalar2=None, op0=ALU.mult),
                         reads=rkeys + ["REP"], writes=wkeys)

        for kc in range(8):
            for (a, b) in ((0, 1024), (1024, 2048), (2048, 2840)):
                T_, tk = stage()
                load(T_[:, 0:b - a], win_d[kc * 128:(kc + 1) * 128, a:b], tk, tk)
                for (s0, s1, d0) in SEGS:
                    lo, hi = max(a, s0), min(b, s1)
                    if lo < hi:
                        conv(Wb[:, kc, d0 + lo - s0:d0 + hi - s0], T_[:, lo - a:hi - a], REP[:, R_GN + kc:R_GN + kc + 1], [tk], ["Wb"])
        for kc in range(8):
            T_, tk = stage()
            load(T_[:], wout_d[kc * 128:(kc + 1) * 128, :], tk, tk)
            conv(Woutb[:, kc, :], T_[:], None, [tk], ["Woutb"])
        for kv, (w1_d, w2_d, p_d) in enumerate(((w1k_d, w2k_d, pk_d), (w1v_d, w2v_d, pv_d))):
            P.op("pool", lambda e, kv=kv: e.memset(W1bd[kv][:], 0.0), writes=["W1bd%d" % kv])
            P.op("pool", lambda e, kv=kv: e.memset(W2bd[kv][:], 0.0), writes=["W2bd%d" % kv])
            for g in range(2):
                for half in range(2):
                    T_, tk = stage()
                    Tv = T_[g * 64:(g + 1) * 64, 0:1024].rearrange("p (l e) -> p l e", e=64)
                    P.op("sp", lambda e, Tv=Tv, half=half, w1_d=w1_d: e.dma_start(
                        out=Tv, in_=w1_d[half * 16:(half + 1) * 16].rearrange("l d e -> d l e")), writes=[tk], dma_slot=tk)
                    conv(W1bd[kv][g * 64:(g + 1) * 64, half * 16:(half + 1) * 16, g * 64:(g + 1) * 64], Tv, None, [tk], ["W1bd%d" % kv])
                T_, tk = stage()
                load(T_[g * 64:(g + 1) * 64, 0:64], w2_d, tk, tk)
                conv(W2bd[kv][g * 64:(g + 1) * 64, g * 64:(g + 1) * 64], T_[g * 64:(g + 1) * 64, 0:64], None, [tk], ["W2bd%d" % kv])
                T_, tk = stage()
                P.op("sp", lambda e, T_=T_, g=g, p_d=p_d: e.dma_start(out=T_[g * 64:(g + 1) * 64, 0:32], in_=p_d.rearrange("l d -> d l"),
                                                                       allow_slow_non_contiguous=True), writes=[tk], dma_slot=tk)
                conv(POST[g * 64:(g + 1) * 64, kv, :], T_[g * 64:(g + 1) * 64, 0:32], None, [tk], ["POST"])
        T_, tk = stage()
        load(T_[0:64, 0:256].rearrange("p (g e) -> p g e", g=4), wpool_d.rearrange("g c e -> c g e"), tk, tk)
        conv(Wpl[:], T_[0:64, 0:256].rearrange("p (g e) -> p g e", g=4), None, [tk], ["Wpl"])
        for kv in range(2):
            for l in range(32):
                P.op("pe", lambda e, kv=kv, l=l: e.matmul(pM[0][:, kv:kv + 1], lhsT=W1bd[kv][:, l, :], rhs=POST[:, kv, l:l + 1],
                                                          start=(l == 0), stop=(l == 31)),
                     reads=["W1bd%d" % kv, "POST"], writes=["pM0"])
        P.op("dve", lambda e: e.tensor_copy(out=CVEC[:], in_=pM[0][:, 0:2]), reads=["pM0"], writes=["CVEC"])
        for g in range(2):
            P.op("pool", lambda e, g=g: e.memset(KE[g][:], 1.0), writes=["KEinit%d" % g])
            for h0 in range(0, S, 4096):
                wd = min(4096, S - h0)
                v = KE[g][:, h0:h0 + wd]
                P.op("pool", lambda e, v=v, wd=wd: e.affine_select(out=v, in_=v, pattern=[[1, wd]], compare_op=ALU.is_ge, fill=0.0,
                                                                   base=4096, channel_multiplier=-64), writes=["KEinit%d" % g])
                P.op("pool", lambda e, v=v, wd=wd: e.affine_select(out=v, in_=v, pattern=[[-1, wd]], compare_op=ALU.is_ge, fill=0.0,
                                                                   base=-4096 + 63, channel_multiplier=64), writes=["KEinit%d" % g])
        P.op("pool", lambda e: e.memset(VS[:], 1.0), writes=["VSinit"])
        P.op("pool", lambda e: e.memset(VW[:], 1.0), writes=["VWinit"])
        P.op("pool", lambda e: e.memset(VC[:], 1.0), writes=["VCinit"])
        P.op("pool", lambda e: e.memset(MV[:], 1.0), writes=["MVinit"])
        for kv in range(2):
            P.op("pool", lambda e, kv=kv: e.memset(HT[kv][:], 0.0), writes=["HT%d" % kv])
            P.op("pool", lambda e, kv=kv: e.memset(RAWT[kv][:], 0.0), writes=["RAWT%d" % kv])
        for g in range(2):
            P.op("pool", lambda e, g=g: e.memset(KCT[g][:], 0.0), writes=["KCT%d" % g])
            for par in range(2):
                P.op("pool", lambda e, g=g, par=par: e.memset(QB2[par][g][:], 0.0), writes=["QB%d_%d" % (par, g), "QBb%d_%d" % (par, g), "QBc%d_%d" % (par, g)])

        def rstd_from_ss(ss_ap, out_ap, n, rk, wk):
            P.op("act", lambda e: e.activation(out=out_ap, in_=ss_ap, func=AF.Ln, scale=1.0 / n, bias=EPS), reads=rk, writes=wk)
            P.op("act", lambda e: e.activation(out=out_ap, in_=out_ap, func=AF.Exp, scale=-0.5), reads=wk, writes=wk)

        def token_norm_T(src, skey):
            P.op("act", lambda e: e.activation(out=Xb[:], in_=src[:], func=AF.Square, accum_out=ST[:, 0:1]),
                 reads=[skey], writes=["Xb", "ST"])
            rstd_from_ss(ST[:, 0:1], RSTD[:, 0:1], 1024.0, ["ST"], ["RSTD"])
            P.op("dve", lambda e: e.tensor_scalar(out=RSTD[:, 1:2], in0=RSTD[:, 0:1], scalar1=-1.0, scalar2=None, op0=ALU.mult),
                 reads=["RSTD"], writes=["RSTD"])
            P.op("dve", lambda e: e.tensor_copy(out=Xb[:], in_=src[:]), reads=[skey], writes=["Xb"])
            for kc in range(8):
                P.op("pe", lambda e, kc=kc: e.transpose(out=pX[:, kc * 128:(kc + 1) * 128], in_=Xb[:, kc * 128:(kc + 1) * 128], identity=IDb[:]),
                     reads=["Xb", "IDb"], writes=["pX"])
            P.op("act", lambda e: e.copy(out=XTr[:].rearrange("p k t -> p (k t)"), in_=pX[:]), reads=["pX"], writes=["XTr"])

        def proj(W, wkey, c0, c1, pt, pkey):
            for kc in range(8):
                P.op("pe", lambda e, kc=kc: e.matmul(pt[:, 0:c1 - c0], lhsT=XTr[:, kc, :], rhs=W[:, kc, c0:c1], start=(kc == 0), stop=(kc == 7)),
                     reads=["XTr", wkey], writes=[pkey])

        def head_norm_rope(nparts, src3, nh, gain3, cos2, sin2, nrope, kp, tmpS, tmpR, stt, out_b3, src_keys=None, out_key=None):
            src_keys = src_keys or [kp + "src"]
            out_key = out_key or (kp + "b")
            P.op("dve", lambda e: e.tensor_tensor(out=tmpS, in0=src3, in1=src3, op=ALU.mult), reads=src_keys, writes=[kp + "sq"])
            P.op("dve", lambda e: e.tensor_reduce(out=stt, in_=tmpS, axis=AX.X, op=ALU.add), reads=[kp + "sq"], writes=[kp + "st"])
            rstd_from_ss(stt, stt, 64.0, [kp + "st"], [kp + "st"])
            P.op("dve", lambda e: e.tensor_tensor(out=tmpS, in0=src3, in1=stt.unsqueeze(2).to_broadcast([nparts, nh, 64]), op=ALU.mult),
                 reads=src_keys + [kp + "st"], writes=[kp + "sq"])
            P.op("dve", lambda e: e.tensor_tensor(out=tmpS, in0=tmpS, in1=gain3, op=ALU.mult), reads=[kp + "sq", "REP"], writes=[kp + "sq"])
            if nrope:
                x1 = tmpS[:, 0:nrope, 0:8]
                x2 = tmpS[:, 0:nrope, 8:16]
                cb = cos2.unsqueeze(1).to_broadcast([nparts, nrope, 8])
                sbb = sin2.unsqueeze(1).to_broadcast([nparts, nrope, 8])
                t = [tmpR[:, 0:nrope, j, :] for j in range(4)]
                for (o, a_, b_) in ((t[0], x1, cb), (t[1], x2, sbb), (t[2], x1, sbb), (t[3], x2, cb)):
                    P.op("dve", lambda e, o=o, a_=a_, b_=b_: e.tensor_tensor(out=o, in0=a_, in1=b_, op=ALU.mult),
                         reads=[kp + "sq", kp + "cs"], writes=[kp + "rt"])
                P.op("dve", lambda e: e.tensor_tensor(out=x1, in0=t[0], in1=t[1], op=ALU.subtract), reads=[kp + "rt"], writes=[kp + "sq"])
                P.op("dve", lambda e: e.tensor_tensor(out=x2, in0=t[2], in1=t[3], op=ALU.add), reads=[kp + "rt"], writes=[kp + "sq"])
            P.op("dve", lambda e: e.tensor_copy(out=out_b3, in_=tmpS), reads=[kp + "sq"], writes=[out_key])

        def silu_from_psum(pt, pkey, out_ap, okey, Z, zkey):
            P.op("act", lambda e: e.mul(out=Z, in_=pt[:], mul=RSTD[:, 0:1]), reads=[pkey, "RSTD"], writes=[zkey])
            P.op("act", lambda e: e.activation(out=EZ[:], in_=Z, func=AF.Exp, scale=-1.0), reads=[zkey], writes=["EZ"])
            P.op("dve", lambda e: e.tensor_scalar(out=EZ[:], in0=EZ[:], scalar1=1.0, scalar2=None, op0=ALU.add), reads=["EZ"], writes=["EZ"])
            P.op("dve", lambda e: e.reciprocal(out=EZ[:], in_=EZ[:]), reads=["EZ"], writes=["EZ"])
            P.op("dve", lambda e: e.tensor_tensor(out=out_ap, in0=Z, in1=EZ[:], op=ALU.mult), reads=[zkey, "EZ"], writes=[okey])

        ptc = [0]

        def next_pt():
            ptc[0] += 1
            j = ptc[0] % 3
            return PTb[j], "PT%d" % j

        pac = [0]

        pa_free = [None]

        def next_pa():
            if pa_free[0] is not None:
                j = pa_free[0]
                return pA[j], "pA%d" % j
            pac[0] += 1
            j = pac[0] % 3
            return pA[j], "pA%d" % j

        poc = [0]

        def next_po():
            return pO[0], "pO0"

        def mk_desc(lhsT, lkeys, rhs, rkeys, m, mask, v_rhs, vkeys, po, pokey, first, last, imp_kc=None, pre=None, post=None):
            def qk(pa, pak):
                add = mask is not None and mask[1] in ("TLE", "TGT")
                P.op("pe", lambda e: e.matmul(pa[0:m, :], lhsT=lhsT, rhs=rhs, start=True, stop=not add), reads=lkeys + rkeys, writes=[pak])
                if add:
                    nb = NEGM[mask[1]]
                    P.op("pe", lambda e: e.matmul(pa[:, :].rearrange("p (r q) -> p r q", r=4), lhsT=IDb[:], rhs=nb[:].unsqueeze(1).to_broadcast([128, 4, 128]),
                                                  start=False, stop=True), reads=["IDb", "N" + mask[1]], writes=[pak])

            def pv(pt, ptk):
                P.op("pe", lambda e: e.matmul(po[0:65, :, :].rearrange("p r q -> p (r q)"), lhsT=v_rhs(0), rhs=pt[0:m, :], start=first, stop=last),
                     reads=[ptk] + vkeys, writes=[pokey])
                if imp_kc is not None:
                    for r in range(4):
                        P.op("pe", lambda e, r=r: e.matmul(pI[:, r, :], lhsT=pt[0:m, r * 128:(r + 1) * 128], rhs=SELM[0:m, imp_kc, :], start=(first and r == 0), stop=last, skip_group_check=True),
                             reads=[ptk, "SELM"], writes=["pI"])
            return dict(m=m, qk=qk, pv=pv, mask=mask, pre=pre, post=post)

        def run_pipeline(descs, hooks=()):
            n = len(descs)
            hooks = sorted(hooks, key=lambda x: x[0])
            hp = [0]

            def run_hooks(k):
                while hp[0] < len(hooks) and hooks[hp[0]][0] <= k:
                    hooks[hp[0]][1]()
                    hp[0] += 1
            bufs = [None] * n

            def issue(k):
                d = descs[k]
                if d["pre"] is not None:
                    d["pre"]()
                pa, pak = pA[k % 3], "pA%d" % (k % 3)
                pt, ptk = next_pt()
                bufs[k] = (pa, pak, pt, ptk)
                d["qk"](pa, pak)
            issue(0)
            if n > 1:
                issue(1)
            deferred = []
            for k in range(n):
                for (due, fn) in [x for x in deferred if x[0] <= k]:
                    fn()
                deferred = [x for x in deferred if x[0] > k]
                if k + 2 < n:
                    issue(k + 2)
                d = descs[k]
                pa, pak, pt, ptk = bufs[k]
                m = d["m"]
                P.op("act", lambda e, pa=pa, pt=pt, m=m: e.activation(out=pt[0:m, :], in_=pa[0:m, :], func=AF.Exp, scale=0.125), reads=[pak], writes=[ptk])
                if d["mask"] is not None and d["mask"][1] not in ("TLE", "TGT"):
                    mt, mk = d["mask"]
                    P.op("pool", lambda e, pt=pt, m=m, mt=mt: e.tensor_tensor(out=pt[0:m, :].rearrange("p (r q) -> p r q", r=4), in0=pt[0:m, :].rearrange("p (r q) -> p r q", r=4),
                                                                              in1=mt[0:m, :].unsqueeze(1).to_broadcast([m, 4, 128]), op=ALU.mult),
                         reads=[ptk, mk], writes=[ptk])
                d["pv"](pt, ptk)
                if d["post"] is not None:
                    later = d["post"]()
                    if later is not None:
                        deferred.append((k + 2, later))
                pa_free[0] = k % 3
                run_hooks(k)
                pa_free[0] = None
            for (due, fn) in deferred:
                fn()
            run_hooks(10 ** 9)

        def finish_branch(po, pokey, fn):
            otc[0] += 1
            j = otc[0] % 2
            ots, otk = OTS[j], "OTS%d" % j
            src = po[0:65, :, :].rearrange("p r q -> p (r q)")
            P.op("act", lambda e: e.copy(out=ots[:], in_=src), reads=[pokey], writes=[otk])

            def later():
                for r in range(4):
                    P.op("pe", lambda e, r=r: e.transpose(out=pT[:, r, 0:65], in_=ots[:, r * 128:(r + 1) * 128], identity=IDb[0:65, 0:65]),
                         reads=[otk, "IDb"], writes=["pT"])
                fn()
            return later

        def combine(po, pokey, g, b, first):
            den = DEN[:, 0:4]
            P.op("dve", lambda e: e.tensor_scalar(out=den, in0=po[:, :, 64], scalar1=1e-30, scalar2=None, op0=ALU.max), reads=[pokey], writes=["DEN"])
            P.op("dve", lambda e: e.reciprocal(out=den, in_=den), reads=["DEN"], writes=["DEN"])
            if b is not None:
                P.op("dve", lambda e: e.tensor_tensor(out=DEN[:, 4:8], in0=den, in1=GATE[:, 12 * g + b:12 * g + 12:3], op=ALU.mult),
                     reads=["DEN", "GATE"], writes=["DEN2"])
                w = DEN[:, 4:8]
                wk = ["DEN2"]
            else:
                w = den
                wk = ["DEN"]
            dst = OACC[:, 4 * g:4 * g + 4, :]
            wb = w.unsqueeze(2).to_broadcast([128, 4, 64])
            if first:
                P.op("dve", lambda e: e.tensor_tensor(out=dst, in0=po[:, :, 0:64], in1=wb, op=ALU.mult), reads=[pokey] + wk, writes=["OACC%d" % g])
            else:
                P.op("dve", lambda e: e.tensor_tensor(out=OTMP[:], in0=po[:, :, 0:64], in1=wb, op=ALU.mult), reads=[pokey] + wk, writes=["OTMP"])
                P.op("pool", lambda e: e.tensor_tensor(out=dst, in0=dst, in1=OTMP[:], op=ALU.add), reads=["OTMP", "OACC%d" % g], writes=["OACC%d" % g])

        for mt in range(2):
            T_, tk = XT[mt], "XT%d" % mt
            load(T_[:], mem_d[mt * 128:(mt + 1) * 128, :], tk, tk)
            token_norm_T(T_, tk)
            pa, pak = next_pa()
            for kc in range(8):
                sl = kc % 2
                stg_t = (EZ[:], OACC[:].rearrange("p h d -> p (h d)"))[sl]
                stg_k = (["EZ"], ["OACC0", "OACC1"])[sl]
                P.op("sp", lambda e, kc=kc, stg_t=stg_t: e.dma_start(out=stg_t, in_=wmem_d[kc * 128:(kc + 1) * 128, :]), writes=stg_k, dma_slot="wm%d" % sl)
                conv(PTb[sl][:], stg_t, REP[:, R_GM + kc:R_GM + kc + 1], stg_k, ["PT%d" % sl])
                P.op("pe", lambda e, kc=kc, sl=sl, pa=pa: e.matmul(pa[:], lhsT=XTr[:, kc, :], rhs=PTb[sl][:], start=(kc == 0), stop=(kc == 7)),
                     reads=["XTr", "PT%d" % sl], writes=[pak])
            P.op("act", lambda e, pa=pa: e.mul(out=Qf[:, 0:4, :].rearrange("p h d -> p (h d)"), in_=pa[:, 0:256], mul=RSTD[:, 0:1]),
                 reads=[pak, "RSTD"], writes=["qsrc"])
            P.op("dve", lambda e, pa=pa, mt=mt: e.tensor_scalar(out=MV[:, mt, :, 0:64], in0=pa[:, 256:512].rearrange("p (h d) -> p h d", h=4),
                                                                scalar1=RSTD[:, 0:1], scalar2=None, op0=ALU.mult),
                 reads=[pak, "RSTD", "MVinit"], writes=["MV"])
            head_norm_rope(128, Qf[:, 0:4, :], 4, REP[:, R_GKM:R_GKM + 256].rearrange("p (h d) -> p h d", h=4), None, None, 0, "q",
                           Qs[:, 0:4, :], None, ST[:, 8:12], Qb16[:, 0:4, :])
            for h in range(4):
                P.op("pe", lambda e, h=h: e.transpose(out=pX[0:64, h * 128:(h + 1) * 128], in_=Qb16[:, h, :], identity=IDb[:]),
                     reads=["qb", "IDb"], writes=["pX"])
            P.op("act", lambda e, mt=mt: e.copy(out=MKT[:, :, mt * 128:(mt + 1) * 128], in_=pX[0:64, 0:512].rearrange("p (h m) -> p h m", h=4)),
                 reads=["pX"], writes=["MKT"])

        def compress_body(i, kv):
            pm, pmk = pM[kv], "pM0"
            for l in range(32):
                P.op("pe", lambda e, l=l: e.matmul(pm[:, 0:8], lhsT=W1bd[kv][:, l, :], rhs=RAWT[kv][:, l:l + 113:16], start=(l == 0), stop=(l == 31)),
                     reads=["W1bd%d" % kv, "RAWT%d" % kv], writes=[pmk])
            gx = [GX[:, j, :] for j in range(4)]
            P.op("dve", lambda e: e.tensor_scalar(out=gx[0], in0=pm[:, 0:8], scalar1=CVEC[:, kv:kv + 1], scalar2=None, op0=ALU.add),
                 reads=[pmk, "CVEC"], writes=["GX"])
            P.op("dve", lambda e: e.tensor_tensor(out=gx[1], in0=gx[0], in1=gx[0], op=ALU.mult), reads=["GX"], writes=["GX"])
            P.op("dve", lambda e: e.tensor_scalar(out=gx[1], in0=gx[1], scalar1=0.044715, scalar2=1.0, op0=ALU.mult, op1=ALU.add), reads=["GX"], writes=["GX"])
            P.op("dve", lambda e: e.tensor_tensor(out=gx[1], in0=gx[1], in1=gx[0], op=ALU.mult), reads=["GX"], writes=["GX"])
            P.op("act", lambda e: e.activation(out=gx[2], in_=gx[1], func=AF.Exp, scale=-2.0 * GELU_C), reads=["GX"], writes=["GX"])
            P.op("dve", lambda e: e.tensor_scalar(out=gx[2], in0=gx[2], scalar1=1.0, scalar2=None, op0=ALU.add), reads=["GX"], writes=["GX"])
            P.op("dve", lambda e: e.reciprocal(out=gx[2], in_=gx[2]), reads=["GX"], writes=["GX"])
            P.op("dve", lambda e: e.tensor_tensor(out=HT[kv][:, 8 * i:8 * i + 8], in0=gx[2], in1=gx[0], op=ALU.mult), reads=["GX"], writes=["HT%d" % kv])
            P.op("pool", lambda e: e.tensor_copy(out=RAWT[kv][:, 0:16], in_=RAWT[kv][:, 128:144]), reads=["RAWT%d" % kv], writes=["RAWT%d" % kv])

        def cmp_descs(i, g):
            nvis = 8 * i + 7
            nkc = (nvis + 127) // 128
            QBt = QB2[i % 2][g]
            q64 = QBt[0:64, 0, :]
            qk = ["QB%d_%d" % (i % 2, g)]
            po, pok = next_po()
            out = []

            def post():
                return finish_branch(po, pok, post2)

            def post2():
                den = DEN[:, 0:4]
                P.op("dve", lambda e: e.tensor_scalar(out=den, in0=pT[:, :, 64], scalar1=1e-30, scalar2=None, op0=ALU.max), reads=["pT"], writes=["DEN"])
                P.op("dve", lambda e: e.reciprocal(out=den, in_=den), reads=["DEN"], writes=["DEN"])
                P.op("dve", lambda e: e.tensor_scalar(out=IMP[:], in0=pI[:, 0, :], scalar1=DEN[:, 0:1], scalar2=None, op0=ALU.mult), reads=["pI", "DEN"], writes=["IMP"])
                for r in range(1, 4):
                    P.op("dve", lambda e, r=r: e.scalar_tensor_tensor(out=IMP[:], in0=pI[:, r, :], scalar=DEN[:, r:r + 1], in1=IMP[:], op0=ALU.mult, op1=ALU.add),
                         reads=["pI", "DEN", "IMP"], writes=["IMP"])
                P.op("dve", lambda e: e.tensor_tensor(out=IMP[:], in0=IMP[:], in1=TMUL[:, 128 - 2 * i:256 - 2 * i], op=ALU.mult), reads=["IMP", "TMUL"], writes=["IMP"])
                P.op("dve", lambda e: e.tensor_tensor(out=IMP[:], in0=IMP[:], in1=TADD[:, 128 - 2 * i:256 - 2 * i], op=ALU.add), reads=["IMP", "TADD"], writes=["IMP"])
                P.op("dve", lambda e: e.memset(IMP[:, 0:1], 1e4), reads=["IMP"], writes=["IMP"])
                P.op("dve", lambda e: e.max(out=MX[:, 0:8], in_=IMP[:]), reads=["IMP"], writes=["MX"])
                P.op("dve", lambda e: e.match_replace(out=WK[:], in_to_replace=MX[:, 0:8], in_values=IMP[:], imm_value=-3.0e38), reads=["IMP", "MX"], writes=["WK"])
                P.op("dve", lambda e: e.max(out=MX[:, 8:16], in_=WK[:]), reads=["WK"], writes=["MX"])
                B_ = BIASg[g]
                bk_ = "BIAS%d" % g
                P.op("dve", lambda e: e.tensor_scalar(out=B_[:, 0:128], in0=IMP[:], scalar1=MX[:, 15:16], scalar2=NEG, op0=ALU.is_lt, op1=ALU.mult), reads=["IMP", "MX"], writes=[bk_])
                P.op("dve", lambda e: e.tensor_scalar(out=B_[:, 128:192], in0=IMP[:, 0:64], scalar1=MX[:, 15:16], scalar2=NEG, op0=ALU.is_lt, op1=ALU.mult), reads=["IMP", "MX", bk_], writes=[bk_])
                combine(pT, "pT", g, 0, True)

            for kc in range(nkc):
                m = min(128, nvis - 128 * kc)
                full = 16 * (128 * kc + m - 1) + 31 <= 128 * i
                mask = None
                pre = None
                if not full:
                    delta = float(128 * (16 * kc - i) + 31)
                    cmc[0] += 1
                    CM = CMs[cmc[0] % 2]
                    cmk = "CM%d" % (cmc[0] % 2)
                    mask = (CM, cmk)

                    def pre(delta=delta, CM=CM, cmk=cmk):
                        P.op("pool", lambda e: e.tensor_scalar(out=CM[:], in0=D16[:], scalar1=delta, scalar2=None, op0=ALU.is_ge), reads=["D16"], writes=[cmk])
                out.append(mk_desc(KCT[g][:, 1 + kc * 128:1 + kc * 128 + m], ["KCT%d" % g], q64, qk, m, mask,
                                   lambda r, kc=kc, m=m: VC[0:m, kc, g, :], ["VC"], po, pok, kc == 0, kc == nkc - 1, imp_kc=kc, pre=pre,
                                   post=(post if kc == nkc - 1 else None)))
            return out

        def win_descs(i, g):
            QBt = QB2[i % 2][g]
            q64 = QBt[0:64, 0, :]
            qk = ["QB%d_%d" % (i % 2, g)]
            po, pok = next_po()
            kts = [kt for kt in range(i - 4, i + 1) if kt >= 0]
            out = []
            for n_, kt in enumerate(kts):
                mask = (TLE, "TLE") if kt == i else ((TGT, "TGT") if kt == i - 4 else None)
                last = n_ == len(kts) - 1
                out.append(mk_desc(KW[g][:, kt % 6, :], [("KW", g, kt % 6)], q64, qk, 128, mask,
                                   lambda r, kt=kt: VW[:, kt % 6, g, :], [("VW", kt % 6)], po, pok, n_ == 0, last,
                                   post=((lambda: finish_branch(po, pok, lambda: combine(pT, "pT", g, 2, False))) if last else None)))
            return out

        def sel_descs(i, g):
            QBt = QB2[i % 2][g]
            kq, kb, kc1 = "QB%d_%d" % (i % 2, g), "QBb%d_%d" % (i % 2, g), "QBc%d_%d" % (i % 2, g)
            qk = [kq, kb]
            po, pok = next_po()
            B_ = BIASg[g]
            bk_ = "BIAS%d" % g

            def pre():
                P.op("pe", lambda e: e.transpose(out=pT[:, 0, :], in_=B_[:, 64:192], identity=IDb[:]), reads=[bk_, "IDb"], writes=["pT"])
                if NT > 32:
                    P.op("pe", lambda e: e.transpose(out=pT[:, 1, :], in_=B_[:, 0:128], identity=IDb[:]), reads=[bk_, "IDb"], writes=["pT"])
                for r in range(4):
                    if r % 2 == 0:
                        P.op("dve", lambda e, r=r: e.tensor_copy(out=QBt[64:128, 0, r * 128:(r + 1) * 128], in_=pT[64:128, 0, :]), reads=["pT"], writes=[kb])
                    else:
                        P.op("act", lambda e, r=r: e.copy(out=QBt[64:128, 0, r * 128:(r + 1) * 128], in_=pT[64:128, 0, :]), reads=["pT"], writes=[kb])
                    if NT > 32:
                        if r % 2 == 1:
                            P.op("dve", lambda e, r=r: e.tensor_copy(out=QBt[64:128, 1, r * 128:(r + 1) * 128], in_=pT[64:128, 1, :]), reads=["pT"], writes=[kb])
                        else:
                            P.op("act", lambda e, r=r: e.copy(out=QBt[64:128, 1, r * 128:(r + 1) * 128], in_=pT[64:128, 1, :]), reads=["pT"], writes=[kb])
            out = []
            for kt in range(i + 1):
                cv = kt // 32
                mask = (TLE, "TLE") if kt == i else None
                out.append(mk_desc(KE[g][:, kt * 128:(kt + 1) * 128], [("KE", g, kt), "KEinit%d" % g], QBt[:, cv, :],
                                   qk + ([kc1] if cv == 1 else []), 128, mask,
                                   lambda r, kt=kt: VS[:, kt, g, :], [("VS", kt)], po, pok, kt == 0, kt == i,
                                   pre=(pre if kt == 0 else None), post=((lambda: finish_branch(po, pok, lambda: combine(pT, "pT", g, 1, False))) if kt == i else None)))
            return out

        def mem_descs(i):
            po, pok = next_po()
            out = []
            for mt in range(2):
                def qk(pa, pak, mt=mt):
                    for h in range(4):
                        P.op("pe", lambda e, h=h: e.matmul(pa[:, h * 128:(h + 1) * 128], lhsT=MKT[:, h, mt * 128:(mt + 1) * 128],
                                                           rhs=QMT[:, h * 128:(h + 1) * 128], start=True, stop=True), reads=["MKT", "QMT"], writes=[pak])

                def pv(pt, ptk, mt=mt):
                    for h in range(4):
                        P.op("pe", lambda e, h=h: e.matmul(po[0:65, h, :], lhsT=MV[:, mt, h, :], rhs=pt[:, h * 128:(h + 1) * 128],
                                                           start=(mt == 0 and h == 0), stop=(mt == 1), skip_group_check=True), reads=[ptk, "MV"], writes=[pok])

                def post():
                    return finish_branch(po, pok, post2)

                def post2():
                    den = DEN[:, 0:4]
                    P.op("dve", lambda e: e.tensor_scalar(out=den, in0=pT[:, :, 64], scalar1=1e-30, scalar2=None, op0=ALU.max), reads=["pT"], writes=["DEN"])
                    P.op("dve", lambda e: e.reciprocal(out=den, in_=den), reads=["DEN"], writes=["DEN"])
                    P.op("dve", lambda e: e.tensor_tensor(out=OTMP[:], in0=pT[:, :, 0:64], in1=den.unsqueeze(2).to_broadcast([128, 4, 64]), op=ALU.mult),
                         reads=["pT", "DEN"], writes=["OTMP"])
                    P.op("dve", lambda e: e.tensor_tensor(out=YM[:], in0=OTMP[:].rearrange("p h d -> p (h d)"), in1=SZ[1][:, 256:512], op=ALU.mult),
                         reads=["OTMP", "SZ1"], writes=["YM"])
                out.append(dict(m=128, qk=qk, pv=pv, mask=None, pre=None, post=(post if mt == 1 else None)))
            return out

        def front_a_pieces(i):
            X_, xk = XT[i % 2], "XT%d" % (i % 2)
            cosi, sini = COS[:, i, :], SIN[:, i, :]
            par = i % 2
            ws = i % 6
            st = {}

            def p_norm():
                P.op("act", lambda e: e.activation(out=Xb[:], in_=X_[:], func=AF.Square, accum_out=ST[:, 0:1]), reads=[xk], writes=["Xb", "ST"])
                rstd_from_ss(ST[:, 0:1], RSTD[:, 0:1], 1024.0, ["ST"], ["RSTD"])
                P.op("dve", lambda e: e.tensor_scalar(out=RSTD[:, 1:2], in0=RSTD[:, 0:1], scalar1=-1.0, scalar2=None, op0=ALU.mult), reads=["RSTD"], writes=["RSTD"])
                P.op("dve", lambda e: e.tensor_copy(out=Xb[:], in_=X_[:]), reads=[xk], writes=["Xb"])

            def p_xT():
                for kc in range(8):
                    P.op("pe", lambda e, kc=kc: e.transpose(out=pX[:, kc * 128:(kc + 1) * 128], in_=Xb[:, kc * 128:(kc + 1) * 128], identity=IDb[:]),
                         reads=["Xb", "IDb"], writes=["pX"])
                P.op("dve", lambda e: e.tensor_copy(out=XTr[:].rearrange("p k t -> p (k t)"), in_=pX[:]), reads=["pX"], writes=["XTr"])

            def p_g0():
                pa, pak = pM[0], "pM0"
                proj(Wb, "Wb", 0, 512, pa, pak)
                P.op("dve", lambda e: e.tensor_scalar(out=Qf[:].rearrange("p h d -> p (h d)"), in0=pa[:], scalar1=RSTD[:, 0:1], scalar2=None, op0=ALU.mult), reads=[pak, "RSTD", "COS", "SIN"], writes=["qsrc", "qcs"])

            def p_qchain():
                head_norm_rope(128, Qf[:], 8, REP[:, R_GQ:R_GQ + 64].unsqueeze(1).to_broadcast([128, 8, 64]), cosi, sini, 8, "q", Qs[:], RT[:], ST[:, 8:16], Qb16[:])

            def p_qT():
                for h in range(8):
                    P.op("pe", lambda e, h=h: e.transpose(out=pX[0:64, h * 128:(h + 1) * 128], in_=Qb16[:, h, :], identity=IDb[:]), reads=["qb", "IDb"], writes=["pX"])
                for g in range(2):
                    P.op("act", lambda e, g=g: e.copy(out=QB2[par][g][0:64, 0, :], in_=pX[0:64, g * 512:(g + 1) * 512]), reads=["pX"], writes=["QB%d_%d" % (par, g)])
                    if NT > 32:
                        P.op("dve", lambda e, g=g: e.tensor_copy(out=QB2[par][g][0:64, 1, :], in_=pX[0:64, g * 512:(g + 1) * 512]), reads=["pX"], writes=["QBc%d_%d" % (par, g)])

            def p_g1():
                pa1, pak1 = pM[0], "pM0"
                proj(Wb, "Wb", 512, 1024, pa1, pak1)
                P.op("dve", lambda e: e.tensor_scalar(out=EZ[:], in0=pa1[:], scalar1=RSTD[:, 0:1], scalar2=None, op0=ALU.mult), reads=[pak1, "RSTD"], writes=["EZ"])

            def p_kchain():
                head_norm_rope(128, EZ[:].rearrange("p (h d) -> p h d", h=8), 8, REP[:, R_GK1:R_GK1 + 512].rearrange("p (h d) -> p h d", h=8), cosi, sini, 4, "q",
                               Qs[:], RT[:], ST[:, 8:16], Kb16[:], src_keys=["EZ"], out_key="kkb")

            def p_kT():
                for h in range(8):
                    P.op("pe", lambda e, h=h: e.transpose(out=pX[0:64, h * 128:(h + 1) * 128], in_=Kb16[:, h, :], identity=IDb[:]), reads=["kkb", "IDb"], writes=["pX"])
                for g in range(2):
                    P.op("act", lambda e, g=g: e.copy(out=KE[g][0:64, i * 128:(i + 1) * 128], in_=pX[0:64, g * 128:(g + 1) * 128]),
                         reads=["pX", "KEinit%d" % g], writes=[("KE", g, i)])
                    P.op("dve", lambda e, g=g: e.tensor_copy(out=KW[g][:, ws, :], in_=pX[0:64, (2 + g) * 128:(3 + g) * 128]), reads=["pX"], writes=[("KW", g, ws)])
                P.op("dve", lambda e: e.tensor_copy(out=QMT[:], in_=pX[0:64, 512:1024]), reads=["pX"], writes=["QMT"])

            def p_g2():
                pa2, pak2 = pM[0], "pM0"
                proj(Wb, "Wb", 1024, 1536, pa2, pak2)
                P.op("act", lambda e: e.mul(out=CKV[:], in_=pa2[:, 0:256], mul=RSTD[:, 0:1]), reads=[pak2, "RSTD"], writes=["CKV"])
                P.op("dve", lambda e: e.tensor_scalar(out=VS[:, i, :, 0:64], in0=pa2[:, 256:384].rearrange("p (g d) -> p g d", g=2), scalar1=RSTD[:, 0:1],
                                                      scalar2=None, op0=ALU.mult), reads=[pak2, "RSTD", "VSinit"], writes=[("VS", i)])
                P.op("dve", lambda e: e.tensor_scalar(out=VW[:, ws, :, 0:64], in0=pa2[:, 384:512].rearrange("p (g d) -> p g d", g=2), scalar1=RSTD[:, 0:1],
                                                      scalar2=None, op0=ALU.mult), reads=[pak2, "RSTD", "VWinit"], writes=[("VW", ws)])

            def p_ckvT():
                for kv in range(2):
                    P.op("pe", lambda e, kv=kv: e.transpose(out=pX[:, kv * 128:(kv + 1) * 128], in_=CKV[:, kv * 128:(kv + 1) * 128], identity=IDb[:]),
                         reads=["CKV", "IDb"], writes=["pX"])
                for kv in range(2):
                    P.op("dve", lambda e, kv=kv: e.tensor_copy(out=RAWT[kv][:, 16:144], in_=pX[:, kv * 128:(kv + 1) * 128]), reads=["pX"], writes=["RAWT%d" % kv])

            def p_k8():
                P.op("pe", lambda e: e.matmul(pM[0][0:8, 0:128], lhsT=HT[0][:, 8 * i:8 * i + 8], rhs=W2bd[0][:], start=True, stop=True),
                     reads=["HT0", "W2bd0"], writes=["pM0"])
                P.op("act", lambda e: e.copy(out=K8[:].rearrange("p g d -> p (g d)"), in_=pM[0][0:8, 0:128]), reads=["pM0", "cCOS", "cSIN"], writes=["ksrc", "kcs"])
                head_norm_rope(8, K8[:], 2, REP[0:8, R_GKC:R_GKC + 128].rearrange("p (g d) -> p g d", g=2), COSC[:, i, :], SINC[:, i, :], 2, "k",
                               K8s[:], K8r[:], K8st[:, 0:2], K8b[:])

            def p_k8T():
                for g in range(2):
                    P.op("pe", lambda e, g=g: e.transpose(out=pX[0:64, g * 8:(g + 1) * 8], in_=K8b[:, g, :], identity=IDb[0:8, 0:8]), reads=["kb", "IDb"], writes=["pX"])
                for g in range(2):
                    P.op("dve", lambda e, g=g: e.tensor_copy(out=KCT[g][:, 8 * i:8 * i + 8], in_=pX[0:64, g * 8:(g + 1) * 8]), reads=["pX"], writes=["KCT%d" % g])

            def p_vc():
                clo, chi = max(8 * i - 1, 0) // 128, (8 * i + 6) // 128
                for c in range(clo, chi + 1):
                    P.op("pe", lambda e, c=c: e.matmul(pM[1][:, 0:128], lhsT=HT[1][:, 1 + c * 128:1 + (c + 1) * 128], rhs=W2bd[1][:], start=True, stop=True),
                         reads=["HT1", "W2bd1"], writes=["pM0"])
                    P.op("act", lambda e, c=c: e.copy(out=VC[:, c, :, 0:64], in_=pM[1][:, 0:128].rearrange("p (g d) -> p g d", g=2)),
                         reads=["pM0", "VCinit"], writes=["VC"])

            return [(0, p_norm), (6, p_xT), (9, p_g0), (10, p_qchain), (11, p_g1), (12, p_kchain), (13, p_g2), (17, p_ckvT),
                    (20, lambda: compress_body(i, 0)), (24, p_qT), (26, lambda: compress_body(i, 1)), (30, p_kT),
                    (34, p_k8), (36, p_vc), (46, p_k8T)]

        def front_b(i):
            pa3, pak3 = next_pa()
            proj(Wb, "Wb", 1536, 2048, pa3, pak3)
            silu_from_psum(pa3, pak3, SZ[0][:], "SZ0", Qf[:].rearrange("p h d -> p (h d)"), "qsrc")
            pa4, pak4 = next_pa()
            proj(Wb, "Wb", 2048, 2560, pa4, pak4)
            silu_from_psum(pa4, pak4, SZ[1][:], "SZ1", Qs[:].rearrange("p h d -> p (h d)"), "qsq")
            pa5, pak5 = next_pa()
            proj(Wb, "Wb", 2560, 2840, pa5, pak5)
            vpc = VP[i % 2]
            P.op("act", lambda e: e.mul(out=vpc[:], in_=pa5[:, 0:256], mul=RSTD[:, 0:1]), reads=[pak5, "RSTD"], writes=["VP%d" % (i % 2)])
            P.op("act", lambda e: e.activation(out=GATE[:], in_=pa5[:, 256:280], func=AF.Exp, scale=RSTD[:, 1:2]), reads=[pak5, "RSTD"], writes=["GATE"])
            P.op("dve", lambda e: e.tensor_scalar(out=GATE[:], in0=GATE[:], scalar1=1.0, scalar2=None, op0=ALU.add), reads=["GATE"], writes=["GATE"])
            P.op("dve", lambda e: e.reciprocal(out=GATE[:], in_=GATE[:]), reads=["GATE"], writes=["GATE"])

        def attention(i, next_pieces):
            descs = []
            for g in range(2):
                descs += cmp_descs(i, g)
                descs += win_descs(i, g)
            descs += mem_descs(i)
            mem_end = len(descs) + 1
            for g in range(2):
                descs += sel_descs(i, g)
            hooks = []
            cur = mem_end
            for (slot, fn) in next_pieces:
                P.capture = []
                fn()
                ops_, P.capture = P.capture, None
                stages = []
                for o in ops_:
                    if stages and stages[-1][-1][0] == o[0]:
                        stages[-1].append(o)
                    else:
                        stages.append([o])
                for stg_ in stages:
                    def replay(stg_=stg_):
                        for (eng, f2, r2, w2, d2) in stg_:
                            P.op(eng, f2, reads=r2, writes=w2, dma_slot=d2)
                    hooks.append((cur, replay))
                    eng0, n_ = stg_[0][0], len(stg_)
                    cur += 1 + (min(2, n_ // 4) if eng0 == "pe" else n_ // 2)
            run_pipeline(descs, hooks)

        def epilogue(i):
            X_, xk = XT[i % 2], "XT%d" % (i % 2)
            vpc, vpp = VP[i % 2], VP[(i + 1) % 2]
            P.op("dve", lambda e: e.tensor_tensor(out=Y[:, 0:512], in0=OACC[:].rearrange("p h d -> p (h d)"), in1=SZ[0][:], op=ALU.mult),
                 reads=["OACC0", "OACC1", "SZ0"], writes=["Xb"])
            bnd = BAND0 if i == 0 else BAND
            bk = "BAND0" if i == 0 else "BAND"
            for g in range(4):
                P.op("pe", lambda e, g=g: e.matmul(pM[0][0:64, g * 128:(g + 1) * 128], lhsT=vpc[:, g * 64:(g + 1) * 64], rhs=bnd[:, g, :],
                                                   start=True, stop=(i == 0)), reads=["VP%d" % (i % 2), bk], writes=["pM0"])
                if i > 0:
                    P.op("pe", lambda e, g=g: e.matmul(pM[0][0:64, g * 128:(g + 1) * 128], lhsT=vpp[:, g * 64:(g + 1) * 64], rhs=BANDP[:, g, :],
                                                       start=False, stop=True), reads=["VP%d" % ((i + 1) % 2), "BANDP"], writes=["pM0"])
            P.op("act", lambda e: e.copy(out=PLT[:].rearrange("p g t -> p (g t)"), in_=pM[0][0:64, :]), reads=["pM0"], writes=["PLT"])
            for g in range(4):
                P.op("pe", lambda e, g=g: e.matmul(pM[1][:, g * 64:(g + 1) * 64], lhsT=PLT[:, g, :], rhs=Wpl[:, g, :], start=True, stop=True),
                     reads=["PLT", "Wpl"], writes=["pM0"])
            P.op("dve", lambda e: e.tensor_tensor(out=EZ[:, 0:256], in0=pM[1][:, 0:256], in1=REP[:, R_PSC:R_PSC + 256], op=ALU.mult), reads=["pM0", "REP"], writes=["EZ"])
            P.op("dve", lambda e: e.tensor_tensor(out=Y[:, 512:768], in0=EZ[:, 0:256], in1=SZ[1][:, 0:256], op=ALU.mult), reads=["EZ", "SZ1"], writes=["Xb"])

        def epilogue_b(i):
            X_, xk = XT[i % 2], "XT%d" % (i % 2)
            for kc in range(8):
                src = Y[:, kc * 128:(kc + 1) * 128] if kc < 6 else YM[:, (kc - 6) * 128:(kc - 5) * 128]
                P.op("pe", lambda e, kc=kc, src=src: e.transpose(out=pX[:, kc * 128:(kc + 1) * 128], in_=src, identity=IDb[:]),
                     reads=["Xb", "YM", "IDb"], writes=["pX"])
            for hh in range(2):
                P.op("act" if hh == 0 else "dve", (lambda e, hh=hh: e.copy(out=PTb[hh][:], in_=pX[:, hh * 512:(hh + 1) * 512])) if hh == 0 else
                     (lambda e, hh=hh: e.tensor_copy(out=PTb[hh][:], in_=pX[:, hh * 512:(hh + 1) * 512])), reads=["pX"], writes=["PT%d" % hh])
            for hf in range(2):
                pao, pako = next_pa()
                for kc in range(8):
                    P.op("pe", lambda e, kc=kc, pao=pao, hf=hf: e.matmul(pao[:], lhsT=PTb[kc // 4][:, (kc % 4) * 128:(kc % 4 + 1) * 128],
                                                                         rhs=Woutb[:, kc, hf * 512:(hf + 1) * 512], start=(kc == 0), stop=(kc == 7)),
                         reads=["PT%d" % (kc // 4), "Woutb"], writes=[pako])
                P.op("dve", lambda e, pao=pao, hf=hf: e.tensor_tensor(out=X_[:, hf * 512:(hf + 1) * 512], in0=pao[:], in1=X_[:, hf * 512:(hf + 1) * 512], op=ALU.add),
                     reads=[pako, xk], writes=[xk])
            fin.append(P.op("pool", lambda e: e.dma_start(out=out_d[i * 128:(i + 1) * 128, :], in_=X_[:]), reads=[xk], dma_slot=xk + "st"))

        load(XT[0][:], x_d[0:128, :], "XT0", "XT0")
        P.epoch = 1
        for (slot, fn) in front_a_pieces(0):
            fn()
        for i in range(NT):
            P.epoch = 1 + i // 8
            if i + 1 < NT:
                load(XT[(i + 1) % 2][:], x_d[(i + 1) * 128:(i + 2) * 128, :], "XT%d" % ((i + 1) % 2), "XT%d" % ((i + 1) % 2))
            if i == 0:
                front_b(0)
            attention(i, front_a_pieces(i + 1) if i + 1 < NT else [])
            epilogue(i)
            if i + 1 < NT:
                front_b(i + 1)
            epilogue_b(i)
        P.emit(final_wait_ops=fin[-2:])
    return nc


def make_in_maps(NT, x, mem, positions, g_norm, w_in, g_q_nsa, g_k_cmp, g_k_slc, g_k_win, cmp_pos_k, w_cmp_k1, w_cmp_k2,
                 cmp_pos_v, w_cmp_v1, w_cmp_v2, w_pool, pool_scale, g_mem, w_mem_kv, g_q_mem, g_k_mem, w_out):
    f = lambda a: np.ascontiguousarray(np.asarray(a, dtype=np.float32))
    B = x.shape[0]
    S = 128 * NT
    consts = host_consts()
    rep = np.zeros((128, R_END), np.float32)
    rep[:, R_GQ:R_GQ + 64] = f(g_q_nsa)[0][None, :]
    gk1 = np.concatenate([f(g_k_slc)[0]] * 2 + [f(g_k_win)[0]] * 2 + [f(g_q_mem)[0]] * 4)
    rep[:, R_GK1:R_GK1 + 512] = gk1[None, :]
    rep[:, R_GKC:R_GKC + 128] = np.concatenate([f(g_k_cmp)[0]] * 2)[None, :]
    rep[:, R_GKM:R_GKM + 256] = np.concatenate([f(g_k_mem)[0]] * 4)[None, :]
    rep[:, R_PSC:R_PSC + 256] = f(pool_scale)[0][None, :]
    rep[:, R_INVF:R_INVF + 8] = (500000.0 ** (-np.arange(8, dtype=np.float32) / 8)).astype(np.float32)[None, :]
    rep[:, R_GN:R_GN + 8] = f(g_norm)[0].reshape(8, 128).T
    rep[:, R_GM:R_GM + 8] = f(g_mem)[0].reshape(8, 128).T
    pos = np.asarray(positions).astype(np.int32)
    maps = []
    for b in range(B):
        posl = np.ascontiguousarray(pos[b, :S].reshape(NT, 128).T)
        idx = 16 * (8 * np.arange(NT)[None, :] - 1 + np.arange(8)[:, None]) + 31
        idx = np.clip(idx, 0, S - 1)
        posc = np.ascontiguousarray(pos[b][idx]).astype(np.int32)
        m = {"x": f(x[b, :S]), "mem": f(mem[b]), "posl": posl, "posc": posc, "w_in": f(w_in[0]), "w_out": f(w_out[0]),
             "w_mem_kv": f(w_mem_kv[0]), "w_cmp_k1": f(w_cmp_k1[0]), "w_cmp_v1": f(w_cmp_v1[0]), "w_cmp_k2": f(w_cmp_k2[0]),
             "w_cmp_v2": f(w_cmp_v2[0]), "cmp_pos_k": f(cmp_pos_k[0]), "cmp_pos_v": f(cmp_pos_v[0]), "w_pool": f(w_pool[0]), "rep": rep}
        for n, v in consts.items():
            m["c_" + n] = v
        maps.append(m)
    return maps


def kernel(**inputs):
    x = np.asarray(inputs["x"])
    B, S, D = x.shape
    NT = S // 128
    nc = build(NT)
    maps = make_in_maps(NT, **inputs)
    res = run_bass_kernel_spmd(nc, maps, core_ids=list(range(B)))
    return np.stack([np.asarray(r["out"]) for r in res.results], axis=0).astype(np.float32)
```

```python
import contextlib
import numpy as np
import concourse.bass as bass
import concourse.mybir as mybir
from concourse.bass_utils import run_bass_kernel_spmd

F32, BF16, I32 = mybir.dt.float32, mybir.dt.bfloat16, mybir.dt.int32
ALU = mybir.AluOpType
AF = mybir.ActivationFunctionType
AX = mybir.AxisListType

ENG_ATTR = {"pe": "tensor", "act": "scalar", "dve": "vector", "pool": "gpsimd", "sp": "sync"}
EPS = 1e-6
NEG = -30000.0
GELU_C = 0.7978845608028654


class Prog:
    def __init__(self, nc):
        self.nc = nc
        self.ops = []
        self.lw = {}
        self.rd = {}
        self.epoch = 0
        self.capture = None

    PSUM_KEYS = {"pX", "pA0", "pA1", "pA2", "pO0", "pI", "pM0", "pT"}

    def op(self, eng, fn, reads=(), writes=(), dma_slot=None):
        if self.capture is not None:
            self.capture.append((eng, fn, list(reads), list(writes), dma_slot))
            return None
        i = len(self.ops)
        pk = [k for k in reads if k in self.PSUM_KEYS]
        if pk:
            reads = [k for k in reads if k not in self.PSUM_KEYS]
            writes = list(writes) + pk
        deps = set()
        for k in reads:
            w = self.lw.get(k)
            if w is not None:
                deps.add(w)
        for k in writes:
            w = self.lw.get(k)
            if w is not None:
                deps.add(w)
            for r in self.rd.get(k, ()):
                deps.add(r)
        deps.discard(i)
        for k in reads:
            self.rd.setdefault(k, []).append(i)
        for k in writes:
            self.lw[k] = i
            self.rd[k] = []
        import sys as _s
        self.ops.append(dict(eng=eng, fn=fn, deps=deps, dma=dma_slot, sig=False, line=_s._getframe(1).f_lineno, epoch=self.epoch))
        return i

    def emit(self, final_wait_ops=()):
        import os
        nc = self.nc
        lim = int(os.environ.get('KLIMIT', '0'))
        print('total ops', len(self.ops))
        if lim:
            self.ops = self.ops[:lim]
            final_wait_ops = ()
            print('total ops limited to', lim)
        ops = self.ops
        print('nops', len(ops))
        for o in ops:
            nd = set()
            for d in o["deps"]:
                p = ops[d]
                if p["dma"] is None and o["dma"] is None and p["eng"] == "pe" and o["eng"] == "pe":
                    continue
                nd.add(d)
            own = [d for d in nd if ops[d]["dma"] is None and o["dma"] is None and ops[d]["eng"] == o["eng"]]
            if len(own) > 1:
                keep = max(own)
                nd = set(d for d in nd if d not in own or d == keep)
            o["deps"] = nd
            for d in nd:
                ops[d]["sig"] = True
        for d in final_wait_ops:
            ops[d]["sig"] = True
        for o in ops:
            if o["dma"] is not None:
                o["sig"] = True
        counts = {}
        for i, o in enumerate(ops):
            if not o["sig"]:
                continue
            if o["dma"] is not None:
                key = ("dma", o["dma"])
                inc = 16
            else:
                own = [d for d in o["deps"] if ops[d]["dma"] is None and ops[d]["eng"] == o["eng"]]
                par = 0
                if own:
                    par = 1 - ops[max(own)]["semkey"][2]
                key = ("eng", o["eng"], par, o["epoch"])
                inc = 1
            counts[key] = counts.get(key, 0) + inc
            o["semkey"] = key
            o["ticket"] = counts[key]
            o["inc"] = inc
        keys = list(counts.keys())
        if os.environ.get('KDEBUG'):
            print('sem counts', {str(k): v for k, v in counts.items()})
        with contextlib.ExitStack() as es:
            sems = {}
            for n, k in enumerate(keys):
                sems[k] = es.enter_context(nc.semaphore("s%d" % n))
            block = es.enter_context(nc.Block())
            per_eng = {}
            for i, o in enumerate(ops):
                per_eng.setdefault(o["eng"], []).append(i)
            if final_wait_ops and "pool" not in per_eng:
                per_eng["pool"] = []

            def make(engname, idxs):
                def body(eng):
                    waited = {}
                    for i in idxs:
                        o = ops[i]
                        need = {}
                        for d in o["deps"]:
                            p = ops[d]
                            k = p["semkey"]
                            need[k] = max(need.get(k, 0), p["ticket"])
                        for k, v in need.items():
                            if waited.get(k, 0) >= v:
                                continue
                            eng.wait_ge(sems[k], v)
                            waited[k] = v
                        ins = o["fn"](eng)
                        if o["sig"]:
                            ins.then_inc(sems[o["semkey"]], o["inc"])
                    if engname == "pool":
                        for d in final_wait_ops:
                            p = ops[d]
                            eng.wait_ge(sems[p["semkey"]], p["ticket"])
                return body

            for engname, idxs in per_eng.items():
                getattr(block, ENG_ATTR[engname])(make(engname, idxs))
        return len(keys)


def cmp_to_sel_map(nc_, ns):
    c0 = np.arange(nc_) * 16
    c1 = c0 + 32
    s0 = np.arange(ns) * 64
    s1 = s0 + 64
    ov = np.clip(np.minimum(c1[:, None], s1[None, :]) - np.maximum(c0[:, None], s0[None, :]), 0, None)
    return (ov / 32).astype(np.float32)


def host_consts():
    k = np.arange(128)[:, None]
    q = np.arange(128)[None, :]
    c = {}
    c["ident"] = np.eye(128, dtype=np.float32)
    c["tri_le"] = (k <= q).astype(np.float32)
    c["tri_gt"] = (k > q).astype(np.float32)
    c["d16"] = (q - 16 * k).astype(np.float32)
    sm = np.zeros((512, 128), np.float32)
    sm[:511] = cmp_to_sel_map(511, 128)
    c["selmap"] = np.ascontiguousarray(sm.reshape(4, 128, 128).transpose(1, 0, 2))
    tq = np.arange(128)[:, None]
    jr = np.arange(256)[None, :] - 128
    cur = (tq >= 64).astype(np.int64)
    forced = (jr == cur) | (jr == cur - 1)
    valid = jr <= cur
    c["tmul"] = (valid & ~forced).astype(np.float32)
    c["tadd"] = np.where(forced, 1e4, np.where(valid, 0.0, -1.0)).astype(np.float32)
    band = np.zeros((128, 4, 128), np.float32)
    bandp = np.zeros((128, 4, 128), np.float32)
    band0 = np.zeros((128, 4, 128), np.float32)
    s = np.arange(128)[:, None]
    t = np.arange(128)[None, :]
    for g, w in enumerate((2, 4, 8, 16)):
        band[:, g, :] = ((s <= t) & (s >= t - w + 1)) / float(w) - (s == t)
        bandp[:, g, :] = ((s - 128 >= t - w + 1) & (s - 128 <= t)) / float(w)
        cnt = np.minimum(t + 1, w).astype(np.float32)
        band0[:, g, :] = ((s <= t) & (s >= t - w + 1)) / cnt - (s == t)
    c["band"] = band
    c["bandp"] = bandp
    c["band0"] = band0
    return c


SEGS = [(0, 512, 0),
        (768, 896, 512), (1024, 1152, 640), (2328, 2584, 768),
        (512, 640, 1024), (640, 768, 1152), (896, 1024, 1280), (1152, 1280, 1408),
        (1304, 1816, 1536),
        (2072, 2328, 2048), (2584, 2840, 2304),
        (1816, 2072, 2560), (1280, 1304, 2816)]
R_GQ, R_GK1, R_GKC, R_GKM, R_PSC, R_INVF, R_GN, R_GM, R_END = 0, 64, 576, 704, 960, 1216, 1224, 1232, 1240


def build(NT):
    S = 128 * NT
    nc = bass.Bass("TRN2", target_bir_lowering=False)

    def din(name, shape, dt=F32):
        return nc.dram_tensor(name, shape, dt, kind="ExternalInput").ap()

    x_d = din("x", [S, 1024])
    mem_d = din("mem", [256, 1024])
    posl_d = din("posl", [128, NT], I32)
    posc_d = din("posc", [8, NT], I32)
    win_d = din("w_in", [1024, 2840])
    wout_d = din("w_out", [1024, 1024])
    wmem_d = din("w_mem_kv", [1024, 512])
    w1k_d = din("w_cmp_k1", [32, 64, 64])
    w1v_d = din("w_cmp_v1", [32, 64, 64])
    w2k_d = din("w_cmp_k2", [64, 64])
    w2v_d = din("w_cmp_v2", [64, 64])
    pk_d = din("cmp_pos_k", [32, 64])
    pv_d = din("cmp_pos_v", [32, 64])
    wpool_d = din("w_pool", [4, 64, 64])
    rep_d = din("rep", [128, R_END])
    cst = {n: din("c_" + n, list(v.shape)) for n, v in host_consts().items()}
    out_d = nc.dram_tensor("out", [S, 1024], F32, kind="ExternalOutput").ap()

    with contextlib.ExitStack() as es:
        def sb(name, shape, dt=F32):
            return es.enter_context(nc.sbuf_tensor(name, shape, dt))

        def ps(name, shape, dt=F32):
            return es.enter_context(nc.psum_tensor(name, shape, dt))

        P = Prog(nc)
        rr = [0]

        def anyeng():
            rr[0] += 1
            return ("dve", "pool", "act")[rr[0] % 3]

        Wb = sb("Wb", [128, 8, 2840], BF16)
        Woutb = sb("Woutb", [128, 8, 1024], BF16)
        KE = [sb("KE%d" % g, [128, S], BF16) for g in range(2)]
        VS = sb("VS", [128, NT, 2, 65], BF16)
        KW = [sb("KW%d" % g, [64, 6, 128], BF16) for g in range(2)]
        VW = sb("VW", [128, 6, 2, 65], BF16)
        KCT = [sb("KCT%d" % g, [64, 520], BF16) for g in range(2)]
        VC = sb("VC", [128, 4, 2, 65], BF16)
        HT = [sb("HT%d" % kv, [128, 520], BF16) for kv in range(2)]
        W1bd = [sb("W1bd%d" % kv, [128, 32, 128], BF16) for kv in range(2)]
        W2bd = [sb("W2bd%d" % kv, [128, 128], BF16) for kv in range(2)]
        CVEC = sb("CVEC", [128, 2])
        RAWT = [sb("RAWT%d" % kv, [128, 144], BF16) for kv in range(2)]
        POST = sb("POST", [128, 2, 32], BF16)
        QB2 = [[sb("QB%d_%d" % (par, g), [128, 2, 512], BF16) for g in range(2)] for par in range(2)]
        QMT = sb("QMT", [64, 512], BF16)
        MKT = sb("MKT", [64, 4, 256], BF16)
        MV = sb("MV", [128, 2, 4, 65], BF16)
        XT = [sb("XT%d" % j, [128, 1024]) for j in range(2)]
        Xb = sb("Xb", [128, 1024], BF16)
        XTr = sb("XTr", [128, 8, 128], BF16)
        ST = sb("ST", [128, 16])
        RSTD = sb("RSTD", [128, 2])
        Qf = sb("Qf", [128, 8, 64])
        Qs = sb("Qs", [128, 8, 64])
        Qb16 = sb("Qb16", [128, 8, 64], BF16)
        Kb16 = sb("Kb16", [128, 8, 64], BF16)
        RT = sb("RT", [128, 8, 4, 8])
        CKV = sb("CKV", [128, 256], BF16)
        SZ = [sb("SZ%d" % j, [128, 512], BF16) for j in range(2)]
        EZ = sb("EZ", [128, 512])
        VP = [sb("VP%d" % j, [128, 256], BF16) for j in range(2)]
        GATE = sb("GATE", [128, 24])
        PLT = sb("PLT", [64, 4, 128], BF16)
        Wpl = sb("Wpl", [64, 4, 64], BF16)
        PTb = [sb("PT%d" % j, [128, 512], BF16) for j in range(3)]
        CMs = [sb("CM%d" % j, [128, 128], BF16) for j in range(2)]
        cmc = [0]
        IMP = sb("IMP", [128, 128])
        WK = sb("WK", [128, 128])
        MX = sb("MX", [128, 16])
        BIASg = [sb("BIAS%d" % g, [128, 192], BF16) for g in range(2)]
        DEN = sb("DEN", [128, 8])
        OACC = sb("OACC", [128, 8, 64])
        OTMP = sb("OTMP", [128, 4, 64])
        OTS = [sb("OTS%d" % j, [65, 512], BF16) for j in range(2)]
        otc = [0]
        Y = Xb
        YM = sb("YM", [128, 256], BF16)
        GX = sb("GX", [128, 4, 8])
        K8 = sb("K8", [8, 2, 64])
        K8s = sb("K8s", [8, 2, 64])
        K8b = sb("K8b", [8, 2, 64], BF16)
        K8r = sb("K8r", [8, 2, 4, 8])
        K8st = sb("K8st", [8, 4])
        REP = sb("REP", [128, R_END])
        POSF = sb("POSF", [128, NT])
        POSI = sb("POSI", [128, NT], I32)
        COS = sb("COS", [128, NT, 8])
        SIN = sb("SIN", [128, NT, 8])
        IDf = sb("IDf", [128, 128])
        IDb = sb("IDb", [128, 128], BF16)
        TLE = sb("TLE", [128, 128], BF16)
        TGT = sb("TGT", [128, 128], BF16)
        D16 = sb("D16", [128, 128])
        SELM = sb("SELM", [128, 4, 128], BF16)
        TMUL = sb("TMUL", [128, 256])
        TADD = sb("TADD", [128, 256])
        BAND = sb("BAND", [128, 4, 128], BF16)
        BANDP = sb("BANDP", [128, 4, 128], BF16)
        BAND0 = sb("BAND0", [128, 4, 128], BF16)
        pX = ps("pX", [128, 1024], BF16)
        pA = [ps("pA%d" % j, [128, 512]) for j in range(3)]
        pO = [ps("pO%d" % j, [128, 4, 128]) for j in range(1)]
        pI = ps("pI", [128, 4, 128])
        pM0_ = ps("pM0", [128, 512])
        pM = [pM0_, pM0_]
        pT = ps("pT", [128, 4, 128], BF16)

        fin = []

        def load(dst_ap, src_ap, key, slot, eng="sp"):
            return P.op(eng, lambda e: e.dma_start(out=dst_ap, in_=src_ap), writes=[key], dma_slot=slot)

        load(REP[:], rep_d, "REP", "REP")
        load(IDf[:], cst["ident"], "IDf", "IDf")
        P.op("dve", lambda e: e.tensor_copy(out=IDb[:], in_=IDf[:]), reads=["IDf"], writes=["IDb"])
        load(D16[:], cst["d16"], "D16", "D16")
        load(TMUL[:], cst["tmul"], "TMUL", "TMUL")
        load(TADD[:], cst["tadd"], "TADD", "TADD")
        for (bt, bn) in ((BAND, "band"), (BANDP, "bandp"), (BAND0, "band0")):
            load(XT[1][:, 0:512], cst[bn].rearrange("p g t -> p (g t)"), "XT1", "XT1")
            P.op("dve", lambda e, bt=bt: e.tensor_copy(out=bt[:].rearrange("p g t -> p (g t)"), in_=XT[1][:, 0:512]), reads=["XT1"], writes=[bn.upper()])
        load(XT[0][:, 0:128], cst["tri_le"], "XT0", "XT0")
        P.op("dve", lambda e: e.tensor_scalar(out=TLE[:], in0=XT[0][:, 0:128], scalar1=-1.0, scalar2=-NEG, op0=ALU.add, op1=ALU.mult), reads=["XT0"], writes=["NTLE"])
        load(XT[0][:, 0:128], cst["tri_gt"], "XT0", "XT0")
        P.op("dve", lambda e: e.tensor_scalar(out=TGT[:], in0=XT[0][:, 0:128], scalar1=-1.0, scalar2=-NEG, op0=ALU.add, op1=ALU.mult), reads=["XT0"], writes=["NTGT"])
        NEGM = {"TLE": TLE, "TGT": TGT}
        load(XT[0][:, 0:512], cst["selmap"].rearrange("p c j -> p (c j)"), "XT0", "XT0")
        P.op("dve", lambda e: e.tensor_copy(out=SELM[:].rearrange("p c j -> p (c j)"), in_=XT[0][:, 0:512]), reads=["XT0"], writes=["SELM"])
        load(POSI[:], posl_d, "POSI", "POSI")
        P.op("dve", lambda e: e.tensor_copy(out=POSF[:], in_=POSI[:]), reads=["POSI"], writes=["POSF"])

        def cos_sin(COSt, SINt, ANGt, POSFt, npart, n, kp, KIt, KFt):
            invf = REP[0:npart, R_INVF:R_INVF + 8]
            P.op("dve", lambda e: e.tensor_tensor(out=ANGt, in0=POSFt.unsqueeze(2).to_broadcast([npart, n, 8]),
                                                  in1=invf.unsqueeze(1).to_broadcast([npart, n, 8]), op=ALU.mult),
                 reads=[kp + "POSF", "REP"], writes=["qsrc"])
            for dst, off, nm in ((SINt, 0.5, "SIN"), (COSt, 0.75, "COS")):
                P.op("dve", lambda e, dst=dst, off=off: e.tensor_scalar(out=dst, in0=ANGt, scalar1=float(1.0 / (2 * np.pi)), scalar2=float(off),
                                                                        op0=ALU.mult, op1=ALU.add), reads=["qsrc"], writes=[kp + nm])
                P.op("dve", lambda e, dst=dst: e.tensor_copy(out=KIt, in_=dst), reads=[kp + nm], writes=["qsq"])
                P.op("dve", lambda e: e.tensor_copy(out=KFt, in_=KIt), reads=["qsq"], writes=["EZ"])
                P.op("dve", lambda e, dst=dst: e.tensor_tensor(out=dst, in0=dst, in1=KFt, op=ALU.subtract), reads=[kp + nm, "EZ"], writes=[kp + nm])
                P.op("dve", lambda e, dst=dst: e.tensor_scalar(out=KFt, in0=dst, scalar1=0.0, scalar2=None, op0=ALU.is_lt), reads=[kp + nm], writes=["EZ"])
                P.op("dve", lambda e, dst=dst: e.tensor_tensor(out=dst, in0=dst, in1=KFt, op=ALU.add), reads=[kp + nm, "EZ"], writes=[kp + nm])
                P.op("act", lambda e, dst=dst: e.activation(out=dst, in_=dst, func=AF.Sin, scale=float(2 * np.pi), bias=float(-np.pi)), reads=[kp + nm], writes=[kp + nm])

        assert NT * 8 <= 512
        ANG = Qf[:].rearrange("p h d -> p (h d)")[:, 0:NT * 8].rearrange("p (n j) -> p n j", j=8)
        KI = Qs[:].rearrange("p h d -> p (h d)")[:, 0:NT * 8].bitcast(I32).rearrange("p (n j) -> p n j", j=8)
        KF = EZ[:, 0:NT * 8].rearrange("p (n j) -> p n j", j=8)
        cos_sin(COS[:], SIN[:], ANG, POSF[:], 128, NT, "", KI, KF)
        POSCI = sb("POSCI", [8, NT], I32)
        POSCF = sb("POSCF", [8, NT])
        COSC = sb("COSC", [8, NT, 8])
        SINC = sb("SINC", [8, NT, 8])
        load(POSCI[:], posc_d, "cPOSI", "cPOSI")
        P.op("dve", lambda e: e.tensor_copy(out=POSCF[:], in_=POSCI[:]), reads=["cPOSI"], writes=["cPOSF"])
        cos_sin(COSC[:], SINC[:], ANG[0:8], POSCF[:], 8, NT, "c", KI[0:8], KF[0:8])

        stg = [(XT[0], "XT0"), (XT[1], "XT1")]
        sidx = [0]

        def stage():
            sidx[0] += 1
            return stg[sidx[0] % 2]

        def conv(out_ap, in_ap, scal, rkeys, wkeys):
            eng = anyeng()
            if scal is None:
                if eng == "act":
                    P.op("act", lambda e: e.copy(out=out_ap, in_=in_ap), reads=rkeys, writes=wkeys)
                else:
                    P.op(eng, lambda e: e.tensor_copy(out=out_ap, in_=in_ap), reads=rkeys, writes=wkeys)
            else:
                if eng == "act":
                    P.op("act", lambda e: e.mul(out=out_ap, in_=in_ap, mul=scal), reads=rkeys + ["REP"], writes=wkeys)
                else:
                    P.op(eng, lambda e: e.tensor_scalar(out=out_ap, in0=in_ap, scalar1=scal, scalar2=None, op0=ALU.mult),
                         reads=rkeys + ["REP"], writes=wkeys)

        for kc in range(8):
            for (a, b) in ((0, 1024), (1024, 2048), (2048, 2840)):
                T_, tk = stage()
                load(T_[:, 0:b - a], win_d[kc * 128:(kc + 1) * 128, a:b], tk, tk)
                for (s0, s1, d0) in SEGS:
                    lo, hi = max(a, s0), min(b, s1)
                    if lo < hi:
                        conv(Wb[:, kc, d0 + lo - s0:d0 + hi - s0], T_[:, lo - a:hi - a], REP[:, R_GN + kc:R_GN + kc + 1], [tk], ["Wb"])
        for kc in range(8):
            T_, tk = stage()
            load(T_[:], wout_d[kc * 128:(kc + 1) * 128, :], tk, tk)
            conv(Woutb[:, kc, :], T_[:], None, [tk], ["Woutb"])
        for kv, (w1_d, w2_d, p_d) in enumerate(((w1k_d, w2k_d, pk_d), (w1v_d, w2v_d, pv_d))):
            P.op("pool", lambda e, kv=kv: e.memset(W1bd[kv][:], 0.0), writes=["W1bd%d" % kv])
            P.op("pool", lambda e, kv=kv: e.memset(W2bd[kv][:], 0.0), writes=["W2bd%d" % kv])
            for g in range(2):
                for half in range(2):
                    T_, tk = stage()
                    Tv = T_[g * 64:(g + 1) * 64, 0:1024].rearrange("p (l e) -> p l e", e=64)
                    P.op("sp", lambda e, Tv=Tv, half=half, w1_d=w1_d: e.dma_start(
                        out=Tv, in_=w1_d[half * 16:(half + 1) * 16].rearrange("l d e -> d l e")), writes=[tk], dma_slot=tk)
                    conv(W1bd[kv][g * 64:(g + 1) * 64, half * 16:(half + 1) * 16, g * 64:(g + 1) * 64], Tv, None, [tk], ["W1bd%d" % kv])
                T_, tk = stage()
                load(T_[g * 64:(g + 1) * 64, 0:64], w2_d, tk, tk)
                conv(W2bd[kv][g * 64:(g + 1) * 64, g * 64:(g + 1) * 64], T_[g * 64:(g + 1) * 64, 0:64], None, [tk], ["W2bd%d" % kv])
                T_, tk = stage()
                P.op("sp", lambda e, T_=T_, g=g, p_d=p_d: e.dma_start(out=T_[g * 64:(g + 1) * 64, 0:32], in_=p_d.rearrange("l d -> d l"),
                                                                       allow_slow_non_contiguous=True), writes=[tk], dma_slot=tk)
                conv(POST[g * 64:(g + 1) * 64, kv, :], T_[g * 64:(g + 1) * 64, 0:32], None, [tk], ["POST"])
        T_, tk = stage()
        load(T_[0:64, 0:256].rearrange("p (g e) -> p g e", g=4), wpool_d.rearrange("g c e -> c g e"), tk, tk)
        conv(Wpl[:], T_[0:64, 0:256].rearrange("p (g e) -> p g e", g=4), None, [tk], ["Wpl"])
        for kv in range(2):
            for l in range(32):
                P.op("pe", lambda e, kv=kv, l=l: e.matmul(pM[0][:, kv:kv + 1], lhsT=W1bd[kv][:, l, :], rhs=POST[:, kv, l:l + 1],
                                                          start=(l == 0), stop=(l == 31)),
                     reads=["W1bd%d" % kv, "POST"], writes=["pM0"])
        P.op("dve", lambda e: e.tensor_copy(out=CVEC[:], in_=pM[0][:, 0:2]), reads=["pM0"], writes=["CVEC"])
        for g in range(2):
            P.op("pool", lambda e, g=g: e.memset(KE[g][:], 1.0), writes=["KEinit%d" % g])
            for h0 in range(0, S, 4096):
                wd = min(4096, S - h0)
                v = KE[g][:, h0:h0 + wd]
                P.op("pool", lambda e, v=v, wd=wd: e.affine_select(out=v, in_=v, pattern=[[1, wd]], compare_op=ALU.is_ge, fill=0.0,
                                                                   base=4096, channel_multiplier=-64), writes=["KEinit%d" % g])
                P.op("pool", lambda e, v=v, wd=wd: e.affine_select(out=v, in_=v, pattern=[[-1, wd]], compare_op=ALU.is_ge, fill=0.0,
                                                                   base=-4096 + 63, channel_multiplier=64), writes=["KEinit%d" % g])
        P.op("pool", lambda e: e.memset(VS[:], 1.0), writes=["VSinit"])
        P.op("pool", lambda e: e.memset(VW[:], 1.0), writes=["VWinit"])
        P.op("pool", lambda e: e.memset(VC[:], 1.0), writes=["VCinit"])
        P.op("pool", lambda e: e.memset(MV[:], 1.0), writes=["MVinit"])
        for kv in range(2):
            P.op("pool", lambda e, kv=kv: e.memset(HT[kv][:], 0.0), writes=["HT%d" % kv])
            P.op("pool", lambda e, kv=kv: e.memset(RAWT[kv][:], 0.0), writes=["RAWT%d" % kv])
        for g in range(2):
            P.op("pool", lambda e, g=g: e.memset(KCT[g][:], 0.0), writes=["KCT%d" % g])
            for par in range(2):
                P.op("pool", lambda e, g=g, par=par: e.memset(QB2[par][g][:], 0.0), writes=["QB%d_%d" % (par, g), "QBb%d_%d" % (par, g), "QBc%d_%d" % (par, g)])

        def rstd_from_ss(ss_ap, out_ap, n, rk, wk):
            P.op("act", lambda e: e.activation(out=out_ap, in_=ss_ap, func=AF.Ln, scale=1.0 / n, bias=EPS), reads=rk, writes=wk)
            P.op("act", lambda e: e.activation(out=out_ap, in_=out_ap, func=AF.Exp, scale=-0.5), reads=wk, writes=wk)

        def token_norm_T(src, skey):
            P.op("act", lambda e: e.activation(out=Xb[:], in_=src[:], func=AF.Square, accum_out=ST[:, 0:1]),
                 reads=[skey], writes=["Xb", "ST"])
            rstd_from_ss(ST[:, 0:1], RSTD[:, 0:1], 1024.0, ["ST"], ["RSTD"])
            P.op("dve", lambda e: e.tensor_scalar(out=RSTD[:, 1:2], in0=RSTD[:, 0:1], scalar1=-1.0, scalar2=None, op0=ALU.mult),
                 reads=["RSTD"], writes=["RSTD"])
            P.op("dve", lambda e: e.tensor_copy(out=Xb[:], in_=src[:]), reads=[skey], writes=["Xb"])
            for kc in range(8):
                P.op("pe", lambda e, kc=kc: e.transpose(out=pX[:, kc * 128:(kc + 1) * 128], in_=Xb[:, kc * 128:(kc + 1) * 128], identity=IDb[:]),
                     reads=["Xb", "IDb"], writes=["pX"])
            P.op("act", lambda e: e.copy(out=XTr[:].rearrange("p k t -> p (k t)"), in_=pX[:]), reads=["pX"], writes=["XTr"])

        def proj(W, wkey, c0, c1, pt, pkey):
            for kc in range(8):
                P.op("pe", lambda e, kc=kc: e.matmul(pt[:, 0:c1 - c0], lhsT=XTr[:, kc, :], rhs=W[:, kc, c0:c1], start=(kc == 0), stop=(kc == 7)),
                     reads=["XTr", wkey], writes=[pkey])

        def head_norm_rope(nparts, src3, nh, gain3, cos2, sin2, nrope, kp, tmpS, tmpR, stt, out_b3, src_keys=None, out_key=None):
            src_keys = src_keys or [kp + "src"]
            out_key = out_key or (kp + "b")
            P.op("dve", lambda e: e.tensor_tensor(out=tmpS, in0=src3, in1=src3, op=ALU.mult), reads=src_keys, writes=[kp + "sq"])
            P.op("dve", lambda e: e.tensor_reduce(out=stt, in_=tmpS, axis=AX.X, op=ALU.add), reads=[kp + "sq"], writes=[kp + "st"])
            rstd_from_ss(stt, stt, 64.0, [kp + "st"], [kp + "st"])
            P.op("dve", lambda e: e.tensor_tensor(out=tmpS, in0=src3, in1=stt.unsqueeze(2).to_broadcast([nparts, nh, 64]), op=ALU.mult),
                 reads=src_keys + [kp + "st"], writes=[kp + "sq"])
            P.op("dve", lambda e: e.tensor_tensor(out=tmpS, in0=tmpS, in1=gain3, op=ALU.mult), reads=[kp + "sq", "REP"], writes=[kp + "sq"])
            if nrope:
                x1 = tmpS[:, 0:nrope, 0:8]
                x2 = tmpS[:, 0:nrope, 8:16]
                cb = cos2.unsqueeze(1).to_broadcast([nparts, nrope, 8])
                sbb = sin2.unsqueeze(1).to_broadcast([nparts, nrope, 8])
                t = [tmpR[:, 0:nrope, j, :] for j in range(4)]
                for (o, a_, b_) in ((t[0], x1, cb), (t[1], x2, sbb), (t[2], x1, sbb), (t[3], x2, cb)):
                    P.op("dve", lambda e, o=o, a_=a_, b_=b_: e.tensor_tensor(out=o, in0=a_, in1=b_, op=ALU.mult),
                         reads=[kp + "sq", kp + "cs"], writes=[kp + "rt"])
                P.op("dve", lambda e: e.tensor_tensor(out=x1, in0=t[0], in1=t[1], op=ALU.subtract), reads=[kp + "rt"], writes=[kp + "sq"])
                P.op("dve", lambda e: e.tensor_tensor(out=x2, in0=t[2], in1=t[3], op=ALU.add), reads=[kp + "rt"], writes=[kp + "sq"])
            P.op("dve", lambda e: e.tensor_copy(out=out_b3, in_=tmpS), reads=[kp + "sq"], writes=[out_key])

        def silu_from_psum(pt, pkey, out_ap, okey, Z, zkey):
            P.op("act", lambda e: e.mul(out=Z, in_=pt[:], mul=RSTD[:, 0:1]), reads=[pkey, "RSTD"], writes=[zkey])
            P.op("act", lambda e: e.activation(out=EZ[:], in_=Z, func=AF.Exp, scale=-1.0), reads=[zkey], writes=["EZ"])
            P.op("dve", lambda e: e.tensor_scalar(out=EZ[:], in0=EZ[:], scalar1=1.0, scalar2=None, op0=ALU.add), reads=["EZ"], writes=["EZ"])
            P.op("dve", lambda e: e.reciprocal(out=EZ[:], in_=EZ[:]), reads=["EZ"], writes=["EZ"])
            P.op("dve", lambda e: e.tensor_tensor(out=out_ap, in0=Z, in1=EZ[:], op=ALU.mult), reads=[zkey, "EZ"], writes=[okey])

        ptc = [0]

        def next_pt():
            ptc[0] += 1
            j = ptc[0] % 3
            return PTb[j], "PT%d" % j

        pac = [0]

        pa_free = [None]

        def next_pa():
            if pa_free[0] is not None:
                j = pa_free[0]
                return pA[j], "pA%d" % j
            pac[0] += 1
            j = pac[0] % 3
            return pA[j], "pA%d" % j

        poc = [0]

        def next_po():
            return pO[0], "pO0"

        def mk_desc(lhsT, lkeys, rhs, rkeys, m, mask, v_rhs, vkeys, po, pokey, first, last, imp_kc=None, pre=None, post=None):
            def qk(pa, pak):
                add = mask is not None and mask[1] in ("TLE", "TGT")
                P.op("pe", lambda e: e.matmul(pa[0:m, :], lhsT=lhsT, rhs=rhs, start=True, stop=not add), reads=lkeys + rkeys, writes=[pak])
                if add:
                    nb = NEGM[mask[1]]
                    P.op("pe", lambda e: e.matmul(pa[:, :].rearrange("p (r q) -> p r q", r=4), lhsT=IDb[:], rhs=nb[:].unsqueeze(1).to_broadcast([128, 4, 128]),
                                                  start=False, stop=True), reads=["IDb", "N" + mask[1]], writes=[pak])

            def pv(pt, ptk):
                P.op("pe", lambda e: e.matmul(po[0:65, :, :].rearrange("p r q -> p (r q)"), lhsT=v_rhs(0), rhs=pt[0:m, :], start=first, stop=last),
                     reads=[ptk] + vkeys, writes=[pokey])
                if imp_kc is not None:
                    for r in range(4):
                        P.op("pe", lambda e, r=r: e.matmul(pI[:, r, :], lhsT=pt[0:m, r * 128:(r + 1) * 128], rhs=SELM[0:m, imp_kc, :], start=(first and r == 0), stop=last, skip_group_check=True),
                             reads=[ptk, "SELM"], writes=["pI"])
            return dict(m=m, qk=qk, pv=pv, mask=mask, pre=pre, post=post)

        def run_pipeline(descs, hooks=()):
            n = len(descs)
            hooks = sorted(hooks, key=lambda x: x[0])
            hp = [0]

            def run_hooks(k):
                while hp[0] < len(hooks) and hooks[hp[0]][0] <= k:
                    hooks[hp[0]][1]()
                    hp[0] += 1
            bufs = [None] * n

            def issue(k):
                d = descs[k]
                if d["pre"] is not None:
                    d["pre"]()
                pa, pak = pA[k % 3], "pA%d" % (k % 3)
                pt, ptk = next_pt()
                bufs[k] = (pa, pak, pt, ptk)
                d["qk"](pa, pak)
            issue(0)
            if n > 1:
                issue(1)
            deferred = []
            for k in range(n):
                for (due, fn) in [x for x in deferred if x[0] <= k]:
                    fn()
                deferred = [x for x in deferred if x[0] > k]
                if k + 2 < n:
                    issue(k + 2)
                d = descs[k]
                pa, pak, pt, ptk = bufs[k]
                m = d["m"]
                P.op("act", lambda e, pa=pa, pt=pt, m=m: e.activation(out=pt[0:m, :], in_=pa[0:m, :], func=AF.Exp, scale=0.125), reads=[pak], writes=[ptk])
                if d["mask"] is not None and d["mask"][1] not in ("TLE", "TGT"):
                    mt, mk = d["mask"]
                    P.op("pool", lambda e, pt=pt, m=m, mt=mt: e.tensor_tensor(out=pt[0:m, :].rearrange("p (r q) -> p r q", r=4), in0=pt[0:m, :].rearrange("p (r q) -> p r q", r=4),
                                                                              in1=mt[0:m, :].unsqueeze(1).to_broadcast([m, 4, 128]), op=ALU.mult),
                         reads=[ptk, mk], writes=[ptk])
                d["pv"](pt, ptk)
                if d["post"] is not None:
                    later = d["post"]()
                    if later is not None:
                        deferred.append((k + 2, later))
                pa_free[0] = k % 3
                run_hooks(k)
                pa_free[0] = None
            for (due, fn) in deferred:
                fn()
            run_hooks(10 ** 9)

        def finish_branch(po, pokey, fn):
            otc[0] += 1
            j = otc[0] % 2
            ots, otk = OTS[j], "OTS%d" % j
            src = po[0:65, :, :].rearrange("p r q -> p (r q)")
            P.op("act", lambda e: e.copy(out=ots[:], in_=src), reads=[pokey], writes=[otk])

            def later():
                for r in range(4):
                    P.op("pe", lambda e, r=r: e.transpose(out=pT[:, r, 0:65], in_=ots[:, r * 128:(r + 1) * 128], identity=IDb[0:65, 0:65]),
                         reads=[otk, "IDb"], writes=["pT"])
                fn()
            return later

        def combine(po, pokey, g, b, first):
            den = DEN[:, 0:4]
            P.op("dve", lambda e: e.tensor_scalar(out=den, in0=po[:, :, 64], scalar1=1e-30, scalar2=None, op0=ALU.max), reads=[pokey], writes=["DEN"])
            P.op("dve", lambda e: e.reciprocal(out=den, in_=den), reads=["DEN"], writes=["DEN"])
            if b is not None:
                P.op("dve", lambda e: e.tensor_tensor(out=DEN[:, 4:8], in0=den, in1=GATE[:, 12 * g + b:12 * g + 12:3], op=ALU.mult),
                     reads=["DEN", "GATE"], writes=["DEN2"])
                w = DEN[:, 4:8]
                wk = ["DEN2"]
            else:
                w = den
                wk = ["DEN"]
            dst = OACC[:, 4 * g:4 * g + 4, :]
            wb = w.unsqueeze(2).to_broadcast([128, 4, 64])
            if first:
                P.op("dve", lambda e: e.tensor_tensor(out=dst, in0=po[:, :, 0:64], in1=wb, op=ALU.mult), reads=[pokey] + wk, writes=["OACC%d" % g])
            else:
                P.op("dve", lambda e: e.tensor_tensor(out=OTMP[:], in0=po[:, :, 0:64], in1=wb, op=ALU.mult), reads=[pokey] + wk, writes=["OTMP"])
                P.op("pool", lambda e: e.tensor_tensor(out=dst, in0=dst, in1=OTMP[:], op=ALU.add), reads=["OTMP", "OACC%d" % g], writes=["OACC%d" % g])

        for mt in range(2):
            T_, tk = XT[mt], "XT%d" % mt
            load(T_[:], mem_d[mt * 128:(mt + 1) * 128, :], tk, tk)
            token_norm_T(T_, tk)
            pa, pak = next_pa()
            for kc in range(8):
                sl = kc % 2
                stg_t = (EZ[:], OACC[:].rearrange("p h d -> p (h d)"))[sl]
                stg_k = (["EZ"], ["OACC0", "OACC1"])[sl]
                P.op("sp", lambda e, kc=kc, stg_t=stg_t: e.dma_start(out=stg_t, in_=wmem_d[kc * 128:(kc + 1) * 128, :]), writes=stg_k, dma_slot="wm%d" % sl)
                conv(PTb[sl][:], stg_t, REP[:, R_GM + kc:R_GM + kc + 1], stg_k, ["PT%d" % sl])
                P.op("pe", lambda e, kc=kc, sl=sl, pa=pa: e.matmul(pa[:], lhsT=XTr[:, kc, :], rhs=PTb[sl][:], start=(kc == 0), stop=(kc == 7)),
                     reads=["XTr", "PT%d" % sl], writes=[pak])
            P.op("act", lambda e, pa=pa: e.mul(out=Qf[:, 0:4, :].rearrange("p h d -> p (h d)"), in_=pa[:, 0:256], mul=RSTD[:, 0:1]),
                 reads=[pak, "RSTD"], writes=["qsrc"])
            P.op("dve", lambda e, pa=pa, mt=mt: e.tensor_scalar(out=MV[:, mt, :, 0:64], in0=pa[:, 256:512].rearrange("p (h d) -> p h d", h=4),
                                                                scalar1=RSTD[:, 0:1], scalar2=None, op0=ALU.mult),
                 reads=[pak, "RSTD", "MVinit"], writes=["MV"])
            head_norm_rope(128, Qf[:, 0:4, :], 4, REP[:, R_GKM:R_GKM + 256].rearrange("p (h d) -> p h d", h=4), None, None, 0, "q",
                           Qs[:, 0:4, :], None, ST[:, 8:12], Qb16[:, 0:4, :])
            for h in range(4):
                P.op("pe", lambda e, h=h: e.transpose(out=pX[0:64, h * 128:(h + 1) * 128], in_=Qb16[:, h, :], identity=IDb[:]),
                     reads=["qb", "IDb"], writes=["pX"])
            P.op("act", lambda e, mt=mt: e.copy(out=MKT[:, :, mt * 128:(mt + 1) * 128], in_=pX[0:64, 0:512].rearrange("p (h m) -> p h m", h=4)),
                 reads=["pX"], writes=["MKT"])

        def compress_body(i, kv):
            pm, pmk = pM[kv], "pM0"
            for l in range(32):
                P.op("pe", lambda e, l=l: e.matmul(pm[:, 0:8], lhsT=W1bd[kv][:, l, :], rhs=RAWT[kv][:, l:l + 113:16], start=(l == 0), stop=(l == 31)),
                     reads=["W1bd%d" % kv, "RAWT%d" % kv], writes=[pmk])
            gx = [GX[:, j, :] for j in range(4)]
            P.op("dve", lambda e: e.tensor_scalar(out=gx[0], in0=pm[:, 0:8], scalar1=CVEC[:, kv:kv + 1], scalar2=None, op0=ALU.add),
                 reads=[pmk, "CVEC"], writes=["GX"])
            P.op("dve", lambda e: e.tensor_tensor(out=gx[1], in0=gx[0], in1=gx[0], op=ALU.mult), reads=["GX"], writes=["GX"])
            P.op("dve", lambda e: e.tensor_scalar(out=gx[1], in0=gx[1], scalar1=0.044715, scalar2=1.0, op0=ALU.mult, op1=ALU.add), reads=["GX"], writes=["GX"])
            P.op("dve", lambda e: e.tensor_tensor(out=gx[1], in0=gx[1], in1=gx[0], op=ALU.mult), reads=["GX"], writes=["GX"])
            P.op("act", lambda e: e.activation(out=gx[2], in_=gx[1], func=AF.Exp, scale=-2.0 * GELU_C), reads=["GX"], writes=["GX"])
            P.op("dve", lambda e: e.tensor_scalar(out=gx[2], in0=gx[2], scalar1=1.0, scalar2=None, op0=ALU.add), reads=["GX"], writes=["GX"])
            P.op("dve", lambda e: e.reciprocal(out=gx[2], in_=gx[2]), reads=["GX"], writes=["GX"])
            P.op("dve", lambda e: e.tensor_tensor(out=HT[kv][:, 8 * i:8 * i + 8], in0=gx[2], in1=gx[0], op=ALU.mult), reads=["GX"], writes=["HT%d" % kv])
            P.op("pool", lambda e: e.tensor_copy(out=RAWT[kv][:, 0:16], in_=RAWT[kv][:, 128:144]), reads=["RAWT%d" % kv], writes=["RAWT%d" % kv])

        def cmp_descs(i, g):
            nvis = 8 * i + 7
            nkc = (nvis + 127) // 128
            QBt = QB2[i % 2][g]
            q64 = QBt[0:64, 0, :]
            qk = ["QB%d_%d" % (i % 2, g)]
            po, pok = next_po()
            out = []

            def post():
                return finish_branch(po, pok, post2)

            def post2():
                den = DEN[:, 0:4]
                P.op("dve", lambda e: e.tensor_scalar(out=den, in0=pT[:, :, 64], scalar1=1e-30, scalar2=None, op0=ALU.max), reads=["pT"], writes=["DEN"])
                P.op("dve", lambda e: e.reciprocal(out=den, in_=den), reads=["DEN"], writes=["DEN"])
                P.op("dve", lambda e: e.tensor_scalar(out=IMP[:], in0=pI[:, 0, :], scalar1=DEN[:, 0:1], scalar2=None, op0=ALU.mult), reads=["pI", "DEN"], writes=["IMP"])
                for r in range(1, 4):
                    P.op("dve", lambda e, r=r: e.scalar_tensor_tensor(out=IMP[:], in0=pI[:, r, :], scalar=DEN[:, r:r + 1], in1=IMP[:], op0=ALU.mult, op1=ALU.add),
                         reads=["pI", "DEN", "IMP"], writes=["IMP"])
                P.op("dve", lambda e: e.tensor_tensor(out=IMP[:], in0=IMP[:], in1=TMUL[:, 128 - 2 * i:256 - 2 * i], op=ALU.mult), reads=["IMP", "TMUL"], writes=["IMP"])
                P.op("dve", lambda e: e.tensor_tensor(out=IMP[:], in0=IMP[:], in1=TADD[:, 128 - 2 * i:256 - 2 * i], op=ALU.add), reads=["IMP", "TADD"], writes=["IMP"])
                P.op("dve", lambda e: e.memset(IMP[:, 0:1], 1e4), reads=["IMP"], writes=["IMP"])
                P.op("dve", lambda e: e.max(out=MX[:, 0:8], in_=IMP[:]), reads=["IMP"], writes=["MX"])
                P.op("dve", lambda e: e.match_replace(out=WK[:], in_to_replace=MX[:, 0:8], in_values=IMP[:], imm_value=-3.0e38), reads=["IMP", "MX"], writes=["WK"])
                P.op("dve", lambda e: e.max(out=MX[:, 8:16], in_=WK[:]), reads=["WK"], writes=["MX"])
                B_ = BIASg[g]
                bk_ = "BIAS%d" % g
                P.op("dve", lambda e: e.tensor_scalar(out=B_[:, 0:128], in0=IMP[:], scalar1=MX[:, 15:16], scalar2=NEG, op0=ALU.is_lt, op1=ALU.mult), reads=["IMP", "MX"], writes=[bk_])
                P.op("dve", lambda e: e.tensor_scalar(out=B_[:, 128:192], in0=IMP[:, 0:64], scalar1=MX[:, 15:16], scalar2=NEG, op0=ALU.is_lt, op1=ALU.mult), reads=["IMP", "MX", bk_], writes=[bk_])
                combine(pT, "pT", g, 0, True)

            for kc in range(nkc):
                m = min(128, nvis - 128 * kc)
                full = 16 * (128 * kc + m - 1) + 31 <= 128 * i
                mask = None
                pre = None
                if not full:
                    delta = float(128 * (16 * kc - i) + 31)
                    cmc[0] += 1
                    CM = CMs[cmc[0] % 2]
                    cmk = "CM%d" % (cmc[0] % 2)
                    mask = (CM, cmk)

                    def pre(delta=delta, CM=CM, cmk=cmk):
                        P.op("pool", lambda e: e.tensor_scalar(out=CM[:], in0=D16[:], scalar1=delta, scalar2=None, op0=ALU.is_ge), reads=["D16"], writes=[cmk])
                out.append(mk_desc(KCT[g][:, 1 + kc * 128:1 + kc * 128 + m], ["KCT%d" % g], q64, qk, m, mask,
                                   lambda r, kc=kc, m=m: VC[0:m, kc, g, :], ["VC"], po, pok, kc == 0, kc == nkc - 1, imp_kc=kc, pre=pre,
                                   post=(post if kc == nkc - 1 else None)))
            return out

        def win_descs(i, g):
            QBt = QB2[i % 2][g]
            q64 = QBt[0:64, 0, :]
            qk = ["QB%d_%d" % (i % 2, g)]
            po, pok = next_po()
            kts = [kt for kt in range(i - 4, i + 1) if kt >= 0]
            out = []
            for n_, kt in enumerate(kts):
                mask = (TLE, "TLE") if kt == i else ((TGT, "TGT") if kt == i - 4 else None)
                last = n_ == len(kts) - 1
                out.append(mk_desc(KW[g][:, kt % 6, :], [("KW", g, kt % 6)], q64, qk, 128, mask,
                                   lambda r, kt=kt: VW[:, kt % 6, g, :], [("VW", kt % 6)], po, pok, n_ == 0, last,
                                   post=((lambda: finish_branch(po, pok, lambda: combine(pT, "pT", g, 2, False))) if last else None)))
            return out

        def sel_descs(i, g):
            QBt = QB2[i % 2][g]
            kq, kb, kc1 = "QB%d_%d" % (i % 2, g), "QBb%d_%d" % (i % 2, g), "QBc%d_%d" % (i % 2, g)
            qk = [kq, kb]
            po, pok = next_po()
            B_ = BIASg[g]
            bk_ = "BIAS%d" % g

            def pre():
                P.op("pe", lambda e: e.transpose(out=pT[:, 0, :], in_=B_[:, 64:192], identity=IDb[:]), reads=[bk_, "IDb"], writes=["pT"])
                if NT > 32:
                    P.op("pe", lambda e: e.transpose(out=pT[:, 1, :], in_=B_[:, 0:128], identity=IDb[:]), reads=[bk_, "IDb"], writes=["pT"])
                for r in range(4):
                    if r % 2 == 0:
                        P.op("dve", lambda e, r=r: e.tensor_copy(out=QBt[64:128, 0, r * 128:(r + 1) * 128], in_=pT[64:128, 0, :]), reads=["pT"], writes=[kb])
                    else:
                        P.op("act", lambda e, r=r: e.copy(out=QBt[64:128, 0, r * 128:(r + 1) * 128], in_=pT[64:128, 0, :]), reads=["pT"], writes=[kb])
                    if NT > 32:
                        if r % 2 == 1:
                            P.op("dve", lambda e, r=r: e.tensor_copy(out=QBt[64:128, 1, r * 128:(r + 1) * 128], in_=pT[64:128, 1, :]), reads=["pT"], writes=[kb])
                        else:
                            P.op("act", lambda e, r=r: e.copy(out=QBt[64:128, 1, r * 128:(r + 1) * 128], in_=pT[64:128, 1, :]), reads=["pT"], writes=[kb])
            out = []
            for kt in range(i + 1):
                cv = kt // 32
                mask = (TLE, "TLE") if kt == i else None
                out.append(mk_desc(KE[g][:, kt * 128:(kt + 1) * 128], [("KE", g, kt), "KEinit%d" % g], QBt[:, cv, :],
                                   qk + ([kc1] if cv == 1 else []), 128, mask,
                                   lambda r, kt=kt: VS[:, kt, g, :], [("VS", kt)], po, pok, kt == 0, kt == i,
                                   pre=(pre if kt == 0 else None), post=((lambda: finish_branch(po, pok, lambda: combine(pT, "pT", g, 1, False))) if kt == i else None)))
            return out

        def mem_descs(i):
            po, pok = next_po()
            out = []
            for mt in range(2):
                def qk(pa, pak, mt=mt):
                    for h in range(4):
                        P.op("pe", lambda e, h=h: e.matmul(pa[:, h * 128:(h + 1) * 128], lhsT=MKT[:, h, mt * 128:(mt + 1) * 128],
                                                           rhs=QMT[:, h * 128:(h + 1) * 128], start=True, stop=True), reads=["MKT", "QMT"], writes=[pak])

                def pv(pt, ptk, mt=mt):
                    for h in range(4):
                        P.op("pe", lambda e, h=h: e.matmul(po[0:65, h, :], lhsT=MV[:, mt, h, :], rhs=pt[:, h * 128:(h + 1) * 128],
                                                           start=(mt == 0 and h == 0), stop=(mt == 1), skip_group_check=True), reads=[ptk, "MV"], writes=[pok])

                def post():
                    return finish_branch(po, pok, post2)

                def post2():
                    den = DEN[:, 0:4]
                    P.op("dve", lambda e: e.tensor_scalar(out=den, in0=pT[:, :, 64], scalar1=1e-30, scalar2=None, op0=ALU.max), reads=["pT"], writes=["DEN"])
                    P.op("dve", lambda e: e.reciprocal(out=den, in_=den), reads=["DEN"], writes=["DEN"])
                    P.op("dve", lambda e: e.tensor_tensor(out=OTMP[:], in0=pT[:, :, 0:64], in1=den.unsqueeze(2).to_broadcast([128, 4, 64]), op=ALU.mult),
                         reads=["pT", "DEN"], writes=["OTMP"])
                    P.op("dve", lambda e: e.tensor_tensor(out=YM[:], in0=OTMP[:].rearrange("p h d -> p (h d)"), in1=SZ[1][:, 256:512], op=ALU.mult),
                         reads=["OTMP", "SZ1"], writes=["YM"])
                out.append(dict(m=128, qk=qk, pv=pv, mask=None, pre=None, post=(post if mt == 1 else None)))
            return out

        def front_a_pieces(i):
            X_, xk = XT[i % 2], "XT%d" % (i % 2)
            cosi, sini = COS[:, i, :], SIN[:, i, :]
            par = i % 2
            ws = i % 6
            st = {}

            def p_norm():
                P.op("act", lambda e: e.activation(out=Xb[:], in_=X_[:], func=AF.Square, accum_out=ST[:, 0:1]), reads=[xk], writes=["Xb", "ST"])
                rstd_from_ss(ST[:, 0:1], RSTD[:, 0:1], 1024.0, ["ST"], ["RSTD"])
                P.op("dve", lambda e: e.tensor_scalar(out=RSTD[:, 1:2], in0=RSTD[:, 0:1], scalar1=-1.0, scalar2=None, op0=ALU.mult), reads=["RSTD"], writes=["RSTD"])
                P.op("dve", lambda e: e.tensor_copy(out=Xb[:], in_=X_[:]), reads=[xk], writes=["Xb"])

            def p_xT():
                for kc in range(8):
                    P.op("pe", lambda e, kc=kc: e.transpose(out=pX[:, kc * 128:(kc + 1) * 128], in_=Xb[:, kc * 128:(kc + 1) * 128], identity=IDb[:]),
                         reads=["Xb", "IDb"], writes=["pX"])
                P.op("dve", lambda e: e.tensor_copy(out=XTr[:].rearrange("p k t -> p (k t)"), in_=pX[:]), reads=["pX"], writes=["XTr"])

            def p_g0():
                pa, pak = pM[0], "pM0"
                proj(Wb, "Wb", 0, 512, pa, pak)
                P.op("dve", lambda e: e.tensor_scalar(out=Qf[:].rearrange("p h d -> p (h d)"), in0=pa[:], scalar1=RSTD[:, 0:1], scalar2=None, op0=ALU.mult), reads=[pak, "RSTD", "COS", "SIN"], writes=["qsrc", "qcs"])

            def p_qchain():
                head_norm_rope(128, Qf[:], 8, REP[:, R_GQ:R_GQ + 64].unsqueeze(1).to_broadcast([128, 8, 64]), cosi, sini, 8, "q", Qs[:], RT[:], ST[:, 8:16], Qb16[:])

            def p_qT():
                for h in range(8):
                    P.op("pe", lambda e, h=h: e.transpose(out=pX[0:64, h * 128:(h + 1) * 128], in_=Qb16[:, h, :], identity=IDb[:]), reads=["qb", "IDb"], writes=["pX"])
                for g in range(2):
                    P.op("dve", lambda e, g=g: e.tensor_copy(out=QB2[par][g][0:64, 0, :], in_=pX[0:64, g * 512:(g + 1) * 512]), reads=["pX"], writes=["QB%d_%d" % (par, g)])
                    if NT > 32:
                        P.op("dve", lambda e, g=g: e.tensor_copy(out=QB2[par][g][0:64, 1, :], in_=pX[0:64, g * 512:(g + 1) * 512]), reads=["pX"], writes=["QBc%d_%d" % (par, g)])

            def p_g1():
                pa1, pak1 = pM[0], "pM0"
                proj(Wb, "Wb", 512, 1024, pa1, pak1)
                P.op("dve", lambda e: e.tensor_scalar(out=EZ[:], in0=pa1[:], scalar1=RSTD[:, 0:1], scalar2=None, op0=ALU.mult), reads=[pak1, "RSTD"], writes=["EZ"])

            def p_kchain():
                head_norm_rope(128, EZ[:].rearrange("p (h d) -> p h d", h=8), 8, REP[:, R_GK1:R_GK1 + 512].rearrange("p (h d) -> p h d", h=8), cosi, sini, 4, "q",
                               Qs[:], RT[:], ST[:, 8:16], Kb16[:], src_keys=["EZ"], out_key="kkb")

            def p_kT():
                for h in range(8):
                    P.op("pe", lambda e, h=h: e.transpose(out=pX[0:64, h * 128:(h + 1) * 128], in_=Kb16[:, h, :], identity=IDb[:]), reads=["kkb", "IDb"], writes=["pX"])
                for g in range(2):
                    P.op("dve", lambda e, g=g: e.tensor_copy(out=KE[g][0:64, i * 128:(i + 1) * 128], in_=pX[0:64, g * 128:(g + 1) * 128]),
                         reads=["pX", "KEinit%d" % g], writes=[("KE", g, i)])
                    P.op("dve", lambda e, g=g: e.tensor_copy(out=KW[g][:, ws, :], in_=pX[0:64, (2 + g) * 128:(3 + g) * 128]), reads=["pX"], writes=[("KW", g, ws)])
                P.op("dve", lambda e: e.tensor_copy(out=QMT[:], in_=pX[0:64, 512:1024]), reads=["pX"], writes=["QMT"])

            def p_g2():
                pa2, pak2 = pM[0], "pM0"
                proj(Wb, "Wb", 1024, 1536, pa2, pak2)
                P.op("dve", lambda e: e.tensor_scalar(out=CKV[:], in0=pa2[:, 0:256], scalar1=RSTD[:, 0:1], scalar2=None, op0=ALU.mult), reads=[pak2, "RSTD"], writes=["CKV"])
                P.op("dve", lambda e: e.tensor_scalar(out=VS[:, i, :, 0:64], in0=pa2[:, 256:384].rearrange("p (g d) -> p g d", g=2), scalar1=RSTD[:, 0:1],
                                                      scalar2=None, op0=ALU.mult), reads=[pak2, "RSTD", "VSinit"], writes=[("VS", i)])
                P.op("dve", lambda e: e.tensor_scalar(out=VW[:, ws, :, 0:64], in0=pa2[:, 384:512].rearrange("p (g d) -> p g d", g=2), scalar1=RSTD[:, 0:1],
                                                      scalar2=None, op0=ALU.mult), reads=[pak2, "RSTD", "VWinit"], writes=[("VW", ws)])

            def p_ckvT():
                for kv in range(2):
                    P.op("pe", lambda e, kv=kv: e.transpose(out=pX[:, kv * 128:(kv + 1) * 128], in_=CKV[:, kv * 128:(kv + 1) * 128], identity=IDb[:]),
                         reads=["CKV", "IDb"], writes=["pX"])
                for kv in range(2):
                    P.op("dve", lambda e, kv=kv: e.tensor_copy(out=RAWT[kv][:, 16:144], in_=pX[:, kv * 128:(kv + 1) * 128]), reads=["pX"], writes=["RAWT%d" % kv])

            def p_k8():
                P.op("pe", lambda e: e.matmul(pM[0][0:8, 0:128], lhsT=HT[0][:, 8 * i:8 * i + 8], rhs=W2bd[0][:], start=True, stop=True),
                     reads=["HT0", "W2bd0"], writes=["pM0"])
                P.op("act", lambda e: e.copy(out=K8[:].rearrange("p g d -> p (g d)"), in_=pM[0][0:8, 0:128]), reads=["pM0", "cCOS", "cSIN"], writes=["ksrc", "kcs"])
                head_norm_rope(8, K8[:], 2, REP[0:8, R_GKC:R_GKC + 128].rearrange("p (g d) -> p g d", g=2), COSC[:, i, :], SINC[:, i, :], 2, "k",
                               K8s[:], K8r[:], K8st[:, 0:2], K8b[:])

            def p_k8T():
                for g in range(2):
                    P.op("pe", lambda e, g=g: e.transpose(out=pX[0:64, g * 8:(g + 1) * 8], in_=K8b[:, g, :], identity=IDb[0:8, 0:8]), reads=["kb", "IDb"], writes=["pX"])
                for g in range(2):
                    P.op("dve", lambda e, g=g: e.tensor_copy(out=KCT[g][:, 8 * i:8 * i + 8], in_=pX[0:64, g * 8:(g + 1) * 8]), reads=["pX"], writes=["KCT%d" % g])

            def p_vc():
                clo, chi = max(8 * i - 1, 0) // 128, (8 * i + 6) // 128
                for c in range(clo, chi + 1):
                    P.op("pe", lambda e, c=c: e.matmul(pM[1][:, 0:128], lhsT=HT[1][:, 1 + c * 128:1 + (c + 1) * 128], rhs=W2bd[1][:], start=True, stop=True),
                         reads=["HT1", "W2bd1"], writes=["pM0"])
                    P.op("dve", lambda e, c=c: e.tensor_copy(out=VC[:, c, :, 0:64], in_=pM[1][:, 0:128].rearrange("p (g d) -> p g d", g=2)),
                         reads=["pM0", "VCinit"], writes=["VC"])

            return [(0, p_norm), (6, p_xT), (9, p_g0), (10, p_qchain), (11, p_g1), (12, p_kchain), (13, p_g2), (17, p_ckvT),
                    (20, lambda: compress_body(i, 0)), (24, p_qT), (26, lambda: compress_body(i, 1)), (30, p_kT),
                    (34, p_k8), (36, p_vc), (46, p_k8T)]

        def front_b(i):
            pa3, pak3 = next_pa()
            proj(Wb, "Wb", 1536, 2048, pa3, pak3)
            silu_from_psum(pa3, pak3, SZ[0][:], "SZ0", Qf[:].rearrange("p h d -> p (h d)"), "qsrc")
            pa4, pak4 = next_pa()
            proj(Wb, "Wb", 2048, 2560, pa4, pak4)
            silu_from_psum(pa4, pak4, SZ[1][:], "SZ1", Qs[:].rearrange("p h d -> p (h d)"), "qsq")
            pa5, pak5 = next_pa()
            proj(Wb, "Wb", 2560, 2840, pa5, pak5)
            vpc = VP[i % 2]
            P.op("act", lambda e: e.mul(out=vpc[:], in_=pa5[:, 0:256], mul=RSTD[:, 0:1]), reads=[pak5, "RSTD"], writes=["VP%d" % (i % 2)])
            P.op("act", lambda e: e.activation(out=GATE[:], in_=pa5[:, 256:280], func=AF.Exp, scale=RSTD[:, 1:2]), reads=[pak5, "RSTD"], writes=["GATE"])
            P.op("dve", lambda e: e.tensor_scalar(out=GATE[:], in0=GATE[:], scalar1=1.0, scalar2=None, op0=ALU.add), reads=["GATE"], writes=["GATE"])
            P.op("dve", lambda e: e.reciprocal(out=GATE[:], in_=GATE[:]), reads=["GATE"], writes=["GATE"])

        def attention(i, next_pieces):
            descs = []
            for g in range(2):
                descs += cmp_descs(i, g)
                descs += win_descs(i, g)
            descs += mem_descs(i)
            mem_end = len(descs) + 1
            for g in range(2):
                descs += sel_descs(i, g)
            hooks = []
            cur = mem_end
            for (slot, fn) in next_pieces:
                P.capture = []
                fn()
                ops_, P.capture = P.capture, None
                stages = []
                for o in ops_:
                    if stages and stages[-1][-1][0] == o[0]:
                        stages[-1].append(o)
                    else:
                        stages.append([o])
                for stg_ in stages:
                    def replay(stg_=stg_):
                        for (eng, f2, r2, w2, d2) in stg_:
                            P.op(eng, f2, reads=r2, writes=w2, dma_slot=d2)
                    hooks.append((cur, replay))
                    eng0, n_ = stg_[0][0], len(stg_)
                    cur += 1 + (min(2, n_ // 4) if eng0 == "pe" else n_ // 2)
            run_pipeline(descs, hooks)

        def epilogue(i):
            X_, xk = XT[i % 2], "XT%d" % (i % 2)
            vpc, vpp = VP[i % 2], VP[(i + 1) % 2]
            P.op("dve", lambda e: e.tensor_tensor(out=Y[:, 0:512], in0=OACC[:].rearrange("p h d -> p (h d)"), in1=SZ[0][:], op=ALU.mult),
                 reads=["OACC0", "OACC1", "SZ0"], writes=["Xb"])
            bnd = BAND0 if i == 0 else BAND
            bk = "BAND0" if i == 0 else "BAND"
            for g in range(4):
                P.op("pe", lambda e, g=g: e.matmul(pM[0][0:64, g * 128:(g + 1) * 128], lhsT=vpc[:, g * 64:(g + 1) * 64], rhs=bnd[:, g, :],
                                                   start=True, stop=(i == 0)), reads=["VP%d" % (i % 2), bk], writes=["pM0"])
                if i > 0:
                    P.op("pe", lambda e, g=g: e.matmul(pM[0][0:64, g * 128:(g + 1) * 128], lhsT=vpp[:, g * 64:(g + 1) * 64], rhs=BANDP[:, g, :],
                                                       start=False, stop=True), reads=["VP%d" % ((i + 1) % 2), "BANDP"], writes=["pM0"])
            P.op("act", lambda e: e.copy(out=PLT[:].rearrange("p g t -> p (g t)"), in_=pM[0][0:64, :]), reads=["pM0"], writes=["PLT"])
            for g in range(4):
                P.op("pe", lambda e, g=g: e.matmul(pM[1][:, g * 64:(g + 1) * 64], lhsT=PLT[:, g, :], rhs=Wpl[:, g, :], start=True, stop=True),
                     reads=["PLT", "Wpl"], writes=["pM0"])
            P.op("dve", lambda e: e.tensor_tensor(out=EZ[:, 0:256], in0=pM[1][:, 0:256], in1=REP[:, R_PSC:R_PSC + 256], op=ALU.mult), reads=["pM0", "REP"], writes=["EZ"])
            P.op("dve", lambda e: e.tensor_tensor(out=Y[:, 512:768], in0=EZ[:, 0:256], in1=SZ[1][:, 0:256], op=ALU.mult), reads=["EZ", "SZ1"], writes=["Xb"])

        def epilogue_b(i):
            X_, xk = XT[i % 2], "XT%d" % (i % 2)
            for kc in range(8):
                src = Y[:, kc * 128:(kc + 1) * 128] if kc < 6 else YM[:, (kc - 6) * 128:(kc - 5) * 128]
                P.op("pe", lambda e, kc=kc, src=src: e.transpose(out=pX[:, kc * 128:(kc + 1) * 128], in_=src, identity=IDb[:]),
                     reads=["Xb", "YM", "IDb"], writes=["pX"])
            for hh in range(2):
                P.op("act" if hh == 0 else "dve", (lambda e, hh=hh: e.copy(out=PTb[hh][:], in_=pX[:, hh * 512:(hh + 1) * 512])) if hh == 0 else
                     (lambda e, hh=hh: e.tensor_copy(out=PTb[hh][:], in_=pX[:, hh * 512:(hh + 1) * 512])), reads=["pX"], writes=["PT%d" % hh])
            for hf in range(2):
                pao, pako = next_pa()
                for kc in range(8):
                    P.op("pe", lambda e, kc=kc, pao=pao, hf=hf: e.matmul(pao[:], lhsT=PTb[kc // 4][:, (kc % 4) * 128:(kc % 4 + 1) * 128],
                                                                         rhs=Woutb[:, kc, hf * 512:(hf + 1) * 512], start=(kc == 0), stop=(kc == 7)),
                         reads=["PT%d" % (kc // 4), "Woutb"], writes=[pako])
                P.op("dve", lambda e, pao=pao, hf=hf: e.tensor_tensor(out=X_[:, hf * 512:(hf + 1) * 512], in0=pao[:], in1=X_[:, hf * 512:(hf + 1) * 512], op=ALU.add),
                     reads=[pako, xk], writes=[xk])
            fin.append(P.op("pool", lambda e: e.dma_start(out=out_d[i * 128:(i + 1) * 128, :], in_=X_[:]), reads=[xk], dma_slot=xk + "st"))

        load(XT[0][:], x_d[0:128, :], "XT0", "XT0")
        P.epoch = 1
        for (slot, fn) in front_a_pieces(0):
            fn()
        for i in range(NT):
            P.epoch = 1 + i // 8
            if i + 1 < NT:
                load(XT[(i + 1) % 2][:], x_d[(i + 1) * 128:(i + 2) * 128, :], "XT%d" % ((i + 1) % 2), "XT%d" % ((i + 1) % 2))
            if i == 0:
                front_b(0)
            attention(i, front_a_pieces(i + 1) if i + 1 < NT else [])
            epilogue(i)
            if i + 1 < NT:
                front_b(i + 1)
            epilogue_b(i)
        P.emit(final_wait_ops=fin[-2:])
    return nc


def make_in_maps(NT, x, mem, positions, g_norm, w_in, g_q_nsa, g_k_cmp, g_k_slc, g_k_win, cmp_pos_k, w_cmp_k1, w_cmp_k2,
                 cmp_pos_v, w_cmp_v1, w_cmp_v2, w_pool, pool_scale, g_mem, w_mem_kv, g_q_mem, g_k_mem, w_out):
    f = lambda a: np.ascontiguousarray(np.asarray(a, dtype=np.float32))
    B = x.shape[0]
    S = 128 * NT
    consts = host_consts()
    rep = np.zeros((128, R_END), np.float32)
    rep[:, R_GQ:R_GQ + 64] = f(g_q_nsa)[0][None, :]
    gk1 = np.concatenate([f(g_k_slc)[0]] * 2 + [f(g_k_win)[0]] * 2 + [f(g_q_mem)[0]] * 4)
    rep[:, R_GK1:R_GK1 + 512] = gk1[None, :]
    rep[:, R_GKC:R_GKC + 128] = np.concatenate([f(g_k_cmp)[0]] * 2)[None, :]
    rep[:, R_GKM:R_GKM + 256] = np.concatenate([f(g_k_mem)[0]] * 4)[None, :]
    rep[:, R_PSC:R_PSC + 256] = f(pool_scale)[0][None, :]
    rep[:, R_INVF:R_INVF + 8] = (500000.0 ** (-np.arange(8, dtype=np.float32) / 8)).astype(np.float32)[None, :]
    rep[:, R_GN:R_GN + 8] = f(g_norm)[0].reshape(8, 128).T
    rep[:, R_GM:R_GM + 8] = f(g_mem)[0].reshape(8, 128).T
    pos = np.asarray(positions).astype(np.int32)
    maps = []
    for b in range(B):
        posl = np.ascontiguousarray(pos[b, :S].reshape(NT, 128).T)
        idx = 16 * (8 * np.arange(NT)[None, :] - 1 + np.arange(8)[:, None]) + 31
        idx = np.clip(idx, 0, S - 1)
        posc = np.ascontiguousarray(pos[b][idx]).astype(np.int32)
        m = {"x": f(x[b, :S]), "mem": f(mem[b]), "posl": posl, "posc": posc, "w_in": f(w_in[0]), "w_out": f(w_out[0]),
             "w_mem_kv": f(w_mem_kv[0]), "w_cmp_k1": f(w_cmp_k1[0]), "w_cmp_v1": f(w_cmp_v1[0]), "w_cmp_k2": f(w_cmp_k2[0]),
             "w_cmp_v2": f(w_cmp_v2[0]), "cmp_pos_k": f(cmp_pos_k[0]), "cmp_pos_v": f(cmp_pos_v[0]), "w_pool": f(w_pool[0]), "rep": rep}
        for n, v in consts.items():
            m["c_" + n] = v
        maps.append(m)
    return maps


def kernel(**inputs):
    x = np.asarray(inputs["x"])
    B, S, D = x.shape
    NT = S // 128
    nc = build(NT)
    maps = make_in_maps(NT, **inputs)
    res = run_bass_kernel_spmd(nc, maps, core_ids=list(range(B)))
    return np.stack([np.asarray(r["out"]) for r in res.results], axis=0).astype(np.float32)
```
